# Optimizing a Trainium2 kernel written in Bass

```python
import jax, jax.numpy as jnp
from jax import lax
import numpy as np

D_MODEL = 2048
BATCH = 16
SEQ = 2048
DEPTH = 4

D_MIX = D_MODEL
POOL_WIDTH = D_MIX // 4
POOL_WINDOWS = (2, 4, 8, 16)
POOL_GROUPS = len(POOL_WINDOWS)
POOL_GROUP = POOL_WIDTH // POOL_GROUPS
HGRN_WIDTH = D_MIX // 4
HGRN_HEAD_DIM = 128
HGRN_HEADS = HGRN_WIDTH // HGRN_HEAD_DIM
HGRN_CHUNK = 64
FOX_WIDTH = D_MIX - POOL_WIDTH - HGRN_WIDTH
FOX_HEAD_DIM = 128
FOX_HEADS = FOX_WIDTH // FOX_HEAD_DIM
FOX_BLOCK = 128
D_FF = 5632
CONV_WIDTH = 3
IN_SIZES = (POOL_WIDTH, HGRN_WIDTH, HGRN_WIDTH, HGRN_WIDTH, HGRN_WIDTH,
            FOX_WIDTH, FOX_WIDTH, FOX_WIDTH, FOX_HEADS)
IN_COLS = sum(IN_SIZES)
IN_SPLITS = tuple(int(v) for v in np.cumsum(IN_SIZES)[:-1])
DEEPNORM_ALPHA = (2 * DEPTH) ** 0.25
DEEPNORM_BETA = (8 * DEPTH) ** -0.25
LN_EPS = 1e-5
RMS_EPS = 1e-6
MASK_VALUE = -1e30
EXP_CLAMP = 80.0

kernel_name = 'hymba_pool_hgrn2_fox_deepnorm_trunk'


def layer_norm(x, g, b):
    xf = x.astype(jnp.float32)
    mu = jnp.mean(xf, axis=-1, keepdims=True)
    var = jnp.mean(jnp.square(xf - mu), axis=-1, keepdims=True)
    return ((xf - mu) * lax.rsqrt(var + LN_EPS)).astype(x.dtype) * g + b


def pool_mixer(u, pool_w, pool_scale):
    B_, S_, _ = u.shape
    max_w = POOL_WINDOWS[-1]
    uf = u.astype(jnp.float32)
    csum = jnp.pad(jnp.cumsum(uf, axis=1), ((0, 0), (max_w, 0), (0, 0)))
    t = jnp.arange(S_, dtype=jnp.float32)
    means = []
    for gi, w in enumerate(POOL_WINDOWS):
        c = csum[:, :, gi * POOL_GROUP:(gi + 1) * POOL_GROUP]
        window_sum = c[:, max_w:] - c[:, max_w - w:max_w - w + S_]
        count = jnp.minimum(t + 1.0, float(w))
        means.append(window_sum / count[None, :, None])
    pooled = jnp.stack(means, axis=2)
    d = (pooled - uf.reshape(B_, S_, POOL_GROUPS, POOL_GROUP)).astype(u.dtype)
    y = jnp.einsum('bsgc,gcd->bsgd', d, pool_w).reshape(B_, S_, POOL_WIDTH)
    return y * pool_scale


def hgrn2_mixer(q, f_logit, i, g, lb, norm_g):
    B_, S_, _ = q.shape
    n_chunks = S_ // HGRN_CHUNK
    f32 = jnp.float32
    z = f_logit.astype(f32)
    lb = lb.astype(f32)
    log_f = jax.nn.log_sigmoid(z) + jnp.log1p(lb * jnp.exp(jnp.minimum(-z, EXP_CLAMP)))
    log_f = jnp.minimum(log_f, 0.0)
    k = (1.0 - lb) * jax.nn.sigmoid(-z)

    def to_chunks(a):
        return a.astype(f32).reshape(B_, n_chunks, HGRN_CHUNK, HGRN_HEADS, HGRN_HEAD_DIM).transpose(1, 0, 3, 2, 4)

    qc, kc, vc = to_chunks(q), to_chunks(k), to_chunks(i)
    bc = jnp.cumsum(to_chunks(log_f), axis=3)
    causal = jnp.tril(jnp.ones((HGRN_CHUNK, HGRN_CHUNK), dtype=bool))[:, :, None]

    def chunk_step(state, xs):
        qb, kb, vb, bb = xs
        o_inter = jnp.einsum('bhtd,bhde->bhte', qb * jnp.exp(bb), state)
        rel = bb[:, :, :, None, :] - bb[:, :, None, :, :]
        decay = jnp.where(causal, jnp.exp(jnp.where(causal, rel, 0.0)), 0.0)
        att = jnp.einsum('bhtd,bhsd,bhtsd->bhts', qb, kb, decay)
        o_intra = jnp.einsum('bhts,bhse->bhte', att, vb)
        b_last = bb[:, :, -1:, :]
        new_state = state * jnp.exp(b_last[:, :, 0, :, None]) + jnp.einsum(
            'bhsd,bhse->bhde', kb * jnp.exp(b_last - bb), vb)
        return new_state, o_inter + o_intra

    state0 = jnp.zeros((B_, HGRN_HEADS, HGRN_HEAD_DIM, HGRN_HEAD_DIM), f32)
    _, o = lax.scan(chunk_step, state0, (qc, kc, vc, bc))
    o = o.transpose(1, 0, 3, 2, 4).reshape(B_, S_, HGRN_HEADS, HGRN_HEAD_DIM)
    o = o * lax.rsqrt(jnp.mean(o * o, axis=-1, keepdims=True) + RMS_EPS)
    o = o.reshape(B_, S_, HGRN_WIDTH) * norm_g.astype(f32) * jax.nn.silu(g.astype(f32))
    return o.astype(q.dtype)


def fox_attention(q, k, v, f_logit, f_bias):
    B_, S_, _ = q.shape
    q = q.reshape(B_, S_, FOX_HEADS, FOX_HEAD_DIM)
    k = k.reshape(B_, S_, FOX_HEADS, FOX_HEAD_DIM)
    v = v.reshape(B_, S_, FOX_HEADS, FOX_HEAD_DIM)
    log_f = jax.nn.log_sigmoid((f_logit + f_bias).astype(jnp.float32))
    cum = jnp.cumsum(log_f, axis=1).transpose(0, 2, 1)
    scale = FOX_HEAD_DIM ** -0.5
    outs = []
    for blk in range(S_ // FOX_BLOCK):
        t0, t1 = blk * FOX_BLOCK, (blk + 1) * FOX_BLOCK
        s = jnp.einsum('bthd,bshd->bhts', q[:, t0:t1], k[:, :t1]).astype(jnp.float32) * scale
        s = s + cum[:, :, t0:t1, None] - cum[:, :, None, :t1]
        mask = (t0 + jnp.arange(FOX_BLOCK))[:, None] >= jnp.arange(t1)[None, :]
        p = jax.nn.softmax(jnp.where(mask, s, MASK_VALUE), axis=-1).astype(v.dtype)
        outs.append(jnp.einsum('bhts,bshd->bthd', p, v[:, :t1]))
    return jnp.concatenate(outs, axis=1).reshape(B_, S_, FOX_WIDTH)


def conv_ffn(x, w_gate, w_val, conv_w, conv_b, w_down):
    gate = jnp.einsum('bsd,df->bsf', x, w_gate)
    val = jnp.einsum('bsd,df->bsf', x, w_val)
    gate = lax.conv_general_dilated(
        gate, conv_w[:, None, :], window_strides=(1,), padding=[(CONV_WIDTH - 1, 0)],
        dimension_numbers=('NWC', 'WIO', 'NWC'), feature_group_count=D_FF) + conv_b
    return jnp.einsum('bsf,fd->bsd', jax.nn.silu(gate) * val, w_down)


def setup_inputs(seed: int = 0) -> dict:
    key = jax.random.key(seed)
    ks = jax.random.split(key, 17)
    n = jax.random.normal
    f32 = jnp.float32
    return {
        'x': n(ks[0], (BATCH, SEQ, D_MODEL), f32),
        'w_in': n(ks[1], (DEPTH, D_MODEL, IN_COLS), f32) * D_MODEL ** -0.5,
        'fox_f_bias': 0.01 * n(ks[2], (DEPTH, FOX_HEADS), f32),
        'pool_w': n(ks[3], (DEPTH, POOL_GROUPS, POOL_GROUP, POOL_GROUP), f32) * POOL_GROUP ** -0.5,
        'pool_scale': 1.0 + 0.02 * n(ks[4], (DEPTH, POOL_WIDTH), f32),
        'hgrn_lb_logits': 0.1 * n(ks[5], (DEPTH, HGRN_WIDTH), f32),
        'hgrn_norm_g': 1.0 + 0.02 * n(ks[6], (DEPTH, HGRN_WIDTH), f32),
        'w_out': n(ks[7], (DEPTH, D_MIX, D_MODEL), f32) * (D_MIX ** -0.5 * DEEPNORM_BETA),
        'ln1_g': 1.0 + 0.02 * n(ks[8], (DEPTH, D_MODEL), f32),
        'ln1_b': 0.01 * n(ks[9], (DEPTH, D_MODEL), f32),
        'w_gate': n(ks[10], (DEPTH, D_MODEL, D_FF), f32) * D_MODEL ** -0.5,
        'w_val': n(ks[11], (DEPTH, D_MODEL, D_FF), f32) * D_MODEL ** -0.5,
        'conv_w': n(ks[12], (DEPTH, CONV_WIDTH, D_FF), f32) * CONV_WIDTH ** -0.5,
        'conv_b': 0.01 * n(ks[13], (DEPTH, D_FF), f32),
        'w_down': n(ks[14], (DEPTH, D_FF, D_MODEL), f32) * (D_FF ** -0.5 * DEEPNORM_BETA),
        'ln2_g': 1.0 + 0.02 * n(ks[15], (DEPTH, D_MODEL), f32),
        'ln2_b': 0.01 * n(ks[16], (DEPTH, D_MODEL), f32),
    }


def reference(x, w_in, fox_f_bias, pool_w, pool_scale, hgrn_lb_logits, hgrn_norm_g, w_out,
              ln1_g, ln1_b, w_gate, w_val, conv_w, conv_b, w_down, ln2_g, ln2_b):
    lb_w = jax.nn.softmax(hgrn_lb_logits.astype(jnp.float32), axis=0)
    lower_bounds = jnp.cumsum(lb_w, axis=0) - lb_w[0:1]
    for l in range(DEPTH):
        h = jnp.einsum('bsd,de->bse', x, w_in[l])
        u_pool, hq, hf, hi, hg, fq, fk, fv, ff = jnp.split(h, IN_SPLITS, axis=-1)
        y = jnp.concatenate([
            pool_mixer(u_pool, pool_w[l], pool_scale[l]),
            hgrn2_mixer(hq, hf, hi, hg, lower_bounds[l], hgrn_norm_g[l]),
            fox_attention(fq, fk, fv, ff, fox_f_bias[l]),
        ], axis=-1)
        x = layer_norm(DEEPNORM_ALPHA * x + jnp.einsum('bse,ed->bsd', y, w_out[l]), ln1_g[l], ln1_b[l])
        x = layer_norm(DEEPNORM_ALPHA * x + conv_ffn(x, w_gate[l], w_val[l], conv_w[l], conv_b[l], w_down[l]),
                       ln2_g[l], ln2_b[l])
    return x
```

```python
import contextlib
import os
import numpy as np
import concourse.bass as bass
import concourse.mybir as mybir
from concourse.bass_utils import run_bass_kernel_spmd

F32 = mybir.dt.float32
BF16 = mybir.dt.bfloat16
AF = mybir.ActivationFunctionType
ALU = mybir.AluOpType

SAME_ENGINE_SYNC = True

D = 2048
KC = 16
DFF = 5632
NFB = 44
INC = 5640
ALPHA = 8.0 ** 0.25
LN_EPS = 1e-5
RMS_EPS = 1e-6
CH = 64


class Prog:
    ENGS = ("pe", "act", "dve", "pool", "sp")

    def __init__(self, nc):
        self.nc = nc
        self.q = {e: [] for e in self.ENGS}
        self.last_write = {}
        self.reads_since = {}
        self.seen = {e: {} for e in self.ENGS}
        self.needed = {e: set() for e in self.ENGS}
        self.dma_cnt = {}
        self.dma_sems = []

    def dma_sem(self, name):
        self.dma_cnt[name] = 0
        self.dma_sems.append(name)
        return name

    def _deps(self, reads, writes):
        deps = []
        for k in reads:
            ev = self.last_write.get(k)
            if ev is not None:
                deps.append(ev)
        for k in writes:
            ev = self.last_write.get(k)
            if ev is not None:
                deps.append(ev)
            deps.extend(self.reads_since.get(k, ()))
        return deps

    def _commit(self, ev, reads, writes):
        for k in reads:
            self.reads_since.setdefault(k, []).append(ev)
        for k in writes:
            self.last_write[k] = ev
            self.reads_since[k] = []

    def _waits(self, eng, deps):
        waits = {}
        seen = self.seen[eng]
        for (sk, idx) in deps:
            if sk == eng and (eng == "pe" or not SAME_ENGINE_SYNC):
                continue
            if seen.get(sk, -1) >= idx:
                continue
            if waits.get(sk, -1) < idx:
                waits[sk] = idx
        for sk, idx in waits.items():
            seen[sk] = idx
            if sk in self.needed:
                self.needed[sk].add(idx)
        return waits

    def op(self, eng, fn, reads=(), writes=()):
        writes = list(writes) + [k for k in reads if k.startswith("pb") and k not in writes]
        deps = self._deps(reads, writes)
        waits = self._waits(eng, deps)
        idx = len(self.q[eng])
        self.q[eng].append(("op", fn, waits, None))
        ev = (eng, idx)
        self._commit(ev, reads, writes)
        return ev

    def dma(self, queue, sem, fn, reads=(), writes=()):
        deps = self._deps(reads, writes)
        waits = self._waits(queue, deps)
        self.dma_cnt[sem] += 16
        ev = (sem, self.dma_cnt[sem])
        self.q[queue].append(("dma", fn, waits, sem))
        self._commit(ev, reads, writes)
        return ev

    def barrier(self):
        deps = []
        for e in self.ENGS:
            for i in range(len(self.q[e]) - 1, -1, -1):
                if self.q[e][i][0] == "op":
                    deps.append((e, i))
                    break
        for s in self.dma_sems:
            if self.dma_cnt[s] > 0:
                deps.append((s, self.dma_cnt[s]))
        for e in self.ENGS:
            waits = self._waits(e, [d for d in deps if d[0] != e])
            self.q[e].append(("wait", None, waits, None))

    def emit(self):
        nc = self.nc
        with contextlib.ExitStack() as st:
            sems = {}
            for e in self.ENGS:
                sems[e] = st.enter_context(nc.semaphore("s_" + e))
            for d in self.dma_sems:
                sems[d] = st.enter_context(nc.semaphore("d_" + d))
            val = {}
            for e in self.ENGS:
                c = 0
                v = {}
                for i in range(len(self.q[e])):
                    if i in self.needed[e]:
                        c += 1
                        v[i] = c
                val[e] = v
            block = st.enter_context(nc.Block())

            def run(e, engh):
                needed = self.needed[e]
                for i, (kind, fn, waits, dsem) in enumerate(self.q[e]):
                    for sk, idx in waits.items():
                        if sk in val:
                            engh.wait_ge(sems[sk], val[sk][idx])
                        else:
                            engh.wait_ge(sems[sk], idx)
                    if kind == "wait":
                        continue
                    mname, margs, mkw = fn
                    bi = getattr(engh, mname)(*margs, **mkw)
                    if kind == "dma":
                        bi.then_inc(sems[dsem], 16)
                    elif i in needed:
                        bi.then_inc(sems[e], 1)

            @block.tensor
            def _(eh):
                run("pe", eh)

            @block.scalar
            def _(eh):
                run("act", eh)

            @block.vector
            def _(eh):
                run("dve", eh)

            @block.gpsimd
            def _(eh):
                run("pool", eh)

            @block.sync
            def _(eh):
                run("sp", eh)


class _Stop(Exception):
    pass


def build(S, NSEQ, L, GF=11):
    assert S % 512 == 0
    NTG = S // 512
    NTB = S // 128
    NCK = S // CH
    TH = min(1024, S)
    NHALF = S // TH
    NTGH = TH // 512
    nc = bass.Bass("TRN2", target_bir_lowering=False)
    P = Prog(nc)
    DBG = int(os.environ.get("KDBG", "9"))
    KSUB = int(os.environ.get("KSUB", "9"))

    def ACT(reads, writes, **kw):
        P.op("act", ("activation", (), kw), reads, writes)

    def MM(reads, writes, out, **kw):
        P.op("pe", ("matmul", (out,), kw), reads, writes)

    def TR(reads, writes, **kw):
        P.op("pe", ("transpose", (), kw), reads, writes)

    def DVE(m, reads, writes, *a, **kw):
        P.op("dve", (m, a, kw), reads, writes)

    def POOL(m, reads, writes, *a, **kw):
        P.op("pool", (m, a, kw), reads, writes)

    def DMA(queue, sem, reads, writes, **kw):
        P.dma(queue, sem, ("dma_start", (), kw), reads, writes)

    def din(name, shape):
        return nc.dram_tensor(name, shape, F32, kind="ExternalInput").ap()

    x_d = din("x", [NSEQ * S, D])
    w_in_d = din("w_in", [L, D, INC])
    fbias_d = din("fox_f_bias", [L, 8])
    pool_w_d = din("pool_w", [L, 4, 128, 128])
    pool_scale_d = din("pool_scale", [L, 512])
    lbl_d = din("hgrn_lb_logits", [L, 512])
    normg_d = din("hgrn_norm_g", [L, 512])
    w_out_d = din("w_out", [L, D, D])
    ln1g_d = din("ln1_g", [L, D])
    ln1b_d = din("ln1_b", [L, D])
    w_gate_d = din("w_gate", [L, D, DFF])
    w_val_d = din("w_val", [L, D, DFF])
    conv_w_d = din("conv_w", [L, 3, DFF])
    conv_b_d = din("conv_b", [L, DFF])
    w_down_d = din("w_down", [L, DFF, D])
    ln2g_d = din("ln2_g", [L, D])
    ln2b_d = din("ln2_b", [L, D])
    out_d = nc.dram_tensor("out", [NSEQ * S, D], F32, kind="ExternalOutput").ap()
    XR = nc.dram_tensor("xr_scr", [D, S], F32).ap()
    YD = nc.dram_tensor("y_scr", [D, S], BF16).ap()
    XRv = XR.rearrange("(c p) t -> p c t", p=128)
    YDv = YD.rearrange("(c p) t -> p c t", p=128)

    st = contextlib.ExitStack()
    with st:
        def sb(name, shape, dt):
            return st.enter_context(nc.sbuf_tensor(name, shape, dt))

        ARENA_B = 198 * 1024
        arena = sb("arena", [128, ARENA_B // 2], BF16)
        NCST = 128 * 4 + 16 + 8 * 4 + L * 8 + L + 2 + L * 256 + 2 * 128
        cst = sb("cst", [128, NCST], F32)
        cstb = sb("cstb", [128, 128 + 128 + 64], BF16)
        banks = [st.enter_context(nc.psum_tensor("pb%d" % i, [128, 512], F32)) for i in range(8)]
        BK = ["pb%d" % i for i in range(8)]

        class Carver:
            def __init__(self):
                self.off = 0

            def reset(self, off=0):
                self.off = off

            def take(self, free_shape, dt):
                n = int(np.prod(free_shape))
                esz = 4 if dt == F32 else 2
                nb = n * esz
                nbp = (nb + 63) // 64 * 64
                assert self.off + nbp <= ARENA_B, (self.off, nbp, ARENA_B)
                a = arena[:, self.off // 2: self.off // 2 + nb // 2]
                self.off += nbp
                if dt == F32:
                    a = a.bitcast(F32)
                if len(free_shape) == 2:
                    a = a.rearrange("p (a b) -> p a b", b=free_shape[1])
                elif len(free_shape) == 3:
                    a = a.rearrange("p (a b c) -> p a b c", b=free_shape[1], c=free_shape[2])
                return a

        CV = Carver()
        try:

            o = 0
            ident_f = cst[:, o:o + 128]; o += 128
            maskneg = cst[:, o:o + 128]; o += 128
            seltmp = cst[:, o:o + 128]; o += 128
            o += 128
            invc = cst[:, o:o + 16]; o += 16
            sel1 = cst[:, o:o + 8]; o += 8
            sel2 = cst[:, o:o + 8]; o += 8
            nsel1 = cst[:, o:o + 8]; o += 8
            o += 8
            LBT = cst[:, o:o + L * 4].rearrange("p (l h) -> p l h", h=4); o += L * 4
            OMLT = cst[:, o:o + L * 4].rearrange("p (l h) -> p l h", h=4); o += L * 4
            NFBT = cst[:, o:o + L]; o += L
            ones_col = cst[:, o:o + 1]; o += 1
            o += 1
            CVT = cst[:, o:o + L * 256].rearrange("p (l c) -> p l c", c=256); o += L * 256
            HALO = cst[:, o:o + 128].rearrange("p (f j) -> p f j", j=2)[:, 0:NFB, :]; o += 128
            SST = cst[:, o:o + 128]; o += 128
            assert o <= NCST
            ident_b = cstb[:, 0:128]
            ones_b = cstb[:, 128:256]
            triu = cstb[0:64, 256:320]

            POOL("memset", [], ["cst"], cst[:], 0.0)
            POOL("memset", ["cst"], ["ident_f"], ident_f, 1.0)
            POOL("affine_select", ["ident_f"], ["ident_f"], out=ident_f, in_=ident_f, pattern=[[1, 128]],
                 compare_op=ALU.is_equal, fill=0.0, base=0, channel_multiplier=-1)
            POOL("memset", [], ["cstb"], cstb[:], 1.0)
            POOL("affine_select", ["cstb"], ["ident_b"], out=ident_b, in_=ident_b, pattern=[[1, 128]],
                 compare_op=ALU.is_equal, fill=0.0, base=0, channel_multiplier=-1)
            POOL("affine_select", ["cstb"], ["triu"], out=triu, in_=triu, pattern=[[1, 64]],
                 compare_op=ALU.is_ge, fill=0.0, base=0, channel_multiplier=-1)
            POOL("affine_select", ["cst"], ["maskneg"], out=maskneg, in_=maskneg, pattern=[[1, 128]],
                 compare_op=ALU.is_ge, fill=-30000.0, base=0, channel_multiplier=-1)
            POOL("memset", ["cst"], ["ones_col"], ones_col, 1.0)
            for t in range(16):
                POOL("memset", ["cst"], ["invc"], invc[:, t:t + 1], 1.0 / (t + 1))
            for si, (dst, offs) in enumerate(((sel1, (0, 32, 64)), (sel2, (8, 40, 72)))):
                for oi, off in enumerate(offs):
                    tmp = seltmp[:, (si * 3 + oi) * 8:(si * 3 + oi) * 8 + 8]
                    POOL("memset", ["cst"], ["seltmp"], tmp, 1.0)
                    POOL("affine_select", ["seltmp"], ["seltmp"], out=tmp, in_=tmp, pattern=[[-1, 8]],
                         compare_op=ALU.is_equal, fill=0.0, base=-off, channel_multiplier=1)
                    POOL("tensor_tensor", ["seltmp", "cst"], ["sel"], out=dst, in0=dst, in1=tmp, op=ALU.add)
            POOL("tensor_scalar", ["sel"], ["nsel1"], out=nsel1, in0=sel1, scalar1=-1.0, scalar2=None, op0=ALU.mult)

            if DBG == -3:
                raise _Stop()
            CV.reset()
            stgf = CV.take([L * 256], F32)
            dsem_c = P.dma_sem("c")
            DVE("memset", [], ["stg"], stgf, 0.0)
            for l in range(L):
                s0 = stgf[:, (l * 2) * 128:(l * 2 + 1) * 128]
                s1 = stgf[:, (l * 2 + 1) * 128:(l * 2 + 2) * 128]
                rows = [(ln1g_d[l], 0, 16), (ln1b_d[l], 16, 16), (ln2g_d[l], 32, 16), (ln2b_d[l], 48, 16),
                        (conv_b_d[l], 64, 44), (pool_scale_d[l], 108, 4), (normg_d[l], 112, 4), (lbl_d[l], 116, 4)]
                for (src, r0, nr) in rows:
                    DMA("sp", dsem_c, ["stg"], ["stgd"], out=s0[r0:r0 + nr, :], in_=src.rearrange("(c p) -> c p", p=128))
                cwv = conv_w_d[l].rearrange("j (c p) -> (j c) p", p=128)
                DMA("sp", dsem_c, ["stg"], ["stgd"], out=s0[120:128, :], in_=cwv[0:8, :])
                DMA("sp", dsem_c, ["stg"], ["stgd"], out=s1[0:124, :], in_=cwv[8:132, :])
                for off in (0, 8, 32, 40, 64, 72):
                    DMA("sp", dsem_c, ["cst"], ["nfbt"], out=NFBT[off:off + 8, l:l + 1],
                        in_=fbias_d[l].rearrange("(h o) -> h o", o=1))
            P.barrier()
            for l in range(L):
                for hh in range(2):
                    bk = banks[hh]
                    TR(["stgd", "stg", "ident_f"], [BK[hh]], out=bk[:, 0:128], in_=stgf[:, (l * 2 + hh) * 128:(l * 2 + hh + 1) * 128], identity=ident_f)
                    DVE("tensor_copy", [BK[hh], "cst"], ["cvt"], out=CVT[:, l, hh * 128:(hh + 1) * 128], in_=bk[:, 0:128])
            DVE("tensor_scalar", ["nfbt"], ["nfbt"], out=NFBT, in0=NFBT, scalar1=-1.0, scalar2=None, op0=ALU.mult)
            if DBG == -2:
                raise _Stop()
            EL = CV.take([L, 4], F32)
            TOT = CV.take([4], F32)
            for l in range(L):
                ACT(["cvt"], ["el"], out=EL[:, l, :], in_=CVT[:, l, 116:120], func=AF.Exp)
            DVE("tensor_copy", ["el"], ["tot"], out=TOT, in_=EL[:, 0, :])
            for l in range(1, L):
                DVE("tensor_tensor", ["el", "tot"], ["tot"], out=TOT, in0=TOT, in1=EL[:, l, :], op=ALU.add)
            DVE("reciprocal", ["tot"], ["tot"], out=TOT, in_=TOT)
            for l in range(1, L):
                DVE("tensor_tensor", ["el", "tot"], ["el"], out=EL[:, l, :], in0=EL[:, l, :], in1=TOT, op=ALU.mult)
                DVE("tensor_tensor", ["el", "lbt", "cst"], ["lbt"], out=LBT[:, l, :], in0=LBT[:, l - 1, :], in1=EL[:, l, :], op=ALU.add)
            DVE("tensor_scalar", ["lbt", "cst"], ["omlt"], out=OMLT, in0=LBT, scalar1=-1.0, scalar2=1.0, op0=ALU.mult, op1=ALU.add)
            P.barrier()

            if DBG == -1:
                raise _Stop()
            NWS = 3
            wsem = [[P.dma_sem("w%d_%d" % (i, j)) for j in range(2)] for i in range(NWS)]
            WK = [["ws%d_%d" % (i, j) for j in range(2)] for i in range(NWS)]
            wstate = {"i": 0}

            def wslot_take():
                i = wstate["i"]
                wstate["i"] = (i + 1) % NWS
                return i

            dsem_x = [P.dma_sem("x0"), P.dma_sem("x1")]
            dsem_o = [P.dma_sem("o0"), P.dma_sem("o1")]
            dsem_m = [P.dma_sem("m%d" % i) for i in range(4)]
            cnt = {"bk": 0, "m": 0, "t": 0}

            def rot(name, n):
                v = cnt[name]
                cnt[name] = (v + 1) % n
                return v % n

            for sq in range(NSEQ):
                tok0 = sq * S
                CV.reset()
                XT = CV.take([KC, S], BF16)
                off_after_xt = CV.off
                xin = [CV.take([D], F32) for _ in range(2)]
                xstf = [CV.take([KC * 128], F32) for _ in range(2)]
                xst = [a.rearrange("p (a b) -> p a b", b=128) for a in xstf]
                for tb in range(NTB):
                    b = tb % 2
                    DMA("sp", dsem_x[b], [], ["xin%d" % b], out=xin[b], in_=x_d[tok0 + tb * 128: tok0 + (tb + 1) * 128, :])
                    for q4 in (range(4) if KSUB >= 0 else []):
                        bi = rot("bk", 2)
                        bk = banks[bi]
                        for j in range(4):
                            dc = q4 * 4 + j
                            TR(["xin%d" % b, "ident_f"], [BK[bi]], out=bk[:, j * 128:(j + 1) * 128],
                               in_=xin[b][:, dc * 128:(dc + 1) * 128], identity=ident_f)
                        if KSUB >= 1:
                            for j in range(4):
                                ACT([BK[bi]], ["XT"], out=XT[:, q4 * 4 + j, tb * 128:(tb + 1) * 128], in_=bk[:, j * 128:(j + 1) * 128], func=AF.Identity)
                        if KSUB >= 2:
                            DVE("tensor_copy", [BK[bi]], ["xst%d" % b], out=xstf[b][:, q4 * 512:(q4 + 1) * 512], in_=bk[:, :])
                    if KSUB >= 3:
                        DMA("sp", dsem_o[b], ["xst%d" % b], ["XR"], out=XRv[:, :, tb * 128:(tb + 1) * 128], in_=xst[b])
                P.barrier()
                if KSUB < 4:
                    raise _Stop()

                for l in range(L):
                    cvl = CVT[:, l, :]
                    CV.reset(off_after_xt)
                    wsl = [CV.take([KC, 256], BF16) for _ in range(NWS)]

                    def load_w(dview, ncols, half=0, slot=None, wsl=wsl):
                        i = wslot_take() if slot is None else slot
                        c0 = half * 128
                        DMA("pool", wsem[i][half], [], [WK[i][half]] if ncols <= 128 else [WK[i][0], WK[i][1]],
                            out=wsl[i][:, :, c0:c0 + ncols], in_=dview)
                        return i

                    winv = w_in_d[l].rearrange("(c p) n -> p c n", p=128)

                    def inproj_fm(col0, m, consume, wtile=None, winv=winv, wsl=wsl, XT=XT):
                        if wtile is None:
                            i = load_w(winv[:, :, col0:col0 + m], m)
                            wt, wk = wsl[i][:, :, 0:m], [WK[i][0]]
                        else:
                            wt, wk = wtile
                        for tg in range(NTG):
                            bi = rot("bk", 4)
                            bk = banks[bi]
                            for kc in range(KC):
                                MM(wk + ["XT"], [BK[bi]], bk[0:m, :], lhsT=wt[:, kc, :], rhs=XT[:, kc, tg * 512:(tg + 1) * 512],
                                   start=(kc == 0), stop=(kc == KC - 1))
                            consume(tg, bk[0:m, :], BK[bi])

                    tmp_off = CV.off
                    yst_i = {"i": 0}

                    CV.reset(tmp_off)
                    U = CV.take([16 + S], F32)
                    A = CV.take([16 + S], F32)
                    B = CV.take([16 + S], F32)
                    Dt = CV.take([S], BF16)
                    PW = CV.take([128], BF16)
                    YST = [CV.take([S], BF16) for _ in range(2)]
                    small = CV.take([32], F32)
                    for buf, nm in ((U, "U"), (A, "A"), (B, "B")):
                        POOL("memset", [], [nm], buf[:, 0:16], 0.0)
                    for g in (range(4) if DBG >= 1 else []):
                        w = 2 ** (g + 1)
                        DMA("pool", dsem_m[0], [], ["PW"], out=PW, in_=pool_w_d[l, g])

                        def cons_u(tg, ps, bkk, U=U):
                            ACT([bkk], ["U"], out=U[:, 16 + tg * 512:16 + (tg + 1) * 512], in_=ps, func=AF.Identity)
                        inproj_fm(g * 128, 128, cons_u)
                        src, sn = U, "U"
                        pp = [(A, "A"), (B, "B")]
                        for stp in range(g + 1):
                            sh = 2 ** stp
                            dst, dn = pp[stp % 2]
                            DVE("tensor_tensor", [sn], [dn], out=dst[:, 16:16 + S], in0=src[:, 16:16 + S],
                                in1=src[:, 16 - sh:16 - sh + S], op=ALU.add)
                            src, sn = dst, dn
                        DVE("scalar_tensor_tensor", [sn, "U"], ["Dt"], out=Dt, in0=src[:, 16:16 + S], scalar=1.0 / w,
                            in1=U[:, 16:16 + S], op0=ALU.mult, op1=ALU.subtract)
                        DVE("tensor_tensor", [sn, "invc"], ["small"], out=small[:, 0:w - 1], in0=src[:, 16:16 + w - 1],
                            in1=invc[:, 0:w - 1], op=ALU.mult)
                        DVE("tensor_tensor", ["small", "U"], ["Dt"], out=Dt[:, 0:w - 1], in0=small[:, 0:w - 1],
                            in1=U[:, 16:16 + w - 1], op=ALU.subtract)
                        yi = yst_i["i"]; yst_i["i"] ^= 1
                        for tg in range(NTG):
                            bi = 4 + rot("m", 2)
                            bk = banks[bi]
                            MM(["PW", "Dt"], [BK[bi]], bk[:, :], lhsT=PW, rhs=Dt[:, tg * 512:(tg + 1) * 512], start=True, stop=True)
                            ACT([BK[bi], "cvt"], ["YST%d" % yi], out=YST[yi][:, tg * 512:(tg + 1) * 512], in_=bk[:, :],
                                func=AF.Identity, scale=cvl[:, 108 + g:109 + g])
                        DMA("sp", dsem_o[yi], ["YST%d" % yi], ["YD"], out=YD[g * 128:(g + 1) * 128, :], in_=YST[yi])
                    P.barrier()

                    CV.reset(tmp_off)
                    T1 = CV.take([S], F32)
                    T2 = CV.take([S], F32)
                    T3 = CV.take([S], F32)
                    T4 = CV.take([S], F32)
                    QD = CV.take([S], BF16)
                    KD = CV.take([S], BF16)
                    KLC = [CV.take([CH], BF16) for _ in range(2)]
                    V64 = CV.take([NCK, 128], BF16)
                    SMK = CV.take([S], BF16)
                    YST = [CV.take([S], BF16) for _ in range(2)]
                    ATM = [CV.take([64], BF16) for _ in range(2)]
                    KLT = [CV.take([128], BF16) for _ in range(2)]
                    SBF = CV.take([128], BF16)
                    SQB = CV.take([512], BF16)
                    RS = CV.take([512], F32)
                    POOL("memset", [], ["SMK"], SMK, 1.0)
                    POOL("memset", ["SMK"], ["SMK"], SMK.rearrange("p (c k) -> p c k", k=CH)[:, :, 0:1], 0.0)
                    for h in (range(4) if DBG >= 2 else []):
                        lb_ap = LBT[:, l, h:h + 1]
                        oml_ap = OMLT[:, l, h:h + 1]

                        def cons_z(tg, ps, bkk, T2=T2):
                            sl = slice(tg * 512, (tg + 1) * 512)
                            DVE("tensor_scalar", [bkk], ["T2"], out=T2[:, sl], in0=ps, scalar1=-1.0, scalar2=80.0, op0=ALU.mult, op1=ALU.min)
                        inproj_fm(1024 + h * 128, 128, cons_z)
                        ACT(["T2"], ["T1"], out=T1, in_=T2, func=AF.Exp)
                        ACT(["T1", "lbt"], ["T2"], out=T2, in_=T1, func=AF.Ln, scale=lb_ap, bias=1.0)
                        ACT(["T1"], ["T3"], out=T3, in_=T1, func=AF.Ln, bias=1.0)
                        DVE("tensor_tensor", ["T2", "T3"], ["T2"], out=T2, in0=T2, in1=T3, op=ALU.subtract)
                        DVE("tensor_scalar", ["T2"], ["T2"], out=T2, in0=T2, scalar1=0.0, scalar2=None, op0=ALU.min)
                        DVE("tensor_scalar", ["T1"], ["T3"], out=T3, in0=T1, scalar1=1.0, scalar2=None, op0=ALU.add)
                        DVE("reciprocal", ["T3"], ["T3"], out=T3, in_=T3)
                        DVE("scalar_tensor_tensor", ["T1", "T3", "omlt"], ["T1"], out=T1, in0=T1, scalar=oml_ap, in1=T3, op0=ALU.mult, op1=ALU.mult)
                        DVE("tensor_tensor_scan", ["SMK", "T2"], ["T3"], out=T3, data0=SMK, data1=T2, initial=0.0, op0=ALU.mult, op1=ALU.add)
                        ACT(["T3"], ["T4"], out=T4, in_=T3, func=AF.Exp)
                        ACT(["T3"], ["T2"], out=T2, in_=T3, func=AF.Exp, scale=-1.0)

                        def cons_q(tg, ps, bkk, QD=QD, T4=T4):
                            sl = slice(tg * 512, (tg + 1) * 512)
                            DVE("tensor_tensor", [bkk, "T4"], ["QD"], out=QD[:, sl], in0=ps, in1=T4[:, sl], op=ALU.mult)
                        inproj_fm(512 + h * 128, 128, cons_q)
                        DVE("tensor_tensor", ["T1", "T2"], ["KD"], out=KD, in0=T1, in1=T2, op=ALU.mult)
                        wi = load_w(winv[:, :, 1536 + h * 128:1536 + (h + 1) * 128], 128)
                        for c in range(NCK):
                            bi = rot("bk", 4)
                            bk = banks[bi]
                            for kc in range(KC):
                                MM([WK[wi][0], "XT"], [BK[bi]], bk[0:64, 0:128], lhsT=XT[:, kc, c * 64:(c + 1) * 64], rhs=wsl[wi][:, kc, 0:128],
                                   start=(kc == 0), stop=(kc == KC - 1))
                            ACT([BK[bi]], ["V64"], out=V64[0:64, c, :], in_=bk[0:64, 0:128], func=AF.Identity)
                        DVE("memset", [], ["SST"], SST, 0.0)
                        DVE("memset", [], ["SBF"], SBF, 0.0)
                        ptb = banks[6][:, :].bitcast(BF16)
                        ob = banks[5]
                        for c in range(NCK):
                            cs = slice(c * CH, (c + 1) * CH)
                            ai = c % 2
                            MM(["KD", "QD"], [BK[4]], banks[4][0:64, 0:64], lhsT=KD[:, cs], rhs=QD[:, cs], start=True, stop=True)
                            DVE("tensor_tensor", [BK[4], "triu"], ["ATM%d" % ai], out=ATM[ai][0:64, :], in0=banks[4][0:64, 0:64], in1=triu, op=ALU.mult)
                            DVE("tensor_scalar", ["KD", "T4"], ["KLC%d" % ai], out=KLC[ai], in0=KD[:, cs],
                                scalar1=T4[:, c * CH + CH - 1:c * CH + CH], scalar2=None, op0=ALU.mult)
                            TR(["KLC%d" % ai, "ident_b"], [BK[6]], out=ptb[0:64, 0:128], in_=KLC[ai], identity=ident_b)
                            ACT([BK[6]], ["KLT%d" % ai], out=KLT[ai][0:64, :], in_=ptb[0:64, 0:128], func=AF.Identity)
                            oc = slice((c % 8) * CH, (c % 8 + 1) * CH)
                            MM(["V64", "ATM%d" % ai], [BK[5]], ob[:, oc], lhsT=V64[0:64, c, :], rhs=ATM[ai][0:64, :], start=True, stop=False)
                            MM(["SBF", "QD"], [BK[5]], ob[:, oc], lhsT=SBF, rhs=QD[:, cs], start=False, stop=True)
                            MM(["KLT%d" % ai, "V64"], [BK[7]], banks[7][:, 0:128], lhsT=KLT[ai][0:64, :], rhs=V64[0:64, c, :], start=True, stop=True)
                            DVE("scalar_tensor_tensor", ["SST", "T4", BK[7]], ["SST"], out=SST, in0=SST,
                                scalar=T4[:, c * CH + CH - 1:c * CH + CH], in1=banks[7][:, 0:128], op0=ALU.mult, op1=ALU.add)
                            ACT(["SST"], ["SBF"], out=SBF, in_=SST, func=AF.Identity)
                            if c % 8 == 7:
                                sl = slice((c // 8) * 512, (c // 8 + 1) * 512)
                                ACT([BK[5], "KD"], ["T1"], out=T1[:, sl], in_=ob[:, :], func=AF.Identity)
                        yi = yst_i["i"]; yst_i["i"] ^= 1

                        def cons_g(tg, ps, bkk, T2=T2):
                            sl = slice(tg * 512, (tg + 1) * 512)
                            ACT([bkk, "KD"], ["T2"], out=T2[:, sl], in_=ps, func=AF.Exp, scale=-1.0)
                            DVE("tensor_scalar", ["T2"], ["T2"], out=T2[:, sl], in0=T2[:, sl], scalar1=1.0, scalar2=None, op0=ALU.add)
                            DVE("reciprocal", ["T2"], ["T2"], out=T2[:, sl], in_=T2[:, sl])
                            DVE("tensor_tensor", [bkk, "T2"], ["T2"], out=T2[:, sl], in0=ps, in1=T2[:, sl], op=ALU.mult)
                        inproj_fm(2048 + h * 128, 128, cons_g)
                        for tg in range(NTG):
                            sl = slice(tg * 512, (tg + 1) * 512)
                            ACT(["T1"], ["SQB"], out=SQB, in_=T1[:, sl], func=AF.Square)
                            MM(["SQB", "cstb"], [BK[4]], banks[4][:, :], lhsT=ones_b, rhs=SQB, start=True, stop=True)
                            ACT([BK[4]], ["RS"], out=RS, in_=banks[4][:, :], func=AF.Ln, scale=1.0 / 128, bias=RMS_EPS)
                            ACT(["RS"], ["RS"], out=RS, in_=RS, func=AF.Exp, scale=-0.5)
                            DVE("tensor_tensor", ["T1", "RS"], ["T1"], out=T1[:, sl], in0=T1[:, sl], in1=RS, op=ALU.mult)
                            DVE("scalar_tensor_tensor", ["T1", "T2", "cvt"], ["YST%d" % yi], out=YST[yi][:, sl], in0=T1[:, sl],
                                scalar=cvl[:, 112 + h:113 + h], in1=T2[:, sl], op0=ALU.mult, op1=ALU.mult)
                        DMA("sp", dsem_o[yi], ["YST%d" % yi], ["YD"], out=YD[512 + h * 128:512 + (h + 1) * 128, :], in_=YST[yi])
                    P.barrier()

                    CV.reset(tmp_off)
                    W80 = CV.take([KC, 80], BF16)
                    W8 = CV.take([KC, 8], BF16)
                    CUMP = CV.take([S], F32)
                    R1 = CV.take([S], F32)
                    R2 = CV.take([S], F32)
                    MID = CV.take([S], BF16)
                    CST = CV.take([S], BF16)
                    LH = CV.take([S], BF16)
                    RH = CV.take([S], BF16)
                    QT = CV.take([S], BF16)
                    KT = CV.take([S], BF16)
                    VT = CV.take([NTB, 256], BF16)
                    PT = [CV.take([512], BF16) for _ in range(2)]
                    SM = [CV.take([128], F32) for _ in range(2)]
                    RC = CV.take([512], F32)
                    YST = [CV.take([S], BF16) for _ in range(2)]
                    ONESB = CV.take([S], BF16)
                    POOL("memset", [], ["ONESB"], ONESB, 1.0)
                    DVE("memset", [], ["W80"], W80, 0.0)
                    for off in (0, 8, 32, 40, 64, 72):
                        DMA("pool", dsem_m[1], ["W80"], ["W80d"], out=W80[:, :, off:off + 8], in_=winv[:, :, 5632:5640])
                    nfb_ap = NFBT[0:80, l:l + 1]

                    def cons_f(tg, ps, bkk, R1=R1, R2=R2, nfb_ap=nfb_ap):
                        sl = slice(tg * 512, (tg + 1) * 512)
                        ACT([bkk, "nfbt"], ["R1"], out=R1[0:80, sl], in_=ps, func=AF.Exp, scale=-1.0, bias=nfb_ap)
                        ACT(["R1"], ["R2"], out=R2[0:80, sl], in_=R1[0:80, sl], func=AF.Ln, bias=1.0)
                    inproj_fm(5632, 80, cons_f, wtile=(W80, ["W80", "W80d"]))
                    DVE("tensor_tensor_scan", ["R2", "ONESB"], ["CUMP"], out=CUMP[0:80, :],
                        data0=ONESB[0:80, :], data1=R2[0:80, :], initial=0.0, op0=ALU.mult, op1=ALU.add)
                    DVE("tensor_copy", ["CUMP"], ["CST"], out=CST[0:80, :], in_=CUMP[0:80, :])
                    DVE("tensor_tensor", ["CUMP", "CST"], ["R1"], out=R1[32:64, :], in0=CUMP[32:64, :], in1=CST[32:64, :], op=ALU.subtract)
                    DVE("tensor_tensor", ["CUMP", "CST"], ["R1"], out=R1[64:80, :], in0=CUMP[64:80, :], in1=CST[64:80, :], op=ALU.subtract)
                    DVE("tensor_copy", ["R1"], ["CST"], out=CST[32:48, :], in_=R1[32:48, :])
                    DVE("tensor_copy", ["R1"], ["MID"], out=MID[64:80, :], in_=R1[64:80, :])
                    DVE("tensor_tensor", ["R1", "MID"], ["R2"], out=R2[64:80, :], in0=R1[64:80, :], in1=MID[64:80, :], op=ALU.subtract)
                    DVE("tensor_copy", ["R2"], ["CST"], out=CST[64:80, :], in_=R2[64:80, :])
                    scale_q = 128.0 ** -0.5
                    for h in (range(8) if DBG >= 3 else []):
                        DVE("tensor_scalar", ["CST", "sel"], ["LH"], out=LH[0:80, :], in0=CST[0:80, :], scalar1=sel2[0:80, h:h + 1],
                            scalar2=sel1[0:80, h:h + 1], op0=ALU.mult, op1=ALU.add)
                        DVE("tensor_scalar", ["CST", "sel", "nsel1"], ["RH"], out=RH[0:80, :], in0=CST[0:80, :], scalar1=nsel1[0:80, h:h + 1],
                            scalar2=sel2[0:80, h:h + 1], op0=ALU.mult, op1=ALU.add)

                        def cons_qf(tg, ps, bkk, QT=QT):
                            ACT([bkk], ["QT"], out=QT[:, tg * 512:(tg + 1) * 512], in_=ps, func=AF.Identity, scale=scale_q)
                        inproj_fm(2560 + h * 128, 128, cons_qf)

                        def cons_kf(tg, ps, bkk, KT=KT):
                            ACT([bkk], ["KT"], out=KT[:, tg * 512:(tg + 1) * 512], in_=ps, func=AF.Identity)
                        inproj_fm(3584 + h * 128, 128, cons_kf)
                        if h % 2 == 0:
                            wi = load_w(winv[:, :, 4608 + h * 128:4608 + (h + 2) * 128], 256)
                            for tb in range(NTB):
                                bi = rot("bk", 4)
                                bk = banks[bi]
                                for kc in range(KC):
                                    MM([WK[wi][0], WK[wi][1], "XT"], [BK[bi]], bk[:, 0:256], lhsT=XT[:, kc, tb * 128:(tb + 1) * 128],
                                       rhs=wsl[wi][:, kc, 0:256], start=(kc == 0), stop=(kc == KC - 1))
                                ACT([BK[bi]], ["VT"], out=VT[:, tb, :], in_=bk[:, 0:256], func=AF.Identity)
                        ho = (h % 2) * 128
                        yi = yst_i["i"]; yst_i["i"] ^= 1
                        for G in range(NTG):
                            nkb = 4 * G + 4
                            for j in range(nkb):
                                c0 = max(0, j - 4 * G)
                                cols = slice(c0 * 128, 512)
                                gcols = slice(G * 512 + c0 * 128, G * 512 + 512)
                                sbi = 4 + rot("m", 2)
                                sbk = banks[sbi]
                                pi = rot("t", 2)
                                MM(["KT", "QT"], [BK[sbi]], sbk[:, cols], lhsT=KT[:, j * 128:(j + 1) * 128], rhs=QT[:, gcols], start=True, stop=False)
                                MM(["LH", "RH"], [BK[sbi]], sbk[:, cols], lhsT=LH[0:80, j * 128:(j + 1) * 128], rhs=RH[0:80, gcols], start=False, stop=True)
                                if j >= 4 * G:
                                    dsl = slice(c0 * 128, (c0 + 1) * 128)
                                    DVE("tensor_tensor", [BK[sbi], "maskneg"], ["SM%d" % pi], out=SM[pi], in0=sbk[:, dsl], in1=maskneg, op=ALU.add)
                                    ACT(["SM%d" % pi], ["PT%d" % pi], out=PT[pi][:, dsl], in_=SM[pi], func=AF.Exp)
                                    if c0 < 3:
                                        rsl = slice((c0 + 1) * 128, 512)
                                        ACT([BK[sbi]], ["PT%d" % pi], out=PT[pi][:, rsl], in_=sbk[:, rsl], func=AF.Exp)
                                else:
                                    ACT([BK[sbi]], ["PT%d" % pi], out=PT[pi], in_=sbk[:, :], func=AF.Exp)
                                MM(["VT", "PT%d" % pi], [BK[6]], banks[6][:, cols], lhsT=VT[:, j, ho:ho + 128], rhs=PT[pi][:, cols],
                                   start=(j == 0), stop=(j == nkb - 1))
                                MM(["cstb", "PT%d" % pi], [BK[7]], banks[7][:, cols], lhsT=ones_b, rhs=PT[pi][:, cols],
                                   start=(j == 0), stop=(j == nkb - 1))
                            DVE("reciprocal", [BK[7]], ["RC"], out=RC, in_=banks[7][:, :])
                            DVE("tensor_tensor", [BK[6], "RC"], ["YST%d" % yi], out=YST[yi][:, G * 512:(G + 1) * 512], in0=banks[6][:, :], in1=RC, op=ALU.mult)
                        DMA("sp", dsem_o[yi], ["YST%d" % yi], ["YD"], out=YD[1024 + h * 128:1024 + (h + 1) * 128, :], in_=YST[yi])
                    P.barrier()

                    def ln_block(PRE, pkey, tcols, gcol, bcol, tmp, XT=XT, cvl=cvl):
                        XB, SQ, MU, M2, RSTD = tmp
                        for nb in range(KC):
                            xi = nb % 2
                            ACT([pkey], ["XB%d" % xi], out=XB[xi], in_=PRE[:, nb, :], func=AF.Identity)
                            ACT([pkey], ["SQ%d" % xi], out=SQ[xi], in_=PRE[:, nb, :], func=AF.Square)
                            MM(["XB%d" % xi, "cstb"], [BK[6]], banks[6][:, :], lhsT=ones_b, rhs=XB[xi], start=(nb == 0), stop=(nb == KC - 1))
                            MM(["SQ%d" % xi, "cstb"], [BK[7]], banks[7][:, :], lhsT=ones_b, rhs=SQ[xi], start=(nb == 0), stop=(nb == KC - 1))
                        DVE("tensor_scalar", [BK[6]], ["MU"], out=MU, in0=banks[6][:, :], scalar1=1.0 / D, scalar2=None, op0=ALU.mult)
                        DVE("tensor_tensor", ["MU"], ["M2"], out=M2, in0=MU, in1=MU, op=ALU.mult)
                        DVE("scalar_tensor_tensor", [BK[7], "M2"], ["M2"], out=M2, in0=banks[7][:, :], scalar=1.0 / D, in1=M2, op0=ALU.mult, op1=ALU.subtract)
                        ACT(["M2"], ["RSTD"], out=RSTD, in_=M2, func=AF.Ln, bias=LN_EPS)
                        ACT(["RSTD"], ["RSTD"], out=RSTD, in_=RSTD, func=AF.Exp, scale=-0.5)
                        for nb in range(KC):
                            DVE("tensor_tensor", [pkey, "MU"], [pkey], out=PRE[:, nb, :], in0=PRE[:, nb, :], in1=MU, op=ALU.subtract)
                            DVE("tensor_tensor", [pkey, "RSTD"], [pkey], out=PRE[:, nb, :], in0=PRE[:, nb, :], in1=RSTD, op=ALU.mult)
                            ACT([pkey, "cvt"], [pkey], out=PRE[:, nb, :], in_=PRE[:, nb, :], func=AF.Identity,
                                scale=cvl[:, gcol + nb:gcol + nb + 1], bias=cvl[:, bcol + nb:bcol + nb + 1])
                            DVE("tensor_copy", [pkey], ["XT"], out=XT[:, nb, tcols], in_=PRE[:, nb, :])

                    CV.reset(off_after_xt)
                    wsl = [CV.take([KC, 256], BF16) for _ in range(NWS)]
                    YT = [CV.take([KC, 512], BF16) for _ in range(2)]
                    PREb = [CV.take([KC, 512], F32) for _ in range(2)]
                    lntmp = ([CV.take([512], BF16) for _ in range(2)], [CV.take([512], BF16) for _ in range(2)],
                             CV.take([512], F32), CV.take([512], F32), CV.take([512], F32))
                    woutv = w_out_d[l].rearrange("(c p) n -> p c n", p=128)

                    def load_w2(dview, wsl=wsl):
                        i = wslot_take()
                        DMA("pool", wsem[i][0], [], [WK[i][0], WK[i][1]], out=wsl[i][:, :, 0:256], in_=dview)
                        return i
                    for tg in (range(NTG) if DBG >= 4 else []):
                        tcols = slice(tg * 512, (tg + 1) * 512)
                        b = tg % 2
                        DMA("sp", dsem_x[b], ["YD"], ["YT%d" % b], out=YT[b], in_=YDv[:, :, tcols])
                        DMA("sp", dsem_m[2 + b], ["XR"], ["PRE%d" % b], out=PREb[b], in_=XRv[:, :, tcols])
                        wis = {0: load_w2(woutv[:, :, 0:256])}
                        for np2 in range(KC // 2):
                            if np2 + 1 < KC // 2:
                                wis[np2 + 1] = load_w2(woutv[:, :, (np2 + 1) * 256:(np2 + 2) * 256])
                            wi = wis[np2]
                            for hf in range(2):
                                nb = np2 * 2 + hf
                                bi = rot("bk", 4)
                                bk = banks[bi]
                                for kc in range(KC):
                                    MM([WK[wi][hf], "YT%d" % b], [BK[bi]], bk[:, :], lhsT=wsl[wi][:, kc, hf * 128:(hf + 1) * 128],
                                       rhs=YT[b][:, kc, :], start=(kc == 0), stop=(kc == KC - 1))
                                DVE("scalar_tensor_tensor", ["PRE%d" % b, BK[bi]], ["PRE%d" % b], out=PREb[b][:, nb, :], in0=PREb[b][:, nb, :],
                                    scalar=ALPHA, in1=bk[:, :], op0=ALU.mult, op1=ALU.add)
                        ln_block(PREb[b], "PRE%d" % b, tcols, 0, 16, lntmp)
                        DMA("sp", dsem_o[b], ["PRE%d" % b], ["XR"], out=XRv[:, :, tcols], in_=PREb[b])
                    P.barrier()

                    CV.reset(off_after_xt)
                    wsl = [CV.take([KC, 256], BF16) for _ in range(NWS)]
                    ACC = CV.take([KC, TH], F32)
                    AT = CV.take([GF, TH], BF16)
                    GS = [CV.take([2 + 512], F32) for _ in range(2)]
                    CC = [CV.take([512], F32) for _ in range(2)]
                    EE = [CV.take([512], F32) for _ in range(2)]
                    lntmp = ([CV.take([512], BF16) for _ in range(2)], [CV.take([512], BF16) for _ in range(2)],
                             CV.take([512], F32), CV.take([512], F32), CV.take([512], F32))
                    wgv = w_gate_d[l].rearrange("(c p) n -> p c n", p=128)
                    wvv = w_val_d[l].rearrange("(c p) n -> p c n", p=128)
                    wdv = w_down_d[l].rearrange("(f p) n -> p f n", p=128)
                    DVE("memset", [], ["HALO"], HALO, 0.0)
                    NG = NFB // GF

                    def load_gv(f, wsl=wsl, wgv=wgv, wvv=wvv):
                        i = wslot_take()
                        DMA("pool", wsem[i][0], [], [WK[i][0]], out=wsl[i][:, :, 0:128], in_=wgv[:, :, f * 128:(f + 1) * 128])
                        DMA("pool", wsem[i][1], [], [WK[i][1]], out=wsl[i][:, :, 128:256], in_=wvv[:, :, f * 128:(f + 1) * 128])
                        return i

                    def load_wd(f0, np2, wsl=wsl, wdv=wdv):
                        i = wslot_take()
                        DMA("pool", wsem[i][0], [], [WK[i][0], WK[i][1]], out=wsl[i][:, 0:GF, :], in_=wdv[:, f0:f0 + GF, np2 * 256:(np2 + 1) * 256])
                        return i
                    for hfq in (range(NHALF) if DBG >= 5 else []):
                        t0 = hfq * TH
                        DMA("sp", dsem_m[2], ["XR"], ["ACC"], out=ACC, in_=XRv[:, :, t0:t0 + TH])
                        for nb in range(KC):
                            ACT(["ACC"], ["ACC"], out=ACC[:, nb, :], in_=ACC[:, nb, :], func=AF.Identity, scale=ALPHA)
                        for g in range(NG):
                            wg = {0: load_gv(g * GF)}
                            for fi in range(GF):
                                f = g * GF + fi
                                if fi + 1 < GF:
                                    wg[fi + 1] = load_gv(f + 1)
                                wi = wg[fi]
                                cw = [cvl[:, 120 + j * 44 + f:121 + j * 44 + f] for j in range(3)]
                                cb = cvl[:, 64 + f:65 + f]
                                for tg in range(NTGH):
                                    tsl = slice(t0 + tg * 512, t0 + (tg + 1) * 512)
                                    si = rot("t", 2)
                                    bgi = si * 2
                                    bvi = si * 2 + 1
                                    for (bi, hf) in ((bgi, 0), (bvi, 1)):
                                        for kc in range(KC):
                                            MM([WK[wi][hf], "XT"], [BK[bi]], banks[bi][:, :], lhsT=wsl[wi][:, kc, hf * 128:(hf + 1) * 128],
                                               rhs=XT[:, kc, tsl], start=(kc == 0), stop=(kc == KC - 1))
                                    gs_, cc_, ee_ = GS[si], CC[si], EE[si]
                                    gk, ck, ek = "GS%d" % si, "CC%d" % si, "EE%d" % si
                                    ACT(["HALO"], [gk], out=gs_[:, 0:2], in_=HALO[:, f, :], func=AF.Identity)
                                    ACT([BK[bgi]], [gk], out=gs_[:, 2:514], in_=banks[bgi][:, :], func=AF.Identity)
                                    ACT([gk], ["HALO"], out=HALO[:, f, :], in_=gs_[:, 512:514], func=AF.Identity)
                                    ACT([gk, "cvt"], [ck], out=cc_, in_=gs_[:, 2:514], func=AF.Identity, scale=cw[2], bias=cb)
                                    DVE("scalar_tensor_tensor", [gk, ck, "cvt"], [ck], out=cc_, in0=gs_[:, 1:513], scalar=cw[1], in1=cc_, op0=ALU.mult, op1=ALU.add)
                                    DVE("scalar_tensor_tensor", [gk, ck, "cvt"], [ck], out=cc_, in0=gs_[:, 0:512], scalar=cw[0], in1=cc_, op0=ALU.mult, op1=ALU.add)
                                    ACT([ck], [ek], out=ee_, in_=cc_, func=AF.Exp, scale=-1.0)
                                    DVE("tensor_scalar", [ek], [ek], out=ee_, in0=ee_, scalar1=1.0, scalar2=None, op0=ALU.add)
                                    DVE("reciprocal", [ek], [ek], out=ee_, in_=ee_)
                                    DVE("tensor_tensor", [ck, ek], [ck], out=cc_, in0=cc_, in1=ee_, op=ALU.mult)
                                    DVE("tensor_tensor", [ck, BK[bvi]], ["AT"], out=AT[:, fi, tg * 512:(tg + 1) * 512], in0=cc_, in1=banks[bvi][:, :], op=ALU.mult)
                            f0 = g * GF
                            wd = {0: load_wd(f0, 0)}
                            for np2 in range(KC // 2):
                                if np2 + 1 < KC // 2:
                                    wd[np2 + 1] = load_wd(f0, np2 + 1)
                                wi = wd[np2]
                                for hf in range(2):
                                    nb = np2 * 2 + hf
                                    for tg in range(NTGH):
                                        bi = 4 + rot("m", 2)
                                        for fi in range(GF):
                                            MM([WK[wi][hf], "AT"], [BK[bi]], banks[bi][:, :], lhsT=wsl[wi][:, fi, hf * 128:(hf + 1) * 128],
                                               rhs=AT[:, fi, tg * 512:(tg + 1) * 512], start=(fi == 0), stop=(fi == GF - 1))
                                        DVE("tensor_tensor", ["ACC", BK[bi]], ["ACC"], out=ACC[:, nb, tg * 512:(tg + 1) * 512],
                                            in0=ACC[:, nb, tg * 512:(tg + 1) * 512], in1=banks[bi][:, :], op=ALU.add)
                        for tg in range(NTGH):
                            tcols = slice(t0 + tg * 512, t0 + (tg + 1) * 512)
                            ln_block(ACC[:, :, tg * 512:(tg + 1) * 512], "ACC", tcols, 32, 48, lntmp)
                        DMA("sp", dsem_o[0], ["ACC"], ["XR"], out=XRv[:, :, t0:t0 + TH], in_=ACC)
                    P.barrier()

                CV.reset(off_after_xt)
                xo_in = [CV.take([KC, 128], F32) for _ in range(2)]
                xo_st = [CV.take([D], F32) for _ in range(2)]
                for tb in range(NTB):
                    b = tb % 2
                    DMA("sp", dsem_x[b], ["XR"], ["xoin%d" % b], out=xo_in[b], in_=XRv[:, :, tb * 128:(tb + 1) * 128])
                    for q4 in range(4):
                        bi = rot("bk", 2)
                        bk = banks[bi]
                        for j in range(4):
                            dc = q4 * 4 + j
                            TR(["xoin%d" % b, "ident_f"], [BK[bi]], out=bk[:, j * 128:(j + 1) * 128], in_=xo_in[b][:, dc, :], identity=ident_f)
                        DVE("tensor_copy", [BK[bi]], ["xost%d" % b], out=xo_st[b][:, q4 * 512:(q4 + 1) * 512], in_=bk[:, :])
                    DMA("sp", dsem_o[b], ["xost%d" % b], ["OUT"], out=out_d[tok0 + tb * 128: tok0 + (tb + 1) * 128, :], in_=xo_st[b])
                P.barrier()
        except _Stop:
            pass
        P.barrier()
        P.emit()
    return nc


_NC_CACHE = {}

WNAMES = ["w_in", "fox_f_bias", "pool_w", "pool_scale", "hgrn_lb_logits", "hgrn_norm_g", "w_out", "ln1_g", "ln1_b",
          "w_gate", "w_val", "conv_w", "conv_b", "w_down", "ln2_g", "ln2_b"]


def kernel(**inputs):
    x = np.ascontiguousarray(inputs["x"], dtype=np.float32)
    B, S, Dm = x.shape
    L = inputs["w_in"].shape[0]
    ncores = 8
    nseq = B // ncores
    key = (S, nseq, L)
    if key not in _NC_CACHE:
        _NC_CACHE[key] = build(S, nseq, L)
    nc = _NC_CACHE[key]
    ws = {k: np.ascontiguousarray(inputs[k], dtype=np.float32) for k in WNAMES}
    in_maps = []
    for c in range(ncores):
        m = {"x": x[c * nseq:(c + 1) * nseq].reshape(nseq * S, Dm)}
        m.update(ws)
        in_maps.append(m)
    res = run_bass_kernel_spmd(nc, in_maps, core_ids=list(range(ncores)))
    outs = [np.asarray(r["out"]).reshape(nseq, S, Dm) for r in res.results]
    return np.concatenate(outs, axis=0).astype(np.float32)
```

```python
import contextlib
import os
import numpy as np
import concourse.bass as bass
import concourse.mybir as mybir
from concourse.bass_utils import run_bass_kernel_spmd

F32 = mybir.dt.float32
BF16 = mybir.dt.bfloat16
AF = mybir.ActivationFunctionType
ALU = mybir.AluOpType

SAME_ENGINE_SYNC = True

D = 2048
KC = 16
DFF = 5632
NFB = 44
INC = 5640
ALPHA = 8.0 ** 0.25
LN_EPS = 1e-5
RMS_EPS = 1e-6
CH = 64


class Prog:
    ENGS = ("pe", "act", "dve", "pool", "sp")

    def __init__(self, nc):
        self.nc = nc
        self.q = {e: [] for e in self.ENGS}
        self.last_write = {}
        self.reads_since = {}
        self.seen = {e: {} for e in self.ENGS}
        self.needed = {e: set() for e in self.ENGS}
        self.dma_cnt = {}
        self.dma_sems = []

    def dma_sem(self, name):
        self.dma_cnt[name] = 0
        self.dma_sems.append(name)
        return name

    def _deps(self, reads, writes):
        deps = []
        for k in reads:
            ev = self.last_write.get(k)
            if ev is not None:
                deps.append(ev)
        for k in writes:
            ev = self.last_write.get(k)
            if ev is not None:
                deps.append(ev)
            deps.extend(self.reads_since.get(k, ()))
        return deps

    def _commit(self, ev, reads, writes):
        for k in reads:
            self.reads_since.setdefault(k, []).append(ev)
        for k in writes:
            self.last_write[k] = ev
            self.reads_since[k] = []

    def _waits(self, eng, deps):
        waits = {}
        seen = self.seen[eng]
        for (sk, idx) in deps:
            if sk == eng and (eng == "pe" or not SAME_ENGINE_SYNC):
                continue
            if seen.get(sk, -1) >= idx:
                continue
            if waits.get(sk, -1) < idx:
                waits[sk] = idx
        for sk, idx in waits.items():
            seen[sk] = idx
            if sk in self.needed:
                self.needed[sk].add(idx)
        return waits

    def op(self, eng, fn, reads=(), writes=()):
        writes = list(writes) + [k for k in reads if k.startswith("pb") and k not in writes]
        deps = self._deps(reads, writes)
        waits = self._waits(eng, deps)
        idx = len(self.q[eng])
        self.q[eng].append(("op", fn, waits, None))
        ev = (eng, idx)
        self._commit(ev, reads, writes)
        return ev

    def dma(self, queue, sem, fn, reads=(), writes=()):
        deps = self._deps(reads, writes)
        waits = self._waits(queue, deps)
        self.dma_cnt[sem] += 16
        ev = (sem, self.dma_cnt[sem])
        self.q[queue].append(("dma", fn, waits, sem))
        self._commit(ev, reads, writes)
        return ev

    def barrier(self):
        deps = []
        for e in self.ENGS:
            for i in range(len(self.q[e]) - 1, -1, -1):
                if self.q[e][i][0] == "op":
                    deps.append((e, i))
                    break
        for s in self.dma_sems:
            if self.dma_cnt[s] > 0:
                deps.append((s, self.dma_cnt[s]))
        for e in self.ENGS:
            waits = self._waits(e, [d for d in deps if d[0] != e])
            self.q[e].append(("wait", None, waits, None))

    def emit(self):
        nc = self.nc
        with contextlib.ExitStack() as st:
            sems = {}
            for e in self.ENGS:
                sems[e] = st.enter_context(nc.semaphore("s_" + e))
            for d in self.dma_sems:
                sems[d] = st.enter_context(nc.semaphore("d_" + d))
            val = {}
            for e in self.ENGS:
                c = 0
                v = {}
                for i in range(len(self.q[e])):
                    if i in self.needed[e]:
                        c += 1
                        v[i] = c
                val[e] = v
            block = st.enter_context(nc.Block())

            def run(e, engh):
                needed = self.needed[e]
                for i, (kind, fn, waits, dsem) in enumerate(self.q[e]):
                    for sk, idx in waits.items():
                        if sk in val:
                            engh.wait_ge(sems[sk], val[sk][idx])
                        else:
                            engh.wait_ge(sems[sk], idx)
                    if kind == "wait":
                        continue
                    mname, margs, mkw = fn
                    bi = getattr(engh, mname)(*margs, **mkw)
                    if kind == "dma":
                        bi.then_inc(sems[dsem], 16)
                    elif i in needed:
                        bi.then_inc(sems[e], 1)

            @block.tensor
            def _(eh):
                run("pe", eh)

            @block.scalar
            def _(eh):
                run("act", eh)

            @block.vector
            def _(eh):
                run("dve", eh)

            @block.gpsimd
            def _(eh):
                run("pool", eh)

            @block.sync
            def _(eh):
                run("sp", eh)


class _Stop(Exception):
    pass


def build(S, NSEQ, L, GF=11):
    assert S % 512 == 0
    NTG = S // 512
    NTB = S // 128
    NCK = S // CH
    TH = min(1024, S)
    NHALF = S // TH
    NTGH = TH // 512
    nc = bass.Bass("TRN2", target_bir_lowering=False)
    P = Prog(nc)
    DBG = int(os.environ.get("KDBG", "9"))
    KSUB = int(os.environ.get("KSUB", "9"))

    def ACT(reads, writes, **kw):
        P.op("act", ("activation", (), kw), reads, writes)

    def MM(reads, writes, out, **kw):
        P.op("pe", ("matmul", (out,), kw), reads, writes)

    def TR(reads, writes, **kw):
        P.op("pe", ("transpose", (), kw), reads, writes)

    def DVE(m, reads, writes, *a, **kw):
        P.op("dve", (m, a, kw), reads, writes)

    def POOL(m, reads, writes, *a, **kw):
        P.op("pool", (m, a, kw), reads, writes)

    def DMA(queue, sem, reads, writes, **kw):
        P.dma(queue, sem, ("dma_start", (), kw), reads, writes)

    def din(name, shape):
        return nc.dram_tensor(name, shape, F32, kind="ExternalInput").ap()

    x_d = din("x", [NSEQ * S, D])
    w_in_d = din("w_in", [L, D, INC])
    fbias_d = din("fox_f_bias", [L, 8])
    pool_w_d = din("pool_w", [L, 4, 128, 128])
    pool_scale_d = din("pool_scale", [L, 512])
    lbl_d = din("hgrn_lb_logits", [L, 512])
    normg_d = din("hgrn_norm_g", [L, 512])
    w_out_d = din("w_out", [L, D, D])
    ln1g_d = din("ln1_g", [L, D])
    ln1b_d = din("ln1_b", [L, D])
    w_gate_d = din("w_gate", [L, D, DFF])
    w_val_d = din("w_val", [L, D, DFF])
    conv_w_d = din("conv_w", [L, 3, DFF])
    conv_b_d = din("conv_b", [L, DFF])
    w_down_d = din("w_down", [L, DFF, D])
    ln2g_d = din("ln2_g", [L, D])
    ln2b_d = din("ln2_b", [L, D])
    out_d = nc.dram_tensor("out", [NSEQ * S, D], F32, kind="ExternalOutput").ap()
    XR = nc.dram_tensor("xr_scr", [D, S], F32).ap()
    YD = nc.dram_tensor("y_scr", [D, S], BF16).ap()
    XRv = XR.rearrange("(c p) t -> p c t", p=128)
    YDv = YD.rearrange("(c p) t -> p c t", p=128)

    st = contextlib.ExitStack()
    with st:
        def sb(name, shape, dt):
            return st.enter_context(nc.sbuf_tensor(name, shape, dt))

        ARENA_B = 198 * 1024
        arena = sb("arena", [128, ARENA_B // 2], BF16)
        NCST = 128 * 4 + 16 + 8 * 4 + L * 8 + L + 2 + L * 256 + 2 * 128
        cst = sb("cst", [128, NCST], F32)
        cstb = sb("cstb", [128, 128 + 128 + 64], BF16)
        banks = [st.enter_context(nc.psum_tensor("pb%d" % i, [128, 512], F32)) for i in range(8)]
        BK = ["pb%d" % i for i in range(8)]

        class Carver:
            def __init__(self):
                self.off = 0

            def reset(self, off=0):
                self.off = off

            def take(self, free_shape, dt):
                n = int(np.prod(free_shape))
                esz = 4 if dt == F32 else 2
                nb = n * esz
                nbp = (nb + 63) // 64 * 64
                assert self.off + nbp <= ARENA_B, (self.off, nbp, ARENA_B)
                a = arena[:, self.off // 2: self.off // 2 + nb // 2]
                self.off += nbp
                if dt == F32:
                    a = a.bitcast(F32)
                if len(free_shape) == 2:
                    a = a.rearrange("p (a b) -> p a b", b=free_shape[1])
                elif len(free_shape) == 3:
                    a = a.rearrange("p (a b c) -> p a b c", b=free_shape[1], c=free_shape[2])
                return a

        CV = Carver()
        try:

            o = 0
            ident_f = cst[:, o:o + 128]; o += 128
            maskneg = cst[:, o:o + 128]; o += 128
            seltmp = cst[:, o:o + 128]; o += 128
            o += 128
            invc = cst[:, o:o + 16]; o += 16
            sel1 = cst[:, o:o + 8]; o += 8
            sel2 = cst[:, o:o + 8]; o += 8
            nsel1 = cst[:, o:o + 8]; o += 8
            o += 8
            LBT = cst[:, o:o + L * 4].rearrange("p (l h) -> p l h", h=4); o += L * 4
            OMLT = cst[:, o:o + L * 4].rearrange("p (l h) -> p l h", h=4); o += L * 4
            NFBT = cst[:, o:o + L]; o += L
            ones_col = cst[:, o:o + 1]; o += 1
            o += 1
            CVT = cst[:, o:o + L * 256].rearrange("p (l c) -> p l c", c=256); o += L * 256
            HALO = cst[:, o:o + 128].rearrange("p (f j) -> p f j", j=2)[:, 0:NFB, :]; o += 128
            SST = cst[:, o:o + 128]; o += 128
            assert o <= NCST
            ident_b = cstb[:, 0:128]
            ones_b = cstb[:, 128:256]
            triu = cstb[0:64, 256:320]

            POOL("memset", [], ["cst"], cst[:], 0.0)
            POOL("memset", ["cst"], ["ident_f"], ident_f, 1.0)
            POOL("affine_select", ["ident_f"], ["ident_f"], out=ident_f, in_=ident_f, pattern=[[1, 128]],
                 compare_op=ALU.is_equal, fill=0.0, base=0, channel_multiplier=-1)
            POOL("memset", [], ["cstb"], cstb[:], 1.0)
            POOL("affine_select", ["cstb"], ["ident_b"], out=ident_b, in_=ident_b, pattern=[[1, 128]],
                 compare_op=ALU.is_equal, fill=0.0, base=0, channel_multiplier=-1)
            POOL("affine_select", ["cstb"], ["triu"], out=triu, in_=triu, pattern=[[1, 64]],
                 compare_op=ALU.is_ge, fill=0.0, base=0, channel_multiplier=-1)
            POOL("affine_select", ["cst"], ["maskneg"], out=maskneg, in_=maskneg, pattern=[[1, 128]],
                 compare_op=ALU.is_ge, fill=-30000.0, base=0, channel_multiplier=-1)
            POOL("memset", ["cst"], ["ones_col"], ones_col, 1.0)
            for t in range(16):
                POOL("memset", ["cst"], ["invc"], invc[:, t:t + 1], 1.0 / (t + 1))
            for si, (dst, offs) in enumerate(((sel1, (0, 32, 64)), (sel2, (8, 40, 72)))):
                for oi, off in enumerate(offs):
                    tmp = seltmp[:, (si * 3 + oi) * 8:(si * 3 + oi) * 8 + 8]
                    POOL("memset", ["cst"], ["seltmp"], tmp, 1.0)
                    POOL("affine_select", ["seltmp"], ["seltmp"], out=tmp, in_=tmp, pattern=[[-1, 8]],
                         compare_op=ALU.is_equal, fill=0.0, base=-off, channel_multiplier=1)
                    POOL("tensor_tensor", ["seltmp", "cst"], ["sel"], out=dst, in0=dst, in1=tmp, op=ALU.add)
            POOL("tensor_scalar", ["sel"], ["nsel1"], out=nsel1, in0=sel1, scalar1=-1.0, scalar2=None, op0=ALU.mult)

            if DBG == -3:
                raise _Stop()
            CV.reset()
            stgf = CV.take([L * 256], F32)
            dsem_c = P.dma_sem("c")
            DVE("memset", [], ["stg"], stgf, 0.0)
            for l in range(L):
                s0 = stgf[:, (l * 2) * 128:(l * 2 + 1) * 128]
                s1 = stgf[:, (l * 2 + 1) * 128:(l * 2 + 2) * 128]
                rows = [(ln1g_d[l], 0, 16), (ln1b_d[l], 16, 16), (ln2g_d[l], 32, 16), (ln2b_d[l], 48, 16),
                        (conv_b_d[l], 64, 44), (pool_scale_d[l], 108, 4), (normg_d[l], 112, 4), (lbl_d[l], 116, 4)]
                for (src, r0, nr) in rows:
                    DMA("sp", dsem_c, ["stg"], ["stgd"], out=s0[r0:r0 + nr, :], in_=src.rearrange("(c p) -> c p", p=128))
                cwv = conv_w_d[l].rearrange("j (c p) -> (j c) p", p=128)
                DMA("sp", dsem_c, ["stg"], ["stgd"], out=s0[120:128, :], in_=cwv[0:8, :])
                DMA("sp", dsem_c, ["stg"], ["stgd"], out=s1[0:124, :], in_=cwv[8:132, :])
                for off in (0, 8, 32, 40, 64, 72):
                    DMA("sp", dsem_c, ["cst"], ["nfbt"], out=NFBT[off:off + 8, l:l + 1],
                        in_=fbias_d[l].rearrange("(h o) -> h o", o=1))
            P.barrier()
            for l in range(L):
                for hh in range(2):
                    bk = banks[hh]
                    TR(["stgd", "stg", "ident_f"], [BK[hh]], out=bk[:, 0:128], in_=stgf[:, (l * 2 + hh) * 128:(l * 2 + hh + 1) * 128], identity=ident_f)
                    DVE("tensor_copy", [BK[hh], "cst"], ["cvt"], out=CVT[:, l, hh * 128:(hh + 1) * 128], in_=bk[:, 0:128])
            DVE("tensor_scalar", ["nfbt"], ["nfbt"], out=NFBT, in0=NFBT, scalar1=-1.0, scalar2=None, op0=ALU.mult)
            if DBG == -2:
                raise _Stop()
            EL = CV.take([L, 4], F32)
            TOT = CV.take([4], F32)
            for l in range(L):
                ACT(["cvt"], ["el"], out=EL[:, l, :], in_=CVT[:, l, 116:120], func=AF.Exp)
            DVE("tensor_copy", ["el"], ["tot"], out=TOT, in_=EL[:, 0, :])
            for l in range(1, L):
                DVE("tensor_tensor", ["el", "tot"], ["tot"], out=TOT, in0=TOT, in1=EL[:, l, :], op=ALU.add)
            DVE("reciprocal", ["tot"], ["tot"], out=TOT, in_=TOT)
            for l in range(1, L):
                DVE("tensor_tensor", ["el", "tot"], ["el"], out=EL[:, l, :], in0=EL[:, l, :], in1=TOT, op=ALU.mult)
                DVE("tensor_tensor", ["el", "lbt", "cst"], ["lbt"], out=LBT[:, l, :], in0=LBT[:, l - 1, :], in1=EL[:, l, :], op=ALU.add)
            DVE("tensor_scalar", ["lbt", "cst"], ["omlt"], out=OMLT, in0=LBT, scalar1=-1.0, scalar2=1.0, op0=ALU.mult, op1=ALU.add)
            P.barrier()

            if DBG == -1:
                raise _Stop()
            NWS = 3
            wsem = [[P.dma_sem("w%d_%d" % (i, j)) for j in range(2)] for i in range(NWS)]
            WK = [["ws%d_%d" % (i, j) for j in range(2)] for i in range(NWS)]
            wstate = {"i": 0}

            def wslot_take():
                i = wstate["i"]
                wstate["i"] = (i + 1) % NWS
                return i

            class WPlan:
                def __init__(self, loaders):
                    self.loaders = loaders
                    self.issued = 0
                    self.consumed = 0
                    self.slots = {}

                def next(self, tag=None):
                    while self.issued < len(self.loaders) and self.issued < self.consumed + NWS:
                        i = wslot_take()
                        t, fn = self.loaders[self.issued]
                        fn(i)
                        self.slots[self.issued] = (i, t)
                        self.issued += 1
                    i, t = self.slots.pop(self.consumed)
                    assert tag is None or t == tag, (t, tag)
                    self.consumed += 1
                    return i

            dsem_x = [P.dma_sem("x0"), P.dma_sem("x1")]
            dsem_o = [P.dma_sem("o0"), P.dma_sem("o1")]
            dsem_m = [P.dma_sem("m%d" % i) for i in range(4)]
            cnt = {"bk": 0, "m": 0, "t": 0}

            def rot(name, n):
                v = cnt[name]
                cnt[name] = (v + 1) % n
                return v % n

            for sq in range(NSEQ):
                tok0 = sq * S
                CV.reset()
                XT = CV.take([KC, S], BF16)
                off_after_xt = CV.off
                xin = [CV.take([D], F32) for _ in range(2)]
                xstf = [CV.take([KC * 128], F32) for _ in range(2)]
                xst = [a.rearrange("p (a b) -> p a b", b=128) for a in xstf]
                for tb in range(NTB):
                    b = tb % 2
                    DMA("sp", dsem_x[b], [], ["xin%d" % b], out=xin[b], in_=x_d[tok0 + tb * 128: tok0 + (tb + 1) * 128, :])
                    for q4 in (range(4) if KSUB >= 0 else []):
                        bi = rot("bk", 2)
                        bk = banks[bi]
                        for j in range(4):
                            dc = q4 * 4 + j
                            TR(["xin%d" % b, "ident_f"], [BK[bi]], out=bk[:, j * 128:(j + 1) * 128],
                               in_=xin[b][:, dc * 128:(dc + 1) * 128], identity=ident_f)
                        if KSUB >= 1:
                            for j in range(4):
                                ACT([BK[bi]], ["XT"], out=XT[:, q4 * 4 + j, tb * 128:(tb + 1) * 128], in_=bk[:, j * 128:(j + 1) * 128], func=AF.Identity)
                        if KSUB >= 2:
                            DVE("tensor_copy", [BK[bi]], ["xst%d" % b], out=xstf[b][:, q4 * 512:(q4 + 1) * 512], in_=bk[:, :])
                    if KSUB >= 3:
                        DMA("sp", dsem_o[b], ["xst%d" % b], ["XR"], out=XRv[:, :, tb * 128:(tb + 1) * 128], in_=xst[b])
                P.barrier()
                if KSUB < 4:
                    raise _Stop()

                for l in range(L):
                    cvl = CVT[:, l, :]
                    CV.reset(off_after_xt)
                    wsl = [CV.take([KC, 256], BF16) for _ in range(NWS)]

                    winv = w_in_d[l].rearrange("(c p) n -> p c n", p=128)

                    def mk_ld(col0, ncols, wsl=wsl, winv=winv):
                        def fn(i):
                            DMA("pool", wsem[i][0], [], [WK[i][0]] if ncols <= 128 else [WK[i][0], WK[i][1]],
                                out=wsl[i][:, :, 0:ncols], in_=winv[:, :, col0:col0 + ncols])
                        return (col0, fn)
                    plan1 = []
                    if DBG >= 1:
                        plan1 += [mk_ld(g * 128, 128) for g in range(4)]
                    if DBG >= 2:
                        for h in range(4):
                            plan1 += [mk_ld(1024 + h * 128, 128), mk_ld(512 + h * 128, 128), mk_ld(1536 + h * 128, 128), mk_ld(2048 + h * 128, 128)]
                    if DBG >= 3:
                        for h in range(8):
                            plan1 += [mk_ld(2560 + h * 128, 128), mk_ld(3584 + h * 128, 128)]
                            if h % 2 == 0:
                                plan1.append(mk_ld(4608 + h * 128, 256))
                    WP1 = WPlan(plan1)

                    def inproj_fm(col0, m, consume, wtile=None, nbk=4, wsl=wsl, XT=XT, WP1=WP1):
                        if wtile is None:
                            i = WP1.next(col0)
                            wt, wk = wsl[i][:, :, 0:m], [WK[i][0]]
                        else:
                            wt, wk = wtile
                        for tg in range(NTG):
                            bi = rot("bk", nbk)
                            bk = banks[bi]
                            for kc in range(KC):
                                MM(wk + ["XT"], [BK[bi]], bk[0:m, :], lhsT=wt[:, kc, :], rhs=XT[:, kc, tg * 512:(tg + 1) * 512],
                                   start=(kc == 0), stop=(kc == KC - 1))
                            consume(tg, bk[0:m, :], BK[bi])

                    tmp_off = CV.off
                    yst_i = {"i": 0}

                    CV.reset(tmp_off)
                    U = CV.take([16 + S], F32)
                    A = CV.take([16 + S], F32)
                    B = CV.take([16 + S], F32)
                    Dt = CV.take([S], BF16)
                    PW = CV.take([128], BF16)
                    YST = [CV.take([S], BF16) for _ in range(2)]
                    small = CV.take([32], F32)
                    for buf, nm in ((U, "U"), (A, "A"), (B, "B")):
                        POOL("memset", [], [nm], buf[:, 0:16], 0.0)
                    for g in (range(4) if DBG >= 1 else []):
                        w = 2 ** (g + 1)
                        DMA("pool", dsem_m[0], [], ["PW"], out=PW, in_=pool_w_d[l, g])

                        def cons_u(tg, ps, bkk, U=U):
                            ACT([bkk], ["U"], out=U[:, 16 + tg * 512:16 + (tg + 1) * 512], in_=ps, func=AF.Identity)
                        inproj_fm(g * 128, 128, cons_u)
                        src, sn = U, "U"
                        pp = [(A, "A"), (B, "B")]
                        for stp in range(g + 1):
                            sh = 2 ** stp
                            dst, dn = pp[stp % 2]
                            DVE("tensor_tensor", [sn], [dn], out=dst[:, 16:16 + S], in0=src[:, 16:16 + S],
                                in1=src[:, 16 - sh:16 - sh + S], op=ALU.add)
                            src, sn = dst, dn
                        DVE("scalar_tensor_tensor", [sn, "U"], ["Dt"], out=Dt, in0=src[:, 16:16 + S], scalar=1.0 / w,
                            in1=U[:, 16:16 + S], op0=ALU.mult, op1=ALU.subtract)
                        DVE("tensor_tensor", [sn, "invc"], ["small"], out=small[:, 0:w - 1], in0=src[:, 16:16 + w - 1],
                            in1=invc[:, 0:w - 1], op=ALU.mult)
                        DVE("tensor_tensor", ["small", "U"], ["Dt"], out=Dt[:, 0:w - 1], in0=small[:, 0:w - 1],
                            in1=U[:, 16:16 + w - 1], op=ALU.subtract)
                        yi = yst_i["i"]; yst_i["i"] ^= 1
                        for tg in range(NTG):
                            bi = 4 + rot("m", 2)
                            bk = banks[bi]
                            MM(["PW", "Dt"], [BK[bi]], bk[:, :], lhsT=PW, rhs=Dt[:, tg * 512:(tg + 1) * 512], start=True, stop=True)
                            ACT([BK[bi], "cvt"], ["YST%d" % yi], out=YST[yi][:, tg * 512:(tg + 1) * 512], in_=bk[:, :],
                                func=AF.Identity, scale=cvl[:, 108 + g:109 + g])
                        DMA("sp", dsem_o[yi], ["YST%d" % yi], ["YD"], out=YD[g * 128:(g + 1) * 128, :], in_=YST[yi])
                    P.barrier()

                    CV.reset(tmp_off)
                    T1 = CV.take([S], F32)
                    T2 = CV.take([S], F32)
                    T3 = CV.take([S], F32)
                    T4 = CV.take([S], F32)
                    QD = CV.take([S], BF16)
                    KD = CV.take([S], BF16)
                    KLC = [CV.take([CH], BF16) for _ in range(2)]
                    V64 = CV.take([NCK, 128], BF16)
                    SMK = CV.take([S], BF16)
                    YST = [CV.take([S], BF16) for _ in range(2)]
                    ATM = [CV.take([64], BF16) for _ in range(2)]
                    KLT = [CV.take([128], BF16) for _ in range(2)]
                    SBF = CV.take([128], BF16)
                    SQB = CV.take([512], BF16)
                    RS = CV.take([512], F32)
                    POOL("memset", [], ["SMK"], SMK, 1.0)
                    POOL("memset", ["SMK"], ["SMK"], SMK.rearrange("p (c k) -> p c k", k=CH)[:, :, 0:1], 0.0)
                    for h in (range(4) if DBG >= 2 else []):
                        lb_ap = LBT[:, l, h:h + 1]
                        oml_ap = OMLT[:, l, h:h + 1]

                        def cons_z(tg, ps, bkk, T2=T2):
                            sl = slice(tg * 512, (tg + 1) * 512)
                            DVE("tensor_scalar", [bkk], ["T2"], out=T2[:, sl], in0=ps, scalar1=-1.0, scalar2=80.0, op0=ALU.mult, op1=ALU.min)
                        inproj_fm(1024 + h * 128, 128, cons_z)
                        ACT(["T2"], ["T1"], out=T1, in_=T2, func=AF.Exp)
                        ACT(["T1", "lbt"], ["T2"], out=T2, in_=T1, func=AF.Ln, scale=lb_ap, bias=1.0)
                        ACT(["T1"], ["T3"], out=T3, in_=T1, func=AF.Ln, bias=1.0)
                        DVE("tensor_tensor", ["T2", "T3"], ["T2"], out=T2, in0=T2, in1=T3, op=ALU.subtract)
                        DVE("tensor_scalar", ["T2"], ["T2"], out=T2, in0=T2, scalar1=0.0, scalar2=None, op0=ALU.min)
                        DVE("tensor_scalar", ["T1"], ["T3"], out=T3, in0=T1, scalar1=1.0, scalar2=None, op0=ALU.add)
                        DVE("reciprocal", ["T3"], ["T3"], out=T3, in_=T3)
                        DVE("scalar_tensor_tensor", ["T1", "T3", "omlt"], ["T1"], out=T1, in0=T1, scalar=oml_ap, in1=T3, op0=ALU.mult, op1=ALU.mult)
                        DVE("tensor_tensor_scan", ["SMK", "T2"], ["T3"], out=T3, data0=SMK, data1=T2, initial=0.0, op0=ALU.mult, op1=ALU.add)
                        ACT(["T3"], ["T4"], out=T4, in_=T3, func=AF.Exp)
                        ACT(["T3"], ["T2"], out=T2, in_=T3, func=AF.Exp, scale=-1.0)

                        def cons_q(tg, ps, bkk, QD=QD, T4=T4):
                            sl = slice(tg * 512, (tg + 1) * 512)
                            DVE("tensor_tensor", [bkk, "T4"], ["QD"], out=QD[:, sl], in0=ps, in1=T4[:, sl], op=ALU.mult)
                        inproj_fm(512 + h * 128, 128, cons_q)
                        DVE("tensor_tensor", ["T1", "T2"], ["KD"], out=KD, in0=T1, in1=T2, op=ALU.mult)
                        wi = WP1.next(1536 + h * 128)
                        for c in range(NCK):
                            bi = rot("bk", 4)
                            bk = banks[bi]
                            for kc in range(KC):
                                MM([WK[wi][0], "XT"], [BK[bi]], bk[0:64, 0:128], lhsT=XT[:, kc, c * 64:(c + 1) * 64], rhs=wsl[wi][:, kc, 0:128],
                                   start=(kc == 0), stop=(kc == KC - 1))
                            ACT([BK[bi]], ["V64"], out=V64[0:64, c, :], in_=bk[0:64, 0:128], func=AF.Identity)
                        DVE("memset", [], ["SST"], SST, 0.0)
                        DVE("memset", [], ["SBF"], SBF, 0.0)
                        ptb = banks[6][:, :].bitcast(BF16)
                        ob = banks[5]
                        def hA(c, KD=KD, QD=QD, ATM=ATM, KLC=KLC, KLT=KLT, T4=T4, ptb=ptb):
                            cs = slice(c * CH, (c + 1) * CH)
                            ai = c % 2
                            MM(["KD", "QD"], [BK[4]], banks[4][0:64, 0:64], lhsT=KD[:, cs], rhs=QD[:, cs], start=True, stop=True)
                            DVE("tensor_tensor", [BK[4], "triu"], ["ATM%d" % ai], out=ATM[ai][0:64, :], in0=banks[4][0:64, 0:64], in1=triu, op=ALU.mult)
                            DVE("tensor_scalar", ["KD", "T4"], ["KLC%d" % ai], out=KLC[ai], in0=KD[:, cs],
                                scalar1=T4[:, c * CH + CH - 1:c * CH + CH], scalar2=None, op0=ALU.mult)
                            TR(["KLC%d" % ai, "ident_b"], [BK[6]], out=ptb[0:64, 0:128], in_=KLC[ai], identity=ident_b)
                            ACT([BK[6]], ["KLT%d" % ai], out=KLT[ai][0:64, :], in_=ptb[0:64, 0:128], func=AF.Identity)

                        def hDS(c, KLT=KLT, V64=V64):
                            ai = c % 2
                            MM(["KLT%d" % ai, "V64"], [BK[7]], banks[7][:, 0:128], lhsT=KLT[ai][0:64, :], rhs=V64[0:64, c, :], start=True, stop=True)

                        def hO(c, V64=V64, ATM=ATM, SBF=SBF, QD=QD, ob=ob):
                            cs = slice(c * CH, (c + 1) * CH)
                            ai = c % 2
                            oc = slice((c % 8) * CH, (c % 8 + 1) * CH)
                            MM(["V64", "ATM%d" % ai], [BK[5]], ob[:, oc], lhsT=V64[0:64, c, :], rhs=ATM[ai][0:64, :], start=True, stop=False)
                            MM(["SBF", "QD"], [BK[5]], ob[:, oc], lhsT=SBF, rhs=QD[:, cs], start=False, stop=True)
                        hA(0)
                        hDS(0)
                        for c in range(NCK):
                            if c + 1 < NCK:
                                hA(c + 1)
                            hO(c)
                            DVE("scalar_tensor_tensor", ["SST", "T4", BK[7]], ["SST"], out=SST, in0=SST,
                                scalar=T4[:, c * CH + CH - 1:c * CH + CH], in1=banks[7][:, 0:128], op0=ALU.mult, op1=ALU.add)
                            ACT(["SST"], ["SBF"], out=SBF, in_=SST, func=AF.Identity)
                            if c + 1 < NCK:
                                hDS(c + 1)
                            if c % 8 == 7:
                                sl = slice((c // 8) * 512, (c // 8 + 1) * 512)
                                ACT([BK[5], "KD"], ["T1"], out=T1[:, sl], in_=ob[:, :], func=AF.Identity)
                        yi = yst_i["i"]; yst_i["i"] ^= 1

                        def cons_g(tg, ps, bkk, T2=T2):
                            sl = slice(tg * 512, (tg + 1) * 512)
                            ACT([bkk, "KD"], ["T2"], out=T2[:, sl], in_=ps, func=AF.Exp, scale=-1.0)
                            DVE("tensor_scalar", ["T2"], ["T2"], out=T2[:, sl], in0=T2[:, sl], scalar1=1.0, scalar2=None, op0=ALU.add)
                            DVE("reciprocal", ["T2"], ["T2"], out=T2[:, sl], in_=T2[:, sl])
                            DVE("tensor_tensor", [bkk, "T2"], ["T2"], out=T2[:, sl], in0=ps, in1=T2[:, sl], op=ALU.mult)
                        inproj_fm(2048 + h * 128, 128, cons_g)
                        for tg in range(NTG):
                            sl = slice(tg * 512, (tg + 1) * 512)
                            ACT(["T1"], ["SQB"], out=SQB, in_=T1[:, sl], func=AF.Square)
                            MM(["SQB", "cstb"], [BK[4]], banks[4][:, :], lhsT=ones_b, rhs=SQB, start=True, stop=True)
                            ACT([BK[4]], ["RS"], out=RS, in_=banks[4][:, :], func=AF.Ln, scale=1.0 / 128, bias=RMS_EPS)
                            ACT(["RS"], ["RS"], out=RS, in_=RS, func=AF.Exp, scale=-0.5)
                            DVE("tensor_tensor", ["T1", "RS"], ["T1"], out=T1[:, sl], in0=T1[:, sl], in1=RS, op=ALU.mult)
                            DVE("scalar_tensor_tensor", ["T1", "T2", "cvt"], ["YST%d" % yi], out=YST[yi][:, sl], in0=T1[:, sl],
                                scalar=cvl[:, 112 + h:113 + h], in1=T2[:, sl], op0=ALU.mult, op1=ALU.mult)
                        DMA("sp", dsem_o[yi], ["YST%d" % yi], ["YD"], out=YD[512 + h * 128:512 + (h + 1) * 128, :], in_=YST[yi])
                    P.barrier()

                    CV.reset(tmp_off)
                    W80 = CV.take([KC, 80], BF16)
                    W8 = CV.take([KC, 8], BF16)
                    CUMP = CV.take([S], F32)
                    R1 = CV.take([S], F32)
                    R2 = CV.take([S], F32)
                    MID = CV.take([S], BF16)
                    CST = CV.take([S], BF16)
                    LH = CV.take([S], BF16)
                    RH = CV.take([S], BF16)
                    QT = CV.take([S], BF16)
                    KT = CV.take([S], BF16)
                    VT = CV.take([NTB, 256], BF16)
                    PT = [CV.take([512], BF16) for _ in range(2)]
                    SM = [CV.take([128], F32) for _ in range(2)]
                    RC = CV.take([512], F32)
                    YST = [CV.take([S], BF16) for _ in range(2)]
                    ONESB = CV.take([S], BF16)
                    POOL("memset", [], ["ONESB"], ONESB, 1.0)
                    DVE("memset", [], ["W80"], W80, 0.0)
                    for off in (0, 8, 32, 40, 64, 72):
                        DMA("pool", dsem_m[1], ["W80"], ["W80d"], out=W80[:, :, off:off + 8], in_=winv[:, :, 5632:5640])
                    nfb_ap = NFBT[0:80, l:l + 1]

                    def cons_f(tg, ps, bkk, R1=R1, R2=R2, nfb_ap=nfb_ap):
                        sl = slice(tg * 512, (tg + 1) * 512)
                        ACT([bkk, "nfbt"], ["R1"], out=R1[0:80, sl], in_=ps, func=AF.Exp, scale=-1.0, bias=nfb_ap)
                        ACT(["R1"], ["R2"], out=R2[0:80, sl], in_=R1[0:80, sl], func=AF.Ln, bias=1.0)
                    inproj_fm(5632, 80, cons_f, wtile=(W80, ["W80", "W80d"]), nbk=2)
                    DVE("tensor_tensor_scan", ["R2", "ONESB"], ["CUMP"], out=CUMP[0:80, :],
                        data0=ONESB[0:80, :], data1=R2[0:80, :], initial=0.0, op0=ALU.mult, op1=ALU.add)
                    DVE("tensor_copy", ["CUMP"], ["CST"], out=CST[0:80, :], in_=CUMP[0:80, :])
                    DVE("tensor_tensor", ["CUMP", "CST"], ["R1"], out=R1[32:64, :], in0=CUMP[32:64, :], in1=CST[32:64, :], op=ALU.subtract)
                    DVE("tensor_tensor", ["CUMP", "CST"], ["R1"], out=R1[64:80, :], in0=CUMP[64:80, :], in1=CST[64:80, :], op=ALU.subtract)
                    DVE("tensor_copy", ["R1"], ["CST"], out=CST[32:48, :], in_=R1[32:48, :])
                    DVE("tensor_copy", ["R1"], ["MID"], out=MID[64:80, :], in_=R1[64:80, :])
                    DVE("tensor_tensor", ["R1", "MID"], ["R2"], out=R2[64:80, :], in0=R1[64:80, :], in1=MID[64:80, :], op=ALU.subtract)
                    DVE("tensor_copy", ["R2"], ["CST"], out=CST[64:80, :], in_=R2[64:80, :])
                    scale_q = 128.0 ** -0.5
                    for h in (range(8) if DBG >= 3 else []):
                        DVE("tensor_scalar", ["CST", "sel"], ["LH"], out=LH[0:80, :], in0=CST[0:80, :], scalar1=sel2[0:80, h:h + 1],
                            scalar2=sel1[0:80, h:h + 1], op0=ALU.mult, op1=ALU.add)
                        DVE("tensor_scalar", ["CST", "sel", "nsel1"], ["RH"], out=RH[0:80, :], in0=CST[0:80, :], scalar1=nsel1[0:80, h:h + 1],
                            scalar2=sel2[0:80, h:h + 1], op0=ALU.mult, op1=ALU.add)

                        def cons_qf(tg, ps, bkk, QT=QT):
                            ACT([bkk], ["QT"], out=QT[:, tg * 512:(tg + 1) * 512], in_=ps, func=AF.Identity, scale=scale_q)
                        inproj_fm(2560 + h * 128, 128, cons_qf, nbk=2)

                        def cons_kf(tg, ps, bkk, KT=KT):
                            ACT([bkk], ["KT"], out=KT[:, tg * 512:(tg + 1) * 512], in_=ps, func=AF.Identity)
                        inproj_fm(3584 + h * 128, 128, cons_kf, nbk=2)
                        if h % 2 == 0:
                            wi = WP1.next(4608 + h * 128)
                            for tb in range(NTB):
                                bi = rot("bk", 2)
                                bk = banks[bi]
                                for kc in range(KC):
                                    MM([WK[wi][0], WK[wi][1], "XT"], [BK[bi]], bk[:, 0:256], lhsT=XT[:, kc, tb * 128:(tb + 1) * 128],
                                       rhs=wsl[wi][:, kc, 0:256], start=(kc == 0), stop=(kc == KC - 1))
                                ACT([BK[bi]], ["VT"], out=VT[:, tb, :], in_=bk[:, 0:256], func=AF.Identity)
                        ho = (h % 2) * 128
                        yi = yst_i["i"]; yst_i["i"] ^= 1
                        ORP = [(6, 7), (2, 3)]

                        def stageA(G, j, KT=KT, QT=QT, LH=LH, RH=RH, SM=SM, PT=PT):
                            c0 = max(0, j - 4 * G)
                            cols = slice(c0 * 128, 512)
                            gcols = slice(G * 512 + c0 * 128, G * 512 + 512)
                            sbi = 4 + rot("m", 2)
                            sbk = banks[sbi]
                            pi = rot("t", 2)
                            MM(["KT", "QT"], [BK[sbi]], sbk[:, cols], lhsT=KT[:, j * 128:(j + 1) * 128], rhs=QT[:, gcols], start=True, stop=False)
                            MM(["LH", "RH"], [BK[sbi]], sbk[:, cols], lhsT=LH[0:80, j * 128:(j + 1) * 128], rhs=RH[0:80, gcols], start=False, stop=True)
                            if j >= 4 * G:
                                dsl = slice(c0 * 128, (c0 + 1) * 128)
                                DVE("tensor_tensor", [BK[sbi], "maskneg"], ["SM%d" % pi], out=SM[pi], in0=sbk[:, dsl], in1=maskneg, op=ALU.add)
                                ACT(["SM%d" % pi], ["PT%d" % pi], out=PT[pi][:, dsl], in_=SM[pi], func=AF.Exp)
                                if c0 < 3:
                                    rsl = slice((c0 + 1) * 128, 512)
                                    ACT([BK[sbi]], ["PT%d" % pi], out=PT[pi][:, rsl], in_=sbk[:, rsl], func=AF.Exp)
                            else:
                                ACT([BK[sbi]], ["PT%d" % pi], out=PT[pi], in_=sbk[:, :], func=AF.Exp)
                            return (G, j, cols, pi)

                        def stageB(G, j, cols, pi, VT=VT, PT=PT, RC=RC, YST=YST, ho=ho, yi=yi):
                            nkb = 4 * G + 4
                            obi, rbi = ORP[G % 2]
                            MM(["VT", "PT%d" % pi], [BK[obi]], banks[obi][:, cols], lhsT=VT[:, j, ho:ho + 128], rhs=PT[pi][:, cols],
                               start=(j == 0), stop=(j == nkb - 1))
                            MM(["cstb", "PT%d" % pi], [BK[rbi]], banks[rbi][:, cols], lhsT=ones_b, rhs=PT[pi][:, cols],
                               start=(j == 0), stop=(j == nkb - 1))
                            if j == nkb - 1:
                                DVE("reciprocal", [BK[rbi]], ["RC"], out=RC, in_=banks[rbi][:, :])
                                DVE("tensor_tensor", [BK[obi], "RC"], ["YST%d" % yi], out=YST[yi][:, G * 512:(G + 1) * 512], in0=banks[obi][:, :], in1=RC, op=ALU.mult)
                        prev = None
                        for G in range(NTG):
                            for j in range(4 * G + 4):
                                cur = stageA(G, j)
                                if prev is not None:
                                    stageB(*prev)
                                prev = cur
                        stageB(*prev)
                        DMA("sp", dsem_o[yi], ["YST%d" % yi], ["YD"], out=YD[1024 + h * 128:1024 + (h + 1) * 128, :], in_=YST[yi])
                    P.barrier()

                    def ln_block(PRE, pkey, tcols, gcol, bcol, tmp, XT=XT, cvl=cvl):
                        XB, SQ, MU, M2, RSTD = tmp
                        for nb in range(KC):
                            xi = nb % 2
                            ACT([pkey], ["XB%d" % xi], out=XB[xi], in_=PRE[:, nb, :], func=AF.Identity)
                            ACT([pkey], ["SQ%d" % xi], out=SQ[xi], in_=PRE[:, nb, :], func=AF.Square)
                            MM(["XB%d" % xi, "cstb"], [BK[6]], banks[6][:, :], lhsT=ones_b, rhs=XB[xi], start=(nb == 0), stop=(nb == KC - 1))
                            MM(["SQ%d" % xi, "cstb"], [BK[7]], banks[7][:, :], lhsT=ones_b, rhs=SQ[xi], start=(nb == 0), stop=(nb == KC - 1))
                        DVE("tensor_scalar", [BK[6]], ["MU"], out=MU, in0=banks[6][:, :], scalar1=1.0 / D, scalar2=None, op0=ALU.mult)
                        DVE("tensor_tensor", ["MU"], ["M2"], out=M2, in0=MU, in1=MU, op=ALU.mult)
                        DVE("scalar_tensor_tensor", [BK[7], "M2"], ["M2"], out=M2, in0=banks[7][:, :], scalar=1.0 / D, in1=M2, op0=ALU.mult, op1=ALU.subtract)
                        ACT(["M2"], ["RSTD"], out=RSTD, in_=M2, func=AF.Ln, bias=LN_EPS)
                        ACT(["RSTD"], ["RSTD"], out=RSTD, in_=RSTD, func=AF.Exp, scale=-0.5)
                        for nb in range(KC):
                            DVE("tensor_tensor", [pkey, "MU"], [pkey], out=PRE[:, nb, :], in0=PRE[:, nb, :], in1=MU, op=ALU.subtract)
                            DVE("tensor_tensor", [pkey, "RSTD"], [pkey], out=PRE[:, nb, :], in0=PRE[:, nb, :], in1=RSTD, op=ALU.mult)
                            ACT([pkey, "cvt"], [pkey], out=PRE[:, nb, :], in_=PRE[:, nb, :], func=AF.Identity,
                                scale=cvl[:, gcol + nb:gcol + nb + 1], bias=cvl[:, bcol + nb:bcol + nb + 1])
                            DVE("tensor_copy", [pkey], ["XT"], out=XT[:, nb, tcols], in_=PRE[:, nb, :])

                    CV.reset(off_after_xt)
                    wsl = [CV.take([KC, 256], BF16) for _ in range(NWS)]
                    YT = [CV.take([KC, 512], BF16) for _ in range(2)]
                    PREb = [CV.take([KC, 512], F32) for _ in range(2)]
                    lntmp = ([CV.take([512], BF16) for _ in range(2)], [CV.take([512], BF16) for _ in range(2)],
                             CV.take([512], F32), CV.take([512], F32), CV.take([512], F32))
                    woutv = w_out_d[l].rearrange("(c p) n -> p c n", p=128)

                    def mk_ld2(np2, wsl=wsl, woutv=woutv):
                        def fn(i):
                            DMA("pool", wsem[i][0], [], [WK[i][0], WK[i][1]], out=wsl[i][:, :, 0:256], in_=woutv[:, :, np2 * 256:(np2 + 1) * 256])
                        return (np2, fn)
                    WP2 = WPlan([mk_ld2(np2) for tg in (range(NTG) if DBG >= 4 else []) for np2 in range(KC // 2)])
                    for tg in (range(NTG) if DBG >= 4 else []):
                        tcols = slice(tg * 512, (tg + 1) * 512)
                        b = tg % 2
                        DMA("sp", dsem_x[b], ["YD"], ["YT%d" % b], out=YT[b], in_=YDv[:, :, tcols])
                        DMA("sp", dsem_m[2 + b], ["XR"], ["PRE%d" % b], out=PREb[b], in_=XRv[:, :, tcols])
                        for np2 in range(KC // 2):
                            wi = WP2.next(np2)
                            for hf in range(2):
                                nb = np2 * 2 + hf
                                bi = rot("bk", 4)
                                bk = banks[bi]
                                for kc in range(KC):
                                    MM([WK[wi][hf], "YT%d" % b], [BK[bi]], bk[:, :], lhsT=wsl[wi][:, kc, hf * 128:(hf + 1) * 128],
                                       rhs=YT[b][:, kc, :], start=(kc == 0), stop=(kc == KC - 1))
                                DVE("scalar_tensor_tensor", ["PRE%d" % b, BK[bi]], ["PRE%d" % b], out=PREb[b][:, nb, :], in0=PREb[b][:, nb, :],
                                    scalar=ALPHA, in1=bk[:, :], op0=ALU.mult, op1=ALU.add)
                        ln_block(PREb[b], "PRE%d" % b, tcols, 0, 16, lntmp)
                        DMA("sp", dsem_o[b], ["PRE%d" % b], ["XR"], out=XRv[:, :, tcols], in_=PREb[b])
                    P.barrier()

                    CV.reset(off_after_xt)
                    wsl = [CV.take([KC, 256], BF16) for _ in range(NWS)]
                    ACC = CV.take([KC, TH], F32)
                    AT = CV.take([GF, TH], BF16)
                    GS = [CV.take([2 + 512], F32) for _ in range(2)]
                    CC = [CV.take([512], F32) for _ in range(2)]
                    EE = [CV.take([512], F32) for _ in range(2)]
                    lntmp = ([CV.take([512], BF16) for _ in range(2)], [CV.take([512], BF16) for _ in range(2)],
                             CV.take([512], F32), CV.take([512], F32), CV.take([512], F32))
                    wgv = w_gate_d[l].rearrange("(c p) n -> p c n", p=128)
                    wvv = w_val_d[l].rearrange("(c p) n -> p c n", p=128)
                    wdv = w_down_d[l].rearrange("(f p) n -> p f n", p=128)
                    DVE("memset", [], ["HALO"], HALO, 0.0)
                    NG = NFB // GF

                    def mk_gv(f, wsl=wsl, wgv=wgv, wvv=wvv):
                        def fn(i):
                            DMA("pool", wsem[i][0], [], [WK[i][0]], out=wsl[i][:, :, 0:128], in_=wgv[:, :, f * 128:(f + 1) * 128])
                            DMA("pool", wsem[i][1], [], [WK[i][1]], out=wsl[i][:, :, 128:256], in_=wvv[:, :, f * 128:(f + 1) * 128])
                        return (("gv", f), fn)

                    def mk_wd(f0, np2, wsl=wsl, wdv=wdv):
                        def fn(i):
                            DMA("pool", wsem[i][0], [], [WK[i][0], WK[i][1]], out=wsl[i][:, 0:GF, :], in_=wdv[:, f0:f0 + GF, np2 * 256:(np2 + 1) * 256])
                        return (("wd", f0, np2), fn)
                    plan3 = []
                    for hfq in (range(NHALF) if DBG >= 5 else []):
                        for g in range(NG):
                            plan3 += [mk_gv(g * GF + fi) for fi in range(GF)]
                            plan3 += [mk_wd(g * GF, np2) for np2 in range(KC // 2)]
                    WP3 = WPlan(plan3)
                    for hfq in (range(NHALF) if DBG >= 5 else []):
                        t0 = hfq * TH
                        DMA("sp", dsem_m[2], ["XR"], ["ACC"], out=ACC, in_=XRv[:, :, t0:t0 + TH])
                        for nb in range(KC):
                            ACT(["ACC"], ["ACC"], out=ACC[:, nb, :], in_=ACC[:, nb, :], func=AF.Identity, scale=ALPHA)
                        for g in range(NG):
                            for fi in range(GF):
                                f = g * GF + fi
                                wi = WP3.next(("gv", f))
                                cw = [cvl[:, 120 + j * 44 + f:121 + j * 44 + f] for j in range(3)]
                                cb = cvl[:, 64 + f:65 + f]
                                for tg in range(NTGH):
                                    tsl = slice(t0 + tg * 512, t0 + (tg + 1) * 512)
                                    si = rot("t", 2)
                                    bgi = si * 2
                                    bvi = si * 2 + 1
                                    for (bi, hf) in ((bgi, 0), (bvi, 1)):
                                        for kc in range(KC):
                                            MM([WK[wi][hf], "XT"], [BK[bi]], banks[bi][:, :], lhsT=wsl[wi][:, kc, hf * 128:(hf + 1) * 128],
                                               rhs=XT[:, kc, tsl], start=(kc == 0), stop=(kc == KC - 1))
                                    gs_, cc_, ee_ = GS[si], CC[si], EE[si]
                                    gk, ck, ek = "GS%d" % si, "CC%d" % si, "EE%d" % si
                                    ACT(["HALO"], [gk], out=gs_[:, 0:2], in_=HALO[:, f, :], func=AF.Identity)
                                    ACT([BK[bgi]], [gk], out=gs_[:, 2:514], in_=banks[bgi][:, :], func=AF.Identity)
                                    ACT([gk], ["HALO"], out=HALO[:, f, :], in_=gs_[:, 512:514], func=AF.Identity)
                                    ACT([gk, "cvt"], [ck], out=cc_, in_=gs_[:, 2:514], func=AF.Identity, scale=cw[2], bias=cb)
                                    DVE("scalar_tensor_tensor", [gk, ck, "cvt"], [ck], out=cc_, in0=gs_[:, 1:513], scalar=cw[1], in1=cc_, op0=ALU.mult, op1=ALU.add)
                                    DVE("scalar_tensor_tensor", [gk, ck, "cvt"], [ck], out=cc_, in0=gs_[:, 0:512], scalar=cw[0], in1=cc_, op0=ALU.mult, op1=ALU.add)
                                    ACT([ck], [ek], out=ee_, in_=cc_, func=AF.Exp, scale=-1.0)
                                    DVE("tensor_scalar", [ek], [ek], out=ee_, in0=ee_, scalar1=1.0, scalar2=None, op0=ALU.add)
                                    DVE("reciprocal", [ek], [ek], out=ee_, in_=ee_)
                                    DVE("tensor_tensor", [ck, ek], [ck], out=cc_, in0=cc_, in1=ee_, op=ALU.mult)
                                    DVE("tensor_tensor", [ck, BK[bvi]], ["AT"], out=AT[:, fi, tg * 512:(tg + 1) * 512], in0=cc_, in1=banks[bvi][:, :], op=ALU.mult)
                            f0 = g * GF
                            for np2 in range(KC // 2):
                                wi = WP3.next(("wd", f0, np2))
                                for hf in range(2):
                                    nb = np2 * 2 + hf
                                    for tg in range(NTGH):
                                        bi = 4 + rot("m", 2)
                                        for fi in range(GF):
                                            MM([WK[wi][hf], "AT"], [BK[bi]], banks[bi][:, :], lhsT=wsl[wi][:, fi, hf * 128:(hf + 1) * 128],
                                               rhs=AT[:, fi, tg * 512:(tg + 1) * 512], start=(fi == 0), stop=(fi == GF - 1))
                                        DVE("tensor_tensor", ["ACC", BK[bi]], ["ACC"], out=ACC[:, nb, tg * 512:(tg + 1) * 512],
                                            in0=ACC[:, nb, tg * 512:(tg + 1) * 512], in1=banks[bi][:, :], op=ALU.add)
                        for tg in range(NTGH):
                            tcols = slice(t0 + tg * 512, t0 + (tg + 1) * 512)
                            ln_block(ACC[:, :, tg * 512:(tg + 1) * 512], "ACC", tcols, 32, 48, lntmp)
                        DMA("sp", dsem_o[0], ["ACC"], ["XR"], out=XRv[:, :, t0:t0 + TH], in_=ACC)
                    P.barrier()

                CV.reset(off_after_xt)
                xo_in = [CV.take([KC, 128], F32) for _ in range(2)]
                xo_st = [CV.take([D], F32) for _ in range(2)]
                for tb in range(NTB):
                    b = tb % 2
                    DMA("sp", dsem_x[b], ["XR"], ["xoin%d" % b], out=xo_in[b], in_=XRv[:, :, tb * 128:(tb + 1) * 128])
                    for q4 in range(4):
                        bi = rot("bk", 2)
                        bk = banks[bi]
                        for j in range(4):
                            dc = q4 * 4 + j
                            TR(["xoin%d" % b, "ident_f"], [BK[bi]], out=bk[:, j * 128:(j + 1) * 128], in_=xo_in[b][:, dc, :], identity=ident_f)
                        DVE("tensor_copy", [BK[bi]], ["xost%d" % b], out=xo_st[b][:, q4 * 512:(q4 + 1) * 512], in_=bk[:, :])
                    DMA("sp", dsem_o[b], ["xost%d" % b], ["OUT"], out=out_d[tok0 + tb * 128: tok0 + (tb + 1) * 128, :], in_=xo_st[b])
                P.barrier()
        except _Stop:
            pass
        P.barrier()
        P.emit()
    return nc


_NC_CACHE = {}

WNAMES = ["w_in", "fox_f_bias", "pool_w", "pool_scale", "hgrn_lb_logits", "hgrn_norm_g", "w_out", "ln1_g", "ln1_b",
          "w_gate", "w_val", "conv_w", "conv_b", "w_down", "ln2_g", "ln2_b"]


def kernel(**inputs):
    x = np.ascontiguousarray(inputs["x"], dtype=np.float32)
    B, S, Dm = x.shape
    L = inputs["w_in"].shape[0]
    ncores = 8
    nseq = B // ncores
    key = (S, nseq, L)
    if key not in _NC_CACHE:
        _NC_CACHE[key] = build(S, nseq, L)
    nc = _NC_CACHE[key]
    ws = {k: np.ascontiguousarray(inputs[k], dtype=np.float32) for k in WNAMES}
    in_maps = []
    for c in range(ncores):
        m = {"x": x[c * nseq:(c + 1) * nseq].reshape(nseq * S, Dm)}
        m.update(ws)
        in_maps.append(m)
    res = run_bass_kernel_spmd(nc, in_maps, core_ids=list(range(ncores)))
    outs = [np.asarray(r["out"]).reshape(nseq, S, Dm) for r in res.results]
    return np.concatenate(outs, axis=0).astype(np.float32)
```

```python
import contextlib
import os
import numpy as np
import concourse.bass as bass
import concourse.mybir as mybir
from concourse.bass_utils import run_bass_kernel_spmd

F32 = mybir.dt.float32
BF16 = mybir.dt.bfloat16
AF = mybir.ActivationFunctionType
ALU = mybir.AluOpType

SAME_ENGINE_SYNC = True

D = 2048
KC = 16
DFF = 5632
NFB = 44
INC = 5640
ALPHA = 8.0 ** 0.25
LN_EPS = 1e-5
RMS_EPS = 1e-6
CH = 64


class Prog:
    ENGS = ("pe", "act", "dve", "pool", "sp")

    def __init__(self, nc):
        self.nc = nc
        self.q = {e: [] for e in self.ENGS}
        self.last_write = {}
        self.reads_since = {}
        self.seen = {e: {} for e in self.ENGS}
        self.needed = {e: set() for e in self.ENGS}
        self.dma_cnt = {}
        self.dma_sems = []

    def dma_sem(self, name):
        self.dma_cnt[name] = 0
        self.dma_sems.append(name)
        return name

    def _deps(self, reads, writes):
        deps = []
        for k in reads:
            ev = self.last_write.get(k)
            if ev is not None:
                deps.append(ev)
        for k in writes:
            ev = self.last_write.get(k)
            if ev is not None:
                deps.append(ev)
            deps.extend(self.reads_since.get(k, ()))
        return deps

    def _commit(self, ev, reads, writes):
        for k in reads:
            self.reads_since.setdefault(k, []).append(ev)
        for k in writes:
            self.last_write[k] = ev
            self.reads_since[k] = []

    def _waits(self, eng, deps):
        waits = {}
        seen = self.seen[eng]
        for (sk, idx) in deps:
            if sk == eng and (eng == "pe" or not SAME_ENGINE_SYNC):
                continue
            if seen.get(sk, -1) >= idx:
                continue
            if waits.get(sk, -1) < idx:
                waits[sk] = idx
        for sk, idx in waits.items():
            seen[sk] = idx
            if sk in self.needed:
                self.needed[sk].add(idx)
        return waits

    def op(self, eng, fn, reads=(), writes=()):
        writes = list(writes) + [k for k in reads if k.startswith("pb") and k not in writes]
        deps = self._deps(reads, writes)
        waits = self._waits(eng, deps)
        idx = len(self.q[eng])
        self.q[eng].append(("op", fn, waits, None))
        ev = (eng, idx)
        self._commit(ev, reads, writes)
        return ev

    def dma(self, queue, sem, fn, reads=(), writes=()):
        deps = self._deps(reads, writes)
        waits = self._waits(queue, deps)
        self.dma_cnt[sem] += 16
        ev = (sem, self.dma_cnt[sem])
        self.q[queue].append(("dma", fn, waits, sem))
        self._commit(ev, reads, writes)
        return ev

    def barrier(self):
        deps = []
        for e in self.ENGS:
            for i in range(len(self.q[e]) - 1, -1, -1):
                if self.q[e][i][0] == "op":
                    deps.append((e, i))
                    break
        for s in self.dma_sems:
            if self.dma_cnt[s] > 0:
                deps.append((s, self.dma_cnt[s]))
        for e in self.ENGS:
            waits = self._waits(e, [d for d in deps if d[0] != e])
            self.q[e].append(("wait", None, waits, None))

    def emit(self):
        nc = self.nc
        with contextlib.ExitStack() as st:
            sems = {}
            for e in self.ENGS:
                sems[e] = st.enter_context(nc.semaphore("s_" + e))
            for d in self.dma_sems:
                sems[d] = st.enter_context(nc.semaphore("d_" + d))
            val = {}
            for e in self.ENGS:
                c = 0
                v = {}
                for i in range(len(self.q[e])):
                    if i in self.needed[e]:
                        c += 1
                        v[i] = c
                val[e] = v
            block = st.enter_context(nc.Block())

            def run(e, engh):
                needed = self.needed[e]
                for i, (kind, fn, waits, dsem) in enumerate(self.q[e]):
                    for sk, idx in waits.items():
                        if sk in val:
                            engh.wait_ge(sems[sk], val[sk][idx])
                        else:
                            engh.wait_ge(sems[sk], idx)
                    if kind == "wait":
                        continue
                    mname, margs, mkw = fn
                    bi = getattr(engh, mname)(*margs, **mkw)
                    if kind == "dma":
                        bi.then_inc(sems[dsem], 16)
                    elif i in needed:
                        bi.then_inc(sems[e], 1)

            @block.tensor
            def _(eh):
                run("pe", eh)

            @block.scalar
            def _(eh):
                run("act", eh)

            @block.vector
            def _(eh):
                run("dve", eh)

            @block.gpsimd
            def _(eh):
                run("pool", eh)

            @block.sync
            def _(eh):
                run("sp", eh)


class _Stop(Exception):
    pass


def build(S, NSEQ, L, GF=11):
    assert S % 512 == 0
    NTG = S // 512
    NTB = S // 128
    NCK = S // CH
    TH = min(1024, S)
    NHALF = S // TH
    NTGH = TH // 512
    nc = bass.Bass("TRN2", target_bir_lowering=False)
    P = Prog(nc)
    DBG = int(os.environ.get("KDBG", "9"))
    KSUB = int(os.environ.get("KSUB", "9"))

    def ACT(reads, writes, **kw):
        P.op("act", ("activation", (), kw), reads, writes)

    def MM(reads, writes, out, **kw):
        P.op("pe", ("matmul", (out,), kw), reads, writes)

    def TR(reads, writes, **kw):
        P.op("pe", ("transpose", (), kw), reads, writes)

    def DVE(m, reads, writes, *a, **kw):
        P.op("dve", (m, a, kw), reads, writes)

    def POOL(m, reads, writes, *a, **kw):
        P.op("pool", (m, a, kw), reads, writes)

    def DMA(queue, sem, reads, writes, **kw):
        P.dma(queue, sem, ("dma_start", (), kw), reads, writes)

    def din(name, shape):
        return nc.dram_tensor(name, shape, F32, kind="ExternalInput").ap()

    x_d = din("x", [NSEQ * S, D])
    w_in_d = din("w_in", [L, D, INC])
    fbias_d = din("fox_f_bias", [L, 8])
    pool_w_d = din("pool_w", [L, 4, 128, 128])
    pool_scale_d = din("pool_scale", [L, 512])
    lbl_d = din("hgrn_lb_logits", [L, 512])
    normg_d = din("hgrn_norm_g", [L, 512])
    w_out_d = din("w_out", [L, D, D])
    ln1g_d = din("ln1_g", [L, D])
    ln1b_d = din("ln1_b", [L, D])
    w_gate_d = din("w_gate", [L, D, DFF])
    w_val_d = din("w_val", [L, D, DFF])
    conv_w_d = din("conv_w", [L, 3, DFF])
    conv_b_d = din("conv_b", [L, DFF])
    w_down_d = din("w_down", [L, DFF, D])
    ln2g_d = din("ln2_g", [L, D])
    ln2b_d = din("ln2_b", [L, D])
    out_d = nc.dram_tensor("out", [NSEQ * S, D], F32, kind="ExternalOutput").ap()
    XR = nc.dram_tensor("xr_scr", [D, S], F32).ap()
    YD = nc.dram_tensor("y_scr", [D, S], BF16).ap()
    XRv = XR.rearrange("(c p) t -> p c t", p=128)
    YDv = YD.rearrange("(c p) t -> p c t", p=128)

    st = contextlib.ExitStack()
    with st:
        def sb(name, shape, dt):
            return st.enter_context(nc.sbuf_tensor(name, shape, dt))

        ARENA_B = 198 * 1024
        arena = sb("arena", [128, ARENA_B // 2], BF16)
        NCST = 128 * 4 + 16 + 8 * 4 + L * 8 + L + 2 + L * 256 + 2 * 128
        cst = sb("cst", [128, NCST], F32)
        cstb = sb("cstb", [128, 128 + 128 + 64], BF16)
        banks = [st.enter_context(nc.psum_tensor("pb%d" % i, [128, 512], F32)) for i in range(8)]
        BK = ["pb%d" % i for i in range(8)]

        class Carver:
            def __init__(self):
                self.off = 0

            def reset(self, off=0):
                self.off = off

            def take(self, free_shape, dt):
                n = int(np.prod(free_shape))
                esz = 4 if dt == F32 else 2
                nb = n * esz
                nbp = (nb + 63) // 64 * 64
                assert self.off + nbp <= ARENA_B, (self.off, nbp, ARENA_B)
                a = arena[:, self.off // 2: self.off // 2 + nb // 2]
                self.off += nbp
                if dt == F32:
                    a = a.bitcast(F32)
                if len(free_shape) == 2:
                    a = a.rearrange("p (a b) -> p a b", b=free_shape[1])
                elif len(free_shape) == 3:
                    a = a.rearrange("p (a b c) -> p a b c", b=free_shape[1], c=free_shape[2])
                return a

        CV = Carver()
        try:

            o = 0
            ident_f = cst[:, o:o + 128]; o += 128
            maskneg = cst[:, o:o + 128]; o += 128
            seltmp = cst[:, o:o + 128]; o += 128
            o += 128
            invc = cst[:, o:o + 16]; o += 16
            sel1 = cst[:, o:o + 8]; o += 8
            sel2 = cst[:, o:o + 8]; o += 8
            nsel1 = cst[:, o:o + 8]; o += 8
            o += 8
            LBT = cst[:, o:o + L * 4].rearrange("p (l h) -> p l h", h=4); o += L * 4
            OMLT = cst[:, o:o + L * 4].rearrange("p (l h) -> p l h", h=4); o += L * 4
            NFBT = cst[:, o:o + L]; o += L
            ones_col = cst[:, o:o + 1]; o += 1
            o += 1
            CVT = cst[:, o:o + L * 256].rearrange("p (l c) -> p l c", c=256); o += L * 256
            HALO = cst[:, o:o + 128].rearrange("p (f j) -> p f j", j=2)[:, 0:NFB, :]; o += 128
            SST = cst[:, o:o + 128]; o += 128
            assert o <= NCST
            ident_b = cstb[:, 0:128]
            ones_b = cstb[:, 128:256]
            triu = cstb[0:64, 256:320]

            POOL("memset", [], ["cst"], cst[:], 0.0)
            POOL("memset", ["cst"], ["ident_f"], ident_f, 1.0)
            POOL("affine_select", ["ident_f"], ["ident_f"], out=ident_f, in_=ident_f, pattern=[[1, 128]],
                 compare_op=ALU.is_equal, fill=0.0, base=0, channel_multiplier=-1)
            POOL("memset", [], ["cstb"], cstb[:], 1.0)
            POOL("affine_select", ["cstb"], ["ident_b"], out=ident_b, in_=ident_b, pattern=[[1, 128]],
                 compare_op=ALU.is_equal, fill=0.0, base=0, channel_multiplier=-1)
            POOL("affine_select", ["cstb"], ["triu"], out=triu, in_=triu, pattern=[[1, 64]],
                 compare_op=ALU.is_ge, fill=0.0, base=0, channel_multiplier=-1)
            POOL("affine_select", ["cst"], ["maskneg"], out=maskneg, in_=maskneg, pattern=[[1, 128]],
                 compare_op=ALU.is_ge, fill=-30000.0, base=0, channel_multiplier=-1)
            POOL("memset", ["cst"], ["ones_col"], ones_col, 1.0)
            for t in range(16):
                POOL("memset", ["cst"], ["invc"], invc[:, t:t + 1], 1.0 / (t + 1))
            for si, (dst, offs) in enumerate(((sel1, (0, 32, 64)), (sel2, (8, 40, 72)))):
                for oi, off in enumerate(offs):
                    tmp = seltmp[:, (si * 3 + oi) * 8:(si * 3 + oi) * 8 + 8]
                    POOL("memset", ["cst"], ["seltmp"], tmp, 1.0)
                    POOL("affine_select", ["seltmp"], ["seltmp"], out=tmp, in_=tmp, pattern=[[-1, 8]],
                         compare_op=ALU.is_equal, fill=0.0, base=-off, channel_multiplier=1)
                    POOL("tensor_tensor", ["seltmp", "cst"], ["sel"], out=dst, in0=dst, in1=tmp, op=ALU.add)
            POOL("tensor_scalar", ["sel"], ["nsel1"], out=nsel1, in0=sel1, scalar1=-1.0, scalar2=None, op0=ALU.mult)

            if DBG == -3:
                raise _Stop()
            CV.reset()
            stgf = CV.take([L * 256], F32)
            dsem_c = P.dma_sem("c")
            DVE("memset", [], ["stg"], stgf, 0.0)
            for l in range(L):
                s0 = stgf[:, (l * 2) * 128:(l * 2 + 1) * 128]
                s1 = stgf[:, (l * 2 + 1) * 128:(l * 2 + 2) * 128]
                rows = [(ln1g_d[l], 0, 16), (ln1b_d[l], 16, 16), (ln2g_d[l], 32, 16), (ln2b_d[l], 48, 16),
                        (conv_b_d[l], 64, 44), (pool_scale_d[l], 108, 4), (normg_d[l], 112, 4), (lbl_d[l], 116, 4)]
                for (src, r0, nr) in rows:
                    DMA("sp", dsem_c, ["stg"], ["stgd"], out=s0[r0:r0 + nr, :], in_=src.rearrange("(c p) -> c p", p=128))
                cwv = conv_w_d[l].rearrange("j (c p) -> (j c) p", p=128)
                DMA("sp", dsem_c, ["stg"], ["stgd"], out=s0[120:128, :], in_=cwv[0:8, :])
                DMA("sp", dsem_c, ["stg"], ["stgd"], out=s1[0:124, :], in_=cwv[8:132, :])
                for off in (0, 8, 32, 40, 64, 72):
                    DMA("sp", dsem_c, ["cst"], ["nfbt"], out=NFBT[off:off + 8, l:l + 1],
                        in_=fbias_d[l].rearrange("(h o) -> h o", o=1))
            P.barrier()
            for l in range(L):
                for hh in range(2):
                    bk = banks[hh]
                    TR(["stgd", "stg", "ident_f"], [BK[hh]], out=bk[:, 0:128], in_=stgf[:, (l * 2 + hh) * 128:(l * 2 + hh + 1) * 128], identity=ident_f)
                    DVE("tensor_copy", [BK[hh], "cst"], ["cvt"], out=CVT[:, l, hh * 128:(hh + 1) * 128], in_=bk[:, 0:128])
            DVE("tensor_scalar", ["nfbt"], ["nfbt"], out=NFBT, in0=NFBT, scalar1=-1.0, scalar2=None, op0=ALU.mult)
            if DBG == -2:
                raise _Stop()
            EL = CV.take([L, 4], F32)
            TOT = CV.take([4], F32)
            for l in range(L):
                ACT(["cvt"], ["el"], out=EL[:, l, :], in_=CVT[:, l, 116:120], func=AF.Exp)
            DVE("tensor_copy", ["el"], ["tot"], out=TOT, in_=EL[:, 0, :])
            for l in range(1, L):
                DVE("tensor_tensor", ["el", "tot"], ["tot"], out=TOT, in0=TOT, in1=EL[:, l, :], op=ALU.add)
            DVE("reciprocal", ["tot"], ["tot"], out=TOT, in_=TOT)
            for l in range(1, L):
                DVE("tensor_tensor", ["el", "tot"], ["el"], out=EL[:, l, :], in0=EL[:, l, :], in1=TOT, op=ALU.mult)
                DVE("tensor_tensor", ["el", "lbt", "cst"], ["lbt"], out=LBT[:, l, :], in0=LBT[:, l - 1, :], in1=EL[:, l, :], op=ALU.add)
            DVE("tensor_scalar", ["lbt", "cst"], ["omlt"], out=OMLT, in0=LBT, scalar1=-1.0, scalar2=1.0, op0=ALU.mult, op1=ALU.add)
            P.barrier()

            if DBG == -1:
                raise _Stop()
            NWS = 3
            wsem = [[P.dma_sem("w%d_%d" % (i, j)) for j in range(2)] for i in range(NWS)]
            WK = [["ws%d_%d" % (i, j) for j in range(2)] for i in range(NWS)]
            wstate = {"i": 0}

            def wslot_take():
                i = wstate["i"]
                wstate["i"] = (i + 1) % NWS
                return i

            class WPlan:
                def __init__(self, loaders):
                    self.loaders = loaders
                    self.issued = 0
                    self.consumed = 0
                    self.slots = {}

                def next(self, tag=None):
                    while self.issued < len(self.loaders) and self.issued < self.consumed + NWS:
                        i = wslot_take()
                        t, fn = self.loaders[self.issued]
                        fn(i)
                        self.slots[self.issued] = (i, t)
                        self.issued += 1
                    i, t = self.slots.pop(self.consumed)
                    assert tag is None or t == tag, (t, tag)
                    self.consumed += 1
                    return i

            dsem_x = [P.dma_sem("x0"), P.dma_sem("x1")]
            dsem_o = [P.dma_sem("o0"), P.dma_sem("o1")]
            dsem_m = [P.dma_sem("m%d" % i) for i in range(4)]
            cnt = {"bk": 0, "m": 0, "t": 0}

            def rot(name, n):
                v = cnt[name]
                cnt[name] = (v + 1) % n
                return v % n

            for sq in range(NSEQ):
                tok0 = sq * S
                CV.reset()
                XT = CV.take([KC, S], BF16)
                off_after_xt = CV.off
                xin = [CV.take([D], F32) for _ in range(2)]
                xstf = [CV.take([KC * 128], F32) for _ in range(2)]
                xst = [a.rearrange("p (a b) -> p a b", b=128) for a in xstf]
                for tb in range(NTB):
                    b = tb % 2
                    DMA("sp", dsem_x[b], [], ["xin%d" % b], out=xin[b], in_=x_d[tok0 + tb * 128: tok0 + (tb + 1) * 128, :])
                    for q4 in (range(4) if KSUB >= 0 else []):
                        bi = rot("bk", 2)
                        bk = banks[bi]
                        for j in range(4):
                            dc = q4 * 4 + j
                            TR(["xin%d" % b, "ident_f"], [BK[bi]], out=bk[:, j * 128:(j + 1) * 128],
                               in_=xin[b][:, dc * 128:(dc + 1) * 128], identity=ident_f)
                        if KSUB >= 1:
                            for j in range(4):
                                ACT([BK[bi]], ["XT"], out=XT[:, q4 * 4 + j, tb * 128:(tb + 1) * 128], in_=bk[:, j * 128:(j + 1) * 128], func=AF.Identity)
                        if KSUB >= 2:
                            DVE("tensor_copy", [BK[bi]], ["xst%d" % b], out=xstf[b][:, q4 * 512:(q4 + 1) * 512], in_=bk[:, :])
                    if KSUB >= 3:
                        DMA("sp", dsem_o[b], ["xst%d" % b], ["XR"], out=XRv[:, :, tb * 128:(tb + 1) * 128], in_=xst[b])
                P.barrier()
                if KSUB < 4:
                    raise _Stop()

                for l in range(L):
                    cvl = CVT[:, l, :]
                    CV.reset(off_after_xt)
                    wsl = [CV.take([KC, 256], BF16) for _ in range(NWS)]

                    winv = w_in_d[l].rearrange("(c p) n -> p c n", p=128)

                    def mk_ld(col0, ncols, wsl=wsl, winv=winv):
                        def fn(i):
                            DMA("pool", wsem[i][0], [], [WK[i][0]] if ncols <= 128 else [WK[i][0], WK[i][1]],
                                out=wsl[i][:, :, 0:ncols], in_=winv[:, :, col0:col0 + ncols])
                        return (col0, fn)
                    plan1 = []
                    if DBG >= 1:
                        plan1 += [mk_ld(g * 128, 128) for g in range(4)]
                    if DBG >= 2:
                        for h in range(4):
                            plan1 += [mk_ld(1024 + h * 128, 128), mk_ld(512 + h * 128, 128), mk_ld(1536 + h * 128, 128), mk_ld(2048 + h * 128, 128)]
                    if DBG >= 3:
                        for h in range(8):
                            plan1 += [mk_ld(2560 + h * 128, 128), mk_ld(3584 + h * 128, 128)]
                            if h % 2 == 0:
                                plan1.append(mk_ld(4608 + h * 128, 256))
                    WP1 = WPlan(plan1)

                    def inproj_fm(col0, m, consume, wtile=None, nbk=4, wsl=wsl, XT=XT, WP1=WP1):
                        if wtile is None:
                            i = WP1.next(col0)
                            wt, wk = wsl[i][:, :, 0:m], [WK[i][0]]
                        else:
                            wt, wk = wtile
                        for tg in range(NTG):
                            bi = rot("bk", nbk)
                            bk = banks[bi]
                            for kc in range(KC):
                                MM(wk + ["XT"], [BK[bi]], bk[0:m, :], lhsT=wt[:, kc, :], rhs=XT[:, kc, tg * 512:(tg + 1) * 512],
                                   start=(kc == 0), stop=(kc == KC - 1))
                            consume(tg, bk[0:m, :], BK[bi])

                    tmp_off = CV.off
                    yst_i = {"i": 0}

                    CV.reset(tmp_off)
                    U = CV.take([16 + S], F32)
                    A = CV.take([16 + S], F32)
                    B = CV.take([16 + S], F32)
                    Dt = CV.take([S], BF16)
                    PW = CV.take([128], BF16)
                    YST = [CV.take([S], BF16) for _ in range(2)]
                    small = CV.take([32], F32)
                    for buf, nm in ((U, "U"), (A, "A"), (B, "B")):
                        POOL("memset", [], [nm], buf[:, 0:16], 0.0)
                    for g in (range(4) if DBG >= 1 else []):
                        w = 2 ** (g + 1)
                        DMA("pool", dsem_m[0], [], ["PW"], out=PW, in_=pool_w_d[l, g])

                        def cons_u(tg, ps, bkk, U=U):
                            ACT([bkk], ["U"], out=U[:, 16 + tg * 512:16 + (tg + 1) * 512], in_=ps, func=AF.Identity)
                        inproj_fm(g * 128, 128, cons_u)
                        src, sn = U, "U"
                        pp = [(A, "A"), (B, "B")]
                        for stp in range(g + 1):
                            sh = 2 ** stp
                            dst, dn = pp[stp % 2]
                            DVE("tensor_tensor", [sn], [dn], out=dst[:, 16:16 + S], in0=src[:, 16:16 + S],
                                in1=src[:, 16 - sh:16 - sh + S], op=ALU.add)
                            src, sn = dst, dn
                        DVE("scalar_tensor_tensor", [sn, "U"], ["Dt"], out=Dt, in0=src[:, 16:16 + S], scalar=1.0 / w,
                            in1=U[:, 16:16 + S], op0=ALU.mult, op1=ALU.subtract)
                        DVE("tensor_tensor", [sn, "invc"], ["small"], out=small[:, 0:w - 1], in0=src[:, 16:16 + w - 1],
                            in1=invc[:, 0:w - 1], op=ALU.mult)
                        DVE("tensor_tensor", ["small", "U"], ["Dt"], out=Dt[:, 0:w - 1], in0=small[:, 0:w - 1],
                            in1=U[:, 16:16 + w - 1], op=ALU.subtract)
                        yi = yst_i["i"]; yst_i["i"] ^= 1
                        for tg in range(NTG):
                            bi = 4 + rot("m", 2)
                            bk = banks[bi]
                            MM(["PW", "Dt"], [BK[bi]], bk[:, :], lhsT=PW, rhs=Dt[:, tg * 512:(tg + 1) * 512], start=True, stop=True)
                            ACT([BK[bi], "cvt"], ["YST%d" % yi], out=YST[yi][:, tg * 512:(tg + 1) * 512], in_=bk[:, :],
                                func=AF.Identity, scale=cvl[:, 108 + g:109 + g])
                        DMA("sp", dsem_o[yi], ["YST%d" % yi], ["YD"], out=YD[g * 128:(g + 1) * 128, :], in_=YST[yi])
                    P.barrier()

                    CV.reset(tmp_off)
                    T1 = CV.take([S], F32)
                    T2 = CV.take([S], F32)
                    T3 = CV.take([S], F32)
                    T4 = CV.take([S], F32)
                    QD = CV.take([S], BF16)
                    KD = CV.take([S], BF16)
                    KLC = [CV.take([CH], BF16) for _ in range(2)]
                    V64 = CV.take([NCK, 128], BF16)
                    SMK = CV.take([S], BF16)
                    YST = [CV.take([S], BF16) for _ in range(2)]
                    ATM = [CV.take([64], BF16) for _ in range(2)]
                    KLT = [CV.take([128], BF16) for _ in range(2)]
                    SBF = CV.take([128], BF16)
                    SQB = CV.take([512], BF16)
                    RS = CV.take([512], F32)
                    POOL("memset", [], ["SMK"], SMK, 1.0)
                    POOL("memset", ["SMK"], ["SMK"], SMK.rearrange("p (c k) -> p c k", k=CH)[:, :, 0:1], 0.0)
                    for h in (range(4) if DBG >= 2 else []):
                        lb_ap = LBT[:, l, h:h + 1]
                        oml_ap = OMLT[:, l, h:h + 1]

                        def cons_z(tg, ps, bkk, T2=T2):
                            sl = slice(tg * 512, (tg + 1) * 512)
                            DVE("tensor_scalar", [bkk], ["T2"], out=T2[:, sl], in0=ps, scalar1=-1.0, scalar2=80.0, op0=ALU.mult, op1=ALU.min)
                        inproj_fm(1024 + h * 128, 128, cons_z)
                        ACT(["T2"], ["T1"], out=T1, in_=T2, func=AF.Exp)
                        ACT(["T1", "lbt"], ["T2"], out=T2, in_=T1, func=AF.Ln, scale=lb_ap, bias=1.0)
                        ACT(["T1"], ["T3"], out=T3, in_=T1, func=AF.Ln, bias=1.0)
                        DVE("tensor_tensor", ["T2", "T3"], ["T2"], out=T2, in0=T2, in1=T3, op=ALU.subtract)
                        DVE("tensor_scalar", ["T2"], ["T2"], out=T2, in0=T2, scalar1=0.0, scalar2=None, op0=ALU.min)
                        DVE("tensor_scalar", ["T1"], ["T3"], out=T3, in0=T1, scalar1=1.0, scalar2=None, op0=ALU.add)
                        DVE("reciprocal", ["T3"], ["T3"], out=T3, in_=T3)
                        DVE("scalar_tensor_tensor", ["T1", "T3", "omlt"], ["T1"], out=T1, in0=T1, scalar=oml_ap, in1=T3, op0=ALU.mult, op1=ALU.mult)
                        DVE("tensor_tensor_scan", ["SMK", "T2"], ["T3"], out=T3, data0=SMK, data1=T2, initial=0.0, op0=ALU.mult, op1=ALU.add)
                        ACT(["T3"], ["T4"], out=T4, in_=T3, func=AF.Exp)
                        ACT(["T3"], ["T2"], out=T2, in_=T3, func=AF.Exp, scale=-1.0)

                        def cons_q(tg, ps, bkk, QD=QD, T4=T4):
                            sl = slice(tg * 512, (tg + 1) * 512)
                            DVE("tensor_tensor", [bkk, "T4"], ["QD"], out=QD[:, sl], in0=ps, in1=T4[:, sl], op=ALU.mult)
                        inproj_fm(512 + h * 128, 128, cons_q)
                        DVE("tensor_tensor", ["T1", "T2"], ["KD"], out=KD, in0=T1, in1=T2, op=ALU.mult)
                        wi = WP1.next(1536 + h * 128)
                        for c in range(NCK):
                            bi = rot("bk", 4)
                            bk = banks[bi]
                            for kc in range(KC):
                                MM([WK[wi][0], "XT"], [BK[bi]], bk[0:64, 0:128], lhsT=XT[:, kc, c * 64:(c + 1) * 64], rhs=wsl[wi][:, kc, 0:128],
                                   start=(kc == 0), stop=(kc == KC - 1))
                            ACT([BK[bi]], ["V64"], out=V64[0:64, c, :], in_=bk[0:64, 0:128], func=AF.Identity)
                        DVE("memset", [], ["SST"], SST, 0.0)
                        DVE("memset", [], ["SBF"], SBF, 0.0)
                        ptb = banks[6][:, :].bitcast(BF16)
                        ob = banks[5]
                        def hA(c, KD=KD, QD=QD, ATM=ATM, KLC=KLC, KLT=KLT, T4=T4, ptb=ptb):
                            cs = slice(c * CH, (c + 1) * CH)
                            ai = c % 2
                            MM(["KD", "QD"], [BK[4]], banks[4][0:64, 0:64], lhsT=KD[:, cs], rhs=QD[:, cs], start=True, stop=True)
                            DVE("tensor_tensor", [BK[4], "triu"], ["ATM%d" % ai], out=ATM[ai][0:64, :], in0=banks[4][0:64, 0:64], in1=triu, op=ALU.mult)
                            DVE("tensor_scalar", ["KD", "T4"], ["KLC%d" % ai], out=KLC[ai], in0=KD[:, cs],
                                scalar1=T4[:, c * CH + CH - 1:c * CH + CH], scalar2=None, op0=ALU.mult)
                            TR(["KLC%d" % ai, "ident_b"], [BK[6]], out=ptb[0:64, 0:128], in_=KLC[ai], identity=ident_b)
                            ACT([BK[6]], ["KLT%d" % ai], out=KLT[ai][0:64, :], in_=ptb[0:64, 0:128], func=AF.Identity)

                        def hDS(c, KLT=KLT, V64=V64):
                            ai = c % 2
                            MM(["KLT%d" % ai, "V64"], [BK[7]], banks[7][:, 0:128], lhsT=KLT[ai][0:64, :], rhs=V64[0:64, c, :], start=True, stop=True)

                        def hO(c, V64=V64, ATM=ATM, SBF=SBF, QD=QD, ob=ob):
                            cs = slice(c * CH, (c + 1) * CH)
                            ai = c % 2
                            oc = slice((c % 8) * CH, (c % 8 + 1) * CH)
                            MM(["V64", "ATM%d" % ai], [BK[5]], ob[:, oc], lhsT=V64[0:64, c, :], rhs=ATM[ai][0:64, :], start=True, stop=False)
                            MM(["SBF", "QD"], [BK[5]], ob[:, oc], lhsT=SBF, rhs=QD[:, cs], start=False, stop=True)
                        hA(0)
                        hDS(0)
                        for c in range(NCK):
                            if c + 1 < NCK:
                                hA(c + 1)
                            hO(c)
                            DVE("scalar_tensor_tensor", ["SST", "T4", BK[7]], ["SST"], out=SST, in0=SST,
                                scalar=T4[:, c * CH + CH - 1:c * CH + CH], in1=banks[7][:, 0:128], op0=ALU.mult, op1=ALU.add)
                            ACT(["SST"], ["SBF"], out=SBF, in_=SST, func=AF.Identity)
                            if c + 1 < NCK:
                                hDS(c + 1)
                            if c % 8 == 7:
                                sl = slice((c // 8) * 512, (c // 8 + 1) * 512)
                                ACT([BK[5], "KD"], ["T1"], out=T1[:, sl], in_=ob[:, :], func=AF.Identity)
                        yi = yst_i["i"]; yst_i["i"] ^= 1

                        def cons_g(tg, ps, bkk, T2=T2):
                            sl = slice(tg * 512, (tg + 1) * 512)
                            ACT([bkk, "KD"], ["T2"], out=T2[:, sl], in_=ps, func=AF.Exp, scale=-1.0)
                            DVE("tensor_scalar", ["T2"], ["T2"], out=T2[:, sl], in0=T2[:, sl], scalar1=1.0, scalar2=None, op0=ALU.add)
                            DVE("reciprocal", ["T2"], ["T2"], out=T2[:, sl], in_=T2[:, sl])
                            DVE("tensor_tensor", [bkk, "T2"], ["T2"], out=T2[:, sl], in0=ps, in1=T2[:, sl], op=ALU.mult)
                        inproj_fm(2048 + h * 128, 128, cons_g)
                        for tg in range(NTG):
                            sl = slice(tg * 512, (tg + 1) * 512)
                            ACT(["T1"], ["SQB"], out=SQB, in_=T1[:, sl], func=AF.Square)
                            MM(["SQB", "cstb"], [BK[4]], banks[4][:, :], lhsT=ones_b, rhs=SQB, start=True, stop=True)
                            ACT([BK[4]], ["RS"], out=RS, in_=banks[4][:, :], func=AF.Ln, scale=1.0 / 128, bias=RMS_EPS)
                            ACT(["RS"], ["RS"], out=RS, in_=RS, func=AF.Exp, scale=-0.5)
                            DVE("tensor_tensor", ["T1", "RS"], ["T1"], out=T1[:, sl], in0=T1[:, sl], in1=RS, op=ALU.mult)
                            DVE("scalar_tensor_tensor", ["T1", "T2", "cvt"], ["YST%d" % yi], out=YST[yi][:, sl], in0=T1[:, sl],
                                scalar=cvl[:, 112 + h:113 + h], in1=T2[:, sl], op0=ALU.mult, op1=ALU.mult)
                        DMA("sp", dsem_o[yi], ["YST%d" % yi], ["YD"], out=YD[512 + h * 128:512 + (h + 1) * 128, :], in_=YST[yi])
                    P.barrier()

                    CV.reset(tmp_off)
                    W80 = CV.take([KC, 80], BF16)
                    W8 = CV.take([KC, 8], BF16)
                    CUMP = CV.take([S], F32)
                    R1 = CV.take([S], F32)
                    R2 = CV.take([S], F32)
                    MID = CV.take([S], BF16)
                    CST = CV.take([S], BF16)
                    LH = CV.take([S], BF16)
                    RH = CV.take([S], BF16)
                    QT = CV.take([S], BF16)
                    KT = CV.take([S], BF16)
                    VT = CV.take([NTB, 256], BF16)
                    PT = [CV.take([512], BF16) for _ in range(2)]
                    SM = [CV.take([128], F32) for _ in range(2)]
                    RC = CV.take([512], F32)
                    YST = [CV.take([S], BF16) for _ in range(2)]
                    ONESB = CV.take([S], BF16)
                    POOL("memset", [], ["ONESB"], ONESB, 1.0)
                    DVE("memset", [], ["W80"], W80, 0.0)
                    for off in (0, 8, 32, 40, 64, 72):
                        DMA("pool", dsem_m[1], ["W80"], ["W80d"], out=W80[:, :, off:off + 8], in_=winv[:, :, 5632:5640])
                    nfb_ap = NFBT[0:80, l:l + 1]

                    def cons_f(tg, ps, bkk, R1=R1, R2=R2, nfb_ap=nfb_ap):
                        sl = slice(tg * 512, (tg + 1) * 512)
                        ACT([bkk, "nfbt"], ["R1"], out=R1[0:80, sl], in_=ps, func=AF.Exp, scale=-1.0, bias=nfb_ap)
                        ACT(["R1"], ["R2"], out=R2[0:80, sl], in_=R1[0:80, sl], func=AF.Ln, bias=1.0)
                    inproj_fm(5632, 80, cons_f, wtile=(W80, ["W80", "W80d"]), nbk=2)
                    DVE("tensor_tensor_scan", ["R2", "ONESB"], ["CUMP"], out=CUMP[0:80, :],
                        data0=ONESB[0:80, :], data1=R2[0:80, :], initial=0.0, op0=ALU.mult, op1=ALU.add)
                    DVE("tensor_copy", ["CUMP"], ["CST"], out=CST[0:80, :], in_=CUMP[0:80, :])
                    DVE("tensor_tensor", ["CUMP", "CST"], ["R1"], out=R1[32:64, :], in0=CUMP[32:64, :], in1=CST[32:64, :], op=ALU.subtract)
                    DVE("tensor_tensor", ["CUMP", "CST"], ["R1"], out=R1[64:80, :], in0=CUMP[64:80, :], in1=CST[64:80, :], op=ALU.subtract)
                    DVE("tensor_copy", ["R1"], ["CST"], out=CST[32:48, :], in_=R1[32:48, :])
                    DVE("tensor_copy", ["R1"], ["MID"], out=MID[64:80, :], in_=R1[64:80, :])
                    DVE("tensor_tensor", ["R1", "MID"], ["R2"], out=R2[64:80, :], in0=R1[64:80, :], in1=MID[64:80, :], op=ALU.subtract)
                    DVE("tensor_copy", ["R2"], ["CST"], out=CST[64:80, :], in_=R2[64:80, :])
                    scale_q = 128.0 ** -0.5
                    for h in (range(8) if DBG >= 3 else []):
                        DVE("tensor_scalar", ["CST", "sel"], ["LH"], out=LH[0:80, :], in0=CST[0:80, :], scalar1=sel2[0:80, h:h + 1],
                            scalar2=sel1[0:80, h:h + 1], op0=ALU.mult, op1=ALU.add)
                        DVE("tensor_scalar", ["CST", "sel", "nsel1"], ["RH"], out=RH[0:80, :], in0=CST[0:80, :], scalar1=nsel1[0:80, h:h + 1],
                            scalar2=sel2[0:80, h:h + 1], op0=ALU.mult, op1=ALU.add)

                        def cons_qf(tg, ps, bkk, QT=QT):
                            ACT([bkk], ["QT"], out=QT[:, tg * 512:(tg + 1) * 512], in_=ps, func=AF.Identity, scale=scale_q)
                        inproj_fm(2560 + h * 128, 128, cons_qf, nbk=2)

                        def cons_kf(tg, ps, bkk, KT=KT):
                            ACT([bkk], ["KT"], out=KT[:, tg * 512:(tg + 1) * 512], in_=ps, func=AF.Identity)
                        inproj_fm(3584 + h * 128, 128, cons_kf, nbk=2)
                        if h % 2 == 0:
                            wi = WP1.next(4608 + h * 128)
                            for tb in range(NTB):
                                bi = rot("bk", 2)
                                bk = banks[bi]
                                for kc in range(KC):
                                    MM([WK[wi][0], WK[wi][1], "XT"], [BK[bi]], bk[:, 0:256], lhsT=XT[:, kc, tb * 128:(tb + 1) * 128],
                                       rhs=wsl[wi][:, kc, 0:256], start=(kc == 0), stop=(kc == KC - 1))
                                ACT([BK[bi]], ["VT"], out=VT[:, tb, :], in_=bk[:, 0:256], func=AF.Identity)
                        ho = (h % 2) * 128
                        yi = yst_i["i"]; yst_i["i"] ^= 1
                        ORP = [(6, 7), (2, 3)]

                        def stageA(G, j, KT=KT, QT=QT, LH=LH, RH=RH, SM=SM, PT=PT):
                            c0 = max(0, j - 4 * G)
                            cols = slice(c0 * 128, 512)
                            gcols = slice(G * 512 + c0 * 128, G * 512 + 512)
                            sbi = 4 + rot("m", 2)
                            sbk = banks[sbi]
                            pi = rot("t", 2)
                            MM(["KT", "QT"], [BK[sbi]], sbk[:, cols], lhsT=KT[:, j * 128:(j + 1) * 128], rhs=QT[:, gcols], start=True, stop=False)
                            MM(["LH", "RH"], [BK[sbi]], sbk[:, cols], lhsT=LH[0:80, j * 128:(j + 1) * 128], rhs=RH[0:80, gcols], start=False, stop=True)
                            if j >= 4 * G:
                                dsl = slice(c0 * 128, (c0 + 1) * 128)
                                DVE("tensor_tensor", [BK[sbi], "maskneg"], ["SM%d" % pi], out=SM[pi], in0=sbk[:, dsl], in1=maskneg, op=ALU.add)
                                ACT(["SM%d" % pi], ["PT%d" % pi], out=PT[pi][:, dsl], in_=SM[pi], func=AF.Exp)
                                if c0 < 3:
                                    rsl = slice((c0 + 1) * 128, 512)
                                    ACT([BK[sbi]], ["PT%d" % pi], out=PT[pi][:, rsl], in_=sbk[:, rsl], func=AF.Exp)
                            else:
                                ACT([BK[sbi]], ["PT%d" % pi], out=PT[pi], in_=sbk[:, :], func=AF.Exp)
                            return (G, j, cols, pi)

                        def stageB(G, j, cols, pi, VT=VT, PT=PT, RC=RC, YST=YST, ho=ho, yi=yi):
                            nkb = 4 * G + 4
                            obi, rbi = ORP[G % 2]
                            MM(["VT", "PT%d" % pi], [BK[obi]], banks[obi][:, cols], lhsT=VT[:, j, ho:ho + 128], rhs=PT[pi][:, cols],
                               start=(j == 0), stop=(j == nkb - 1))
                            MM(["cstb", "PT%d" % pi], [BK[rbi]], banks[rbi][:, cols], lhsT=ones_b, rhs=PT[pi][:, cols],
                               start=(j == 0), stop=(j == nkb - 1))
                            if j == nkb - 1:
                                DVE("reciprocal", [BK[rbi]], ["RC"], out=RC, in_=banks[rbi][:, :])
                                DVE("tensor_tensor", [BK[obi], "RC"], ["YST%d" % yi], out=YST[yi][:, G * 512:(G + 1) * 512], in0=banks[obi][:, :], in1=RC, op=ALU.mult)
                        prev = None
                        for G in range(NTG):
                            for j in range(4 * G + 4):
                                cur = stageA(G, j)
                                if prev is not None:
                                    stageB(*prev)
                                prev = cur
                        stageB(*prev)
                        DMA("sp", dsem_o[yi], ["YST%d" % yi], ["YD"], out=YD[1024 + h * 128:1024 + (h + 1) * 128, :], in_=YST[yi])
                    P.barrier()

                    def ln_a_piece(PRE, kf, nb, tmp):
                        XB, SQ, MU, M2, RSTD = tmp
                        xi = nb % 2
                        ACT([kf(nb)], ["XB%d" % xi], out=XB[xi], in_=PRE[:, nb, :], func=AF.Identity)
                        ACT([kf(nb)], ["SQ%d" % xi], out=SQ[xi], in_=PRE[:, nb, :], func=AF.Square)
                        MM(["XB%d" % xi, "cstb"], [BK[6]], banks[6][:, :], lhsT=ones_b, rhs=XB[xi], start=(nb == 0), stop=(nb == KC - 1))
                        MM(["SQ%d" % xi, "cstb"], [BK[7]], banks[7][:, :], lhsT=ones_b, rhs=SQ[xi], start=(nb == 0), stop=(nb == KC - 1))

                    def ln_a_final(tmp):
                        XB, SQ, MU, M2, RSTD = tmp
                        DVE("tensor_scalar", [BK[6]], ["MU"], out=MU, in0=banks[6][:, :], scalar1=1.0 / D, scalar2=None, op0=ALU.mult)
                        DVE("tensor_tensor", ["MU"], ["M2"], out=M2, in0=MU, in1=MU, op=ALU.mult)
                        DVE("scalar_tensor_tensor", [BK[7], "M2"], ["M2"], out=M2, in0=banks[7][:, :], scalar=1.0 / D, in1=M2, op0=ALU.mult, op1=ALU.subtract)
                        ACT(["M2"], ["RSTD"], out=RSTD, in_=M2, func=AF.Ln, bias=LN_EPS)
                        ACT(["RSTD"], ["RSTD"], out=RSTD, in_=RSTD, func=AF.Exp, scale=-0.5)

                    def ln_b_piece(PRE, kf, nb, tcols, gcol, bcol, tmp, XT=XT, cvl=cvl):
                        XB, SQ, MU, M2, RSTD = tmp
                        k = kf(nb)
                        DVE("tensor_tensor", [k, "MU"], [k], out=PRE[:, nb, :], in0=PRE[:, nb, :], in1=MU, op=ALU.subtract)
                        DVE("tensor_tensor", [k, "RSTD"], [k], out=PRE[:, nb, :], in0=PRE[:, nb, :], in1=RSTD, op=ALU.mult)
                        ACT([k, "cvt"], [k], out=PRE[:, nb, :], in_=PRE[:, nb, :], func=AF.Identity,
                            scale=cvl[:, gcol + nb:gcol + nb + 1], bias=cvl[:, bcol + nb:bcol + nb + 1])
                        DVE("tensor_copy", [k], ["XT"], out=XT[:, nb, tcols], in_=PRE[:, nb, :])

                    CV.reset(off_after_xt)
                    wsl = [CV.take([KC, 256], BF16) for _ in range(NWS)]
                    YT = [CV.take([KC, 512], BF16) for _ in range(2)]
                    PREb = [CV.take([KC, 512], F32) for _ in range(2)]
                    lntmp = ([CV.take([512], BF16) for _ in range(2)], [CV.take([512], BF16) for _ in range(2)],
                             CV.take([512], F32), CV.take([512], F32), CV.take([512], F32))
                    woutv = w_out_d[l].rearrange("(c p) n -> p c n", p=128)

                    def mk_ld2(np2, wsl=wsl, woutv=woutv):
                        def fn(i):
                            DMA("pool", wsem[i][0], [], [WK[i][0], WK[i][1]], out=wsl[i][:, :, 0:256], in_=woutv[:, :, np2 * 256:(np2 + 1) * 256])
                        return (np2, fn)
                    WP2 = WPlan([mk_ld2(np2) for tg in (range(NTG) if DBG >= 4 else []) for np2 in range(KC // 2)])
                    def pk2(bb):
                        return lambda nb: "PRE%d_%d" % (bb, nb)
                    allk = lambda bb: ["PRE%d_%d" % (bb, nb) for nb in range(KC)]
                    prev_tg = None
                    for tg in (range(NTG) if DBG >= 4 else []):
                        tcols = slice(tg * 512, (tg + 1) * 512)
                        b = tg % 2
                        DMA("sp", dsem_x[b], ["YD"], ["YT%d" % b], out=YT[b], in_=YDv[:, :, tcols])
                        DMA("sp", dsem_m[2 + b], ["XR"], allk(b), out=PREb[b], in_=XRv[:, :, tcols])
                        for np2 in range(KC // 2):
                            wi = WP2.next(np2)
                            for hf in range(2):
                                nb = np2 * 2 + hf
                                bi = rot("bk", 4)
                                bk = banks[bi]
                                for kc in range(KC):
                                    MM([WK[wi][hf], "YT%d" % b], [BK[bi]], bk[:, :], lhsT=wsl[wi][:, kc, hf * 128:(hf + 1) * 128],
                                       rhs=YT[b][:, kc, :], start=(kc == 0), stop=(kc == KC - 1))
                                DVE("scalar_tensor_tensor", [pk2(b)(nb), BK[bi]], [pk2(b)(nb)], out=PREb[b][:, nb, :], in0=PREb[b][:, nb, :],
                                    scalar=ALPHA, in1=bk[:, :], op0=ALU.mult, op1=ALU.add)
                                if nb > 0:
                                    ln_a_piece(PREb[b], pk2(b), nb - 1, lntmp)
                                if prev_tg is not None:
                                    pb_, ptc = prev_tg
                                    ln_b_piece(PREb[pb_], pk2(pb_), nb, ptc, 0, 16, lntmp)
                        if prev_tg is not None:
                            pb_, ptc = prev_tg
                            DMA("sp", dsem_o[pb_], allk(pb_), ["XR"], out=XRv[:, :, ptc], in_=PREb[pb_])
                        ln_a_piece(PREb[b], pk2(b), KC - 1, lntmp)
                        ln_a_final(lntmp)
                        prev_tg = (b, tcols)
                    if prev_tg is not None:
                        pb_, ptc = prev_tg
                        for nb in range(KC):
                            ln_b_piece(PREb[pb_], pk2(pb_), nb, ptc, 0, 16, lntmp)
                        DMA("sp", dsem_o[pb_], allk(pb_), ["XR"], out=XRv[:, :, ptc], in_=PREb[pb_])
                    P.barrier()

                    CV.reset(off_after_xt)
                    wsl = [CV.take([KC, 256], BF16) for _ in range(NWS)]
                    ACC = CV.take([KC, TH], F32)
                    AT = CV.take([GF, TH], BF16)
                    GS = [CV.take([2 + 512], F32) for _ in range(2)]
                    CC = [CV.take([512], F32) for _ in range(2)]
                    EE = [CV.take([512], F32) for _ in range(2)]
                    lntmp = ([CV.take([512], BF16) for _ in range(2)], [CV.take([512], BF16) for _ in range(2)],
                             CV.take([512], F32), CV.take([512], F32), CV.take([512], F32))
                    wgv = w_gate_d[l].rearrange("(c p) n -> p c n", p=128)
                    wvv = w_val_d[l].rearrange("(c p) n -> p c n", p=128)
                    wdv = w_down_d[l].rearrange("(f p) n -> p f n", p=128)
                    DVE("memset", [], ["HALO"], HALO, 0.0)
                    NG = NFB // GF

                    def mk_gv(f, wsl=wsl, wgv=wgv, wvv=wvv):
                        def fn(i):
                            DMA("pool", wsem[i][0], [], [WK[i][0]], out=wsl[i][:, :, 0:128], in_=wgv[:, :, f * 128:(f + 1) * 128])
                            DMA("pool", wsem[i][1], [], [WK[i][1]], out=wsl[i][:, :, 128:256], in_=wvv[:, :, f * 128:(f + 1) * 128])
                        return (("gv", f), fn)

                    def mk_wd(f0, np2, wsl=wsl, wdv=wdv):
                        def fn(i):
                            DMA("pool", wsem[i][0], [], [WK[i][0], WK[i][1]], out=wsl[i][:, 0:GF, :], in_=wdv[:, f0:f0 + GF, np2 * 256:(np2 + 1) * 256])
                        return (("wd", f0, np2), fn)
                    plan3 = []
                    for hfq in (range(NHALF) if DBG >= 5 else []):
                        for g in range(NG):
                            plan3 += [mk_gv(g * GF + fi) for fi in range(GF)]
                            plan3 += [mk_wd(g * GF, np2) for np2 in range(KC // 2)]
                    WP3 = WPlan(plan3)
                    for hfq in (range(NHALF) if DBG >= 5 else []):
                        t0 = hfq * TH
                        ak = lambda nb, tg: "ACC_%d_%d" % (nb, tg)
                        akall = [ak(nb, tg) for nb in range(KC) for tg in range(NTGH)]
                        DMA("sp", dsem_m[2], ["XR"], akall, out=ACC, in_=XRv[:, :, t0:t0 + TH])
                        for nb in range(KC):
                            ACT([], [ak(nb, tg) for tg in range(NTGH)], out=ACC[:, nb, :], in_=ACC[:, nb, :], func=AF.Identity, scale=ALPHA)
                        for g in range(NG):
                            for fi in range(GF):
                                f = g * GF + fi
                                wi = WP3.next(("gv", f))
                                cw = [cvl[:, 120 + j * 44 + f:121 + j * 44 + f] for j in range(3)]
                                cb = cvl[:, 64 + f:65 + f]
                                for tg in range(NTGH):
                                    tsl = slice(t0 + tg * 512, t0 + (tg + 1) * 512)
                                    si = rot("t", 2)
                                    bgi = si * 2
                                    bvi = si * 2 + 1
                                    for (bi, hf) in ((bgi, 0), (bvi, 1)):
                                        for kc in range(KC):
                                            MM([WK[wi][hf], "XT"], [BK[bi]], banks[bi][:, :], lhsT=wsl[wi][:, kc, hf * 128:(hf + 1) * 128],
                                               rhs=XT[:, kc, tsl], start=(kc == 0), stop=(kc == KC - 1))
                                    gs_, cc_, ee_ = GS[si], CC[si], EE[si]
                                    gk, ck, ek = "GS%d" % si, "CC%d" % si, "EE%d" % si
                                    ACT(["HALO"], [gk], out=gs_[:, 0:2], in_=HALO[:, f, :], func=AF.Identity)
                                    ACT([BK[bgi]], [gk], out=gs_[:, 2:514], in_=banks[bgi][:, :], func=AF.Identity)
                                    ACT([gk], ["HALO"], out=HALO[:, f, :], in_=gs_[:, 512:514], func=AF.Identity)
                                    ACT([gk, "cvt"], [ck], out=cc_, in_=gs_[:, 2:514], func=AF.Identity, scale=cw[2], bias=cb)
                                    DVE("scalar_tensor_tensor", [gk, ck, "cvt"], [ck], out=cc_, in0=gs_[:, 1:513], scalar=cw[1], in1=cc_, op0=ALU.mult, op1=ALU.add)
                                    DVE("scalar_tensor_tensor", [gk, ck, "cvt"], [ck], out=cc_, in0=gs_[:, 0:512], scalar=cw[0], in1=cc_, op0=ALU.mult, op1=ALU.add)
                                    ACT([ck], [ek], out=ee_, in_=cc_, func=AF.Exp, scale=-1.0)
                                    DVE("tensor_scalar", [ek], [ek], out=ee_, in0=ee_, scalar1=1.0, scalar2=None, op0=ALU.add)
                                    DVE("reciprocal", [ek], [ek], out=ee_, in_=ee_)
                                    DVE("tensor_tensor", [ck, ek], [ck], out=cc_, in0=cc_, in1=ee_, op=ALU.mult)
                                    DVE("tensor_tensor", [ck, BK[bvi]], ["AT"], out=AT[:, fi, tg * 512:(tg + 1) * 512], in0=cc_, in1=banks[bvi][:, :], op=ALU.mult)
                            f0 = g * GF
                            for np2 in range(KC // 2):
                                wi = WP3.next(("wd", f0, np2))
                                for hf in range(2):
                                    nb = np2 * 2 + hf
                                    for tg in range(NTGH):
                                        bi = 4 + rot("m", 2)
                                        for fi in range(GF):
                                            MM([WK[wi][hf], "AT"], [BK[bi]], banks[bi][:, :], lhsT=wsl[wi][:, fi, hf * 128:(hf + 1) * 128],
                                               rhs=AT[:, fi, tg * 512:(tg + 1) * 512], start=(fi == 0), stop=(fi == GF - 1))
                                        DVE("tensor_tensor", [ak(nb, tg), BK[bi]], [ak(nb, tg)], out=ACC[:, nb, tg * 512:(tg + 1) * 512],
                                            in0=ACC[:, nb, tg * 512:(tg + 1) * 512], in1=banks[bi][:, :], op=ALU.add)
                        for tg in range(NTGH):
                            tcols = slice(t0 + tg * 512, t0 + (tg + 1) * 512)
                            PREv = ACC[:, :, tg * 512:(tg + 1) * 512]
                            kf = (lambda tg: (lambda nb: ak(nb, tg)))(tg)
                            for nb in range(KC):
                                ln_a_piece(PREv, kf, nb, lntmp)
                            ln_a_final(lntmp)
                            for nb in range(KC):
                                ln_b_piece(PREv, kf, nb, tcols, 32, 48, lntmp)
                        DMA("sp", dsem_o[0], akall, ["XR"], out=XRv[:, :, t0:t0 + TH], in_=ACC)
                    P.barrier()

                CV.reset(off_after_xt)
                xo_in = [CV.take([KC, 128], F32) for _ in range(2)]
                xo_st = [CV.take([D], F32) for _ in range(2)]
                for tb in range(NTB):
                    b = tb % 2
                    DMA("sp", dsem_x[b], ["XR"], ["xoin%d" % b], out=xo_in[b], in_=XRv[:, :, tb * 128:(tb + 1) * 128])
                    for q4 in range(4):
                        bi = rot("bk", 2)
                        bk = banks[bi]
                        for j in range(4):
                            dc = q4 * 4 + j
                            TR(["xoin%d" % b, "ident_f"], [BK[bi]], out=bk[:, j * 128:(j + 1) * 128], in_=xo_in[b][:, dc, :], identity=ident_f)
                        DVE("tensor_copy", [BK[bi]], ["xost%d" % b], out=xo_st[b][:, q4 * 512:(q4 + 1) * 512], in_=bk[:, :])
                    DMA("sp", dsem_o[b], ["xost%d" % b], ["OUT"], out=out_d[tok0 + tb * 128: tok0 + (tb + 1) * 128, :], in_=xo_st[b])
                P.barrier()
        except _Stop:
            pass
        P.barrier()
        P.emit()
    return nc


_NC_CACHE = {}

WNAMES = ["w_in", "fox_f_bias", "pool_w", "pool_scale", "hgrn_lb_logits", "hgrn_norm_g", "w_out", "ln1_g", "ln1_b",
          "w_gate", "w_val", "conv_w", "conv_b", "w_down", "ln2_g", "ln2_b"]


def kernel(**inputs):
    x = np.ascontiguousarray(inputs["x"], dtype=np.float32)
    B, S, Dm = x.shape
    L = inputs["w_in"].shape[0]
    ncores = 8
    nseq = B // ncores
    key = (S, nseq, L)
    if key not in _NC_CACHE:
        _NC_CACHE[key] = build(S, nseq, L)
    nc = _NC_CACHE[key]
    ws = {k: np.ascontiguousarray(inputs[k], dtype=np.float32) for k in WNAMES}
    in_maps = []
    for c in range(ncores):
        m = {"x": x[c * nseq:(c + 1) * nseq].reshape(nseq * S, Dm)}
        m.update(ws)
        in_maps.append(m)
    res = run_bass_kernel_spmd(nc, in_maps, core_ids=list(range(ncores)))
    outs = [np.asarray(r["out"]).reshape(nseq, S, Dm) for r in res.results]
    return np.concatenate(outs, axis=0).astype(np.float32)
```

```python
import contextlib
import os
import numpy as np
import concourse.bass as bass
import concourse.mybir as mybir
from concourse.bass_utils import run_bass_kernel_spmd

F32 = mybir.dt.float32
BF16 = mybir.dt.bfloat16
AF = mybir.ActivationFunctionType
ALU = mybir.AluOpType

SAME_ENGINE_SYNC = True

D = 2048
KC = 16
DFF = 5632
NFB = 44
INC = 5640
ALPHA = 8.0 ** 0.25
LN_EPS = 1e-5
RMS_EPS = 1e-6
CH = 64


class Prog:
    ENGS = ("pe", "act", "dve", "pool", "sp")

    def __init__(self, nc):
        self.nc = nc
        self.q = {e: [] for e in self.ENGS}
        self.last_write = {}
        self.reads_since = {}
        self.seen = {e: {} for e in self.ENGS}
        self.needed = {e: set() for e in self.ENGS}
        self.dma_cnt = {}
        self.dma_sems = []

    def dma_sem(self, name):
        self.dma_cnt[name] = 0
        self.dma_sems.append(name)
        return name

    def _deps(self, reads, writes):
        deps = []
        for k in reads:
            ev = self.last_write.get(k)
            if ev is not None:
                deps.append(ev)
        for k in writes:
            ev = self.last_write.get(k)
            if ev is not None:
                deps.append(ev)
            deps.extend(self.reads_since.get(k, ()))
        return deps

    def _commit(self, ev, reads, writes):
        for k in reads:
            self.reads_since.setdefault(k, []).append(ev)
        for k in writes:
            self.last_write[k] = ev
            self.reads_since[k] = []

    def _waits(self, eng, deps):
        waits = {}
        seen = self.seen[eng]
        for (sk, idx) in deps:
            if sk == eng and (eng == "pe" or not SAME_ENGINE_SYNC):
                continue
            if seen.get(sk, -1) >= idx:
                continue
            if waits.get(sk, -1) < idx:
                waits[sk] = idx
        for sk, idx in waits.items():
            seen[sk] = idx
            if sk in self.needed:
                self.needed[sk].add(idx)
        return waits

    def op(self, eng, fn, reads=(), writes=()):
        writes = list(writes) + [k for k in reads if k.startswith("pb") and k not in writes]
        deps = self._deps(reads, writes)
        waits = self._waits(eng, deps)
        idx = len(self.q[eng])
        self.q[eng].append(("op", fn, waits, None))
        ev = (eng, idx)
        self._commit(ev, reads, writes)
        return ev

    def dma(self, queue, sem, fn, reads=(), writes=()):
        deps = self._deps(reads, writes)
        waits = self._waits(queue, deps)
        self.dma_cnt[sem] += 16
        ev = (sem, self.dma_cnt[sem])
        self.q[queue].append(("dma", fn, waits, sem))
        self._commit(ev, reads, writes)
        return ev

    def barrier(self):
        deps = []
        for e in self.ENGS:
            for i in range(len(self.q[e]) - 1, -1, -1):
                if self.q[e][i][0] == "op":
                    deps.append((e, i))
                    break
        for s in self.dma_sems:
            if self.dma_cnt[s] > 0:
                deps.append((s, self.dma_cnt[s]))
        for e in self.ENGS:
            waits = self._waits(e, [d for d in deps if d[0] != e])
            self.q[e].append(("wait", None, waits, None))

    def emit(self):
        nc = self.nc
        with contextlib.ExitStack() as st:
            sems = {}
            for e in self.ENGS:
                sems[e] = st.enter_context(nc.semaphore("s_" + e))
            for d in self.dma_sems:
                sems[d] = st.enter_context(nc.semaphore("d_" + d))
            val = {}
            for e in self.ENGS:
                c = 0
                v = {}
                for i in range(len(self.q[e])):
                    if i in self.needed[e]:
                        c += 1
                        v[i] = c
                val[e] = v
            block = st.enter_context(nc.Block())

            def run(e, engh):
                needed = self.needed[e]
                for i, (kind, fn, waits, dsem) in enumerate(self.q[e]):
                    for sk, idx in waits.items():
                        if sk in val:
                            engh.wait_ge(sems[sk], val[sk][idx])
                        else:
                            engh.wait_ge(sems[sk], idx)
                    if kind == "wait":
                        continue
                    mname, margs, mkw = fn
                    bi = getattr(engh, mname)(*margs, **mkw)
                    if kind == "dma":
                        bi.then_inc(sems[dsem], 16)
                    elif i in needed:
                        bi.then_inc(sems[e], 1)

            @block.tensor
            def _(eh):
                run("pe", eh)

            @block.scalar
            def _(eh):
                run("act", eh)

            @block.vector
            def _(eh):
                run("dve", eh)

            @block.gpsimd
            def _(eh):
                run("pool", eh)

            @block.sync
            def _(eh):
                run("sp", eh)


class _Stop(Exception):
    pass


def build(S, NSEQ, L, GF=11):
    assert S % 512 == 0
    NTG = S // 512
    NTB = S // 128
    NCK = S // CH
    TH = min(1024, S)
    NHALF = S // TH
    NTGH = TH // 512
    nc = bass.Bass("TRN2", target_bir_lowering=False)
    P = Prog(nc)
    DBG = int(os.environ.get("KDBG", "9"))
    KSUB = int(os.environ.get("KSUB", "9"))

    def ACT(reads, writes, **kw):
        P.op("act", ("activation", (), kw), reads, writes)

    def MM(reads, writes, out, **kw):
        P.op("pe", ("matmul", (out,), kw), reads, writes)

    def TR(reads, writes, **kw):
        P.op("pe", ("transpose", (), kw), reads, writes)

    def DVE(m, reads, writes, *a, **kw):
        P.op("dve", (m, a, kw), reads, writes)

    def POOL(m, reads, writes, *a, **kw):
        P.op("pool", (m, a, kw), reads, writes)

    def DMA(queue, sem, reads, writes, **kw):
        P.dma(queue, sem, ("dma_start", (), kw), reads, writes)

    def din(name, shape):
        return nc.dram_tensor(name, shape, F32, kind="ExternalInput").ap()

    x_d = din("x", [NSEQ * S, D])
    w_in_d = din("w_in", [L, D, INC])
    fbias_d = din("fox_f_bias", [L, 8])
    pool_w_d = din("pool_w", [L, 4, 128, 128])
    pool_scale_d = din("pool_scale", [L, 512])
    lbl_d = din("hgrn_lb_logits", [L, 512])
    normg_d = din("hgrn_norm_g", [L, 512])
    w_out_d = din("w_out", [L, D, D])
    ln1g_d = din("ln1_g", [L, D])
    ln1b_d = din("ln1_b", [L, D])
    w_gate_d = din("w_gate", [L, D, DFF])
    w_val_d = din("w_val", [L, D, DFF])
    conv_w_d = din("conv_w", [L, 3, DFF])
    conv_b_d = din("conv_b", [L, DFF])
    w_down_d = din("w_down", [L, DFF, D])
    ln2g_d = din("ln2_g", [L, D])
    ln2b_d = din("ln2_b", [L, D])
    out_d = nc.dram_tensor("out", [NSEQ * S, D], F32, kind="ExternalOutput").ap()
    XR = nc.dram_tensor("xr_scr", [D, S], F32).ap()
    YD = nc.dram_tensor("y_scr", [D, S], BF16).ap()
    XRv = XR.rearrange("(c p) t -> p c t", p=128)
    YDv = YD.rearrange("(c p) t -> p c t", p=128)

    st = contextlib.ExitStack()
    with st:
        def sb(name, shape, dt):
            return st.enter_context(nc.sbuf_tensor(name, shape, dt))

        ARENA_B = 198 * 1024
        arena = sb("arena", [128, ARENA_B // 2], BF16)
        NCST = 128 * 4 + 16 + 8 * 4 + L * 8 + L + 2 + L * 256 + 2 * 128
        cst = sb("cst", [128, NCST], F32)
        cstb = sb("cstb", [128, 128 + 128 + 64], BF16)
        banks = [st.enter_context(nc.psum_tensor("pb%d" % i, [128, 512], F32)) for i in range(8)]
        BK = ["pb%d" % i for i in range(8)]

        class Carver:
            def __init__(self):
                self.off = 0

            def reset(self, off=0):
                self.off = off

            def take(self, free_shape, dt):
                n = int(np.prod(free_shape))
                esz = 4 if dt == F32 else 2
                nb = n * esz
                nbp = (nb + 63) // 64 * 64
                assert self.off + nbp <= ARENA_B, (self.off, nbp, ARENA_B)
                a = arena[:, self.off // 2: self.off // 2 + nb // 2]
                self.off += nbp
                if dt == F32:
                    a = a.bitcast(F32)
                if len(free_shape) == 2:
                    a = a.rearrange("p (a b) -> p a b", b=free_shape[1])
                elif len(free_shape) == 3:
                    a = a.rearrange("p (a b c) -> p a b c", b=free_shape[1], c=free_shape[2])
                return a

        CV = Carver()
        try:

            o = 0
            ident_f = cst[:, o:o + 128]; o += 128
            maskneg = cst[:, o:o + 128]; o += 128
            seltmp = cst[:, o:o + 128]; o += 128
            o += 128
            invc = cst[:, o:o + 16]; o += 16
            sel1 = cst[:, o:o + 8]; o += 8
            sel2 = cst[:, o:o + 8]; o += 8
            nsel1 = cst[:, o:o + 8]; o += 8
            o += 8
            LBT = cst[:, o:o + L * 4].rearrange("p (l h) -> p l h", h=4); o += L * 4
            OMLT = cst[:, o:o + L * 4].rearrange("p (l h) -> p l h", h=4); o += L * 4
            NFBT = cst[:, o:o + L]; o += L
            ones_col = cst[:, o:o + 1]; o += 1
            o += 1
            CVT = cst[:, o:o + L * 256].rearrange("p (l c) -> p l c", c=256); o += L * 256
            HALO = cst[:, o:o + 128].rearrange("p (f j) -> p f j", j=2)[:, 0:NFB, :]; o += 128
            SST = cst[:, o:o + 128]; o += 128
            assert o <= NCST
            ident_b = cstb[:, 0:128]
            ones_b = cstb[:, 128:256]
            triu = cstb[0:64, 256:320]

            POOL("memset", [], ["cst"], cst[:], 0.0)
            POOL("memset", ["cst"], ["ident_f"], ident_f, 1.0)
            POOL("affine_select", ["ident_f"], ["ident_f"], out=ident_f, in_=ident_f, pattern=[[1, 128]],
                 compare_op=ALU.is_equal, fill=0.0, base=0, channel_multiplier=-1)
            POOL("memset", [], ["cstb"], cstb[:], 1.0)
            POOL("affine_select", ["cstb"], ["ident_b"], out=ident_b, in_=ident_b, pattern=[[1, 128]],
                 compare_op=ALU.is_equal, fill=0.0, base=0, channel_multiplier=-1)
            POOL("affine_select", ["cstb"], ["triu"], out=triu, in_=triu, pattern=[[1, 64]],
                 compare_op=ALU.is_ge, fill=0.0, base=0, channel_multiplier=-1)
            POOL("affine_select", ["cst"], ["maskneg"], out=maskneg, in_=maskneg, pattern=[[1, 128]],
                 compare_op=ALU.is_ge, fill=-30000.0, base=0, channel_multiplier=-1)
            POOL("memset", ["cst"], ["ones_col"], ones_col, 1.0)
            for t in range(16):
                POOL("memset", ["cst"], ["invc"], invc[:, t:t + 1], 1.0 / (t + 1))
            for si, (dst, offs) in enumerate(((sel1, (0, 32, 64)), (sel2, (8, 40, 72)))):
                for oi, off in enumerate(offs):
                    tmp = seltmp[:, (si * 3 + oi) * 8:(si * 3 + oi) * 8 + 8]
                    POOL("memset", ["cst"], ["seltmp"], tmp, 1.0)
                    POOL("affine_select", ["seltmp"], ["seltmp"], out=tmp, in_=tmp, pattern=[[-1, 8]],
                         compare_op=ALU.is_equal, fill=0.0, base=-off, channel_multiplier=1)
                    POOL("tensor_tensor", ["seltmp", "cst"], ["sel"], out=dst, in0=dst, in1=tmp, op=ALU.add)
            POOL("tensor_scalar", ["sel"], ["nsel1"], out=nsel1, in0=sel1, scalar1=-1.0, scalar2=None, op0=ALU.mult)

            if DBG == -3:
                raise _Stop()
            CV.reset()
            stgf = CV.take([L * 256], F32)
            dsem_c = P.dma_sem("c")
            DVE("memset", [], ["stg"], stgf, 0.0)
            for l in range(L):
                s0 = stgf[:, (l * 2) * 128:(l * 2 + 1) * 128]
                s1 = stgf[:, (l * 2 + 1) * 128:(l * 2 + 2) * 128]
                rows = [(ln1g_d[l], 0, 16), (ln1b_d[l], 16, 16), (ln2g_d[l], 32, 16), (ln2b_d[l], 48, 16),
                        (conv_b_d[l], 64, 44), (pool_scale_d[l], 108, 4), (normg_d[l], 112, 4), (lbl_d[l], 116, 4)]
                for (src, r0, nr) in rows:
                    DMA("sp", dsem_c, ["stg"], ["stgd"], out=s0[r0:r0 + nr, :], in_=src.rearrange("(c p) -> c p", p=128))
                cwv = conv_w_d[l].rearrange("j (c p) -> (j c) p", p=128)
                DMA("sp", dsem_c, ["stg"], ["stgd"], out=s0[120:128, :], in_=cwv[0:8, :])
                DMA("sp", dsem_c, ["stg"], ["stgd"], out=s1[0:124, :], in_=cwv[8:132, :])
                for off in (0, 8, 32, 40, 64, 72):
                    DMA("sp", dsem_c, ["cst"], ["nfbt"], out=NFBT[off:off + 8, l:l + 1],
                        in_=fbias_d[l].rearrange("(h o) -> h o", o=1))
            P.barrier()
            for l in range(L):
                for hh in range(2):
                    bk = banks[hh]
                    TR(["stgd", "stg", "ident_f"], [BK[hh]], out=bk[:, 0:128], in_=stgf[:, (l * 2 + hh) * 128:(l * 2 + hh + 1) * 128], identity=ident_f)
                    DVE("tensor_copy", [BK[hh], "cst"], ["cvt"], out=CVT[:, l, hh * 128:(hh + 1) * 128], in_=bk[:, 0:128])
            DVE("tensor_scalar", ["nfbt"], ["nfbt"], out=NFBT, in0=NFBT, scalar1=-1.0, scalar2=None, op0=ALU.mult)
            if DBG == -2:
                raise _Stop()
            EL = CV.take([L, 4], F32)
            TOT = CV.take([4], F32)
            for l in range(L):
                ACT(["cvt"], ["el"], out=EL[:, l, :], in_=CVT[:, l, 116:120], func=AF.Exp)
            DVE("tensor_copy", ["el"], ["tot"], out=TOT, in_=EL[:, 0, :])
            for l in range(1, L):
                DVE("tensor_tensor", ["el", "tot"], ["tot"], out=TOT, in0=TOT, in1=EL[:, l, :], op=ALU.add)
            DVE("reciprocal", ["tot"], ["tot"], out=TOT, in_=TOT)
            for l in range(1, L):
                DVE("tensor_tensor", ["el", "tot"], ["el"], out=EL[:, l, :], in0=EL[:, l, :], in1=TOT, op=ALU.mult)
                DVE("tensor_tensor", ["el", "lbt", "cst"], ["lbt"], out=LBT[:, l, :], in0=LBT[:, l - 1, :], in1=EL[:, l, :], op=ALU.add)
            DVE("tensor_scalar", ["lbt", "cst"], ["omlt"], out=OMLT, in0=LBT, scalar1=-1.0, scalar2=1.0, op0=ALU.mult, op1=ALU.add)
            P.barrier()

            if DBG == -1:
                raise _Stop()
            NWS = 3
            wsem = [[P.dma_sem("w%d_%d" % (i, j)) for j in range(2)] for i in range(NWS)]
            WK = [["ws%d_%d" % (i, j) for j in range(2)] for i in range(NWS)]
            wstate = {"i": 0}

            def wslot_take():
                i = wstate["i"]
                wstate["i"] = (i + 1) % NWS
                return i

            class WPlan:
                def __init__(self, loaders):
                    self.loaders = loaders
                    self.issued = 0
                    self.consumed = 0
                    self.slots = {}

                def next(self, tag=None):
                    while self.issued < len(self.loaders) and self.issued < self.consumed + NWS:
                        i = wslot_take()
                        t, fn = self.loaders[self.issued]
                        fn(i)
                        self.slots[self.issued] = (i, t)
                        self.issued += 1
                    i, t = self.slots.pop(self.consumed)
                    assert tag is None or t == tag, (t, tag)
                    self.consumed += 1
                    return i

            dsem_x = [P.dma_sem("x0"), P.dma_sem("x1")]
            dsem_o = [P.dma_sem("o0"), P.dma_sem("o1")]
            dsem_m = [P.dma_sem("m%d" % i) for i in range(4)]
            cnt = {"bk": 0, "m": 0, "t": 0}

            def rot(name, n):
                v = cnt[name]
                cnt[name] = (v + 1) % n
                return v % n

            for sq in range(NSEQ):
                tok0 = sq * S
                CV.reset()
                XT = CV.take([KC, S], BF16)
                off_after_xt = CV.off
                xin = [CV.take([D], F32) for _ in range(2)]
                xstf = [CV.take([KC * 128], F32) for _ in range(2)]
                xst = [a.rearrange("p (a b) -> p a b", b=128) for a in xstf]
                for tb in range(NTB):
                    b = tb % 2
                    DMA("sp", dsem_x[b], [], ["xin%d" % b], out=xin[b], in_=x_d[tok0 + tb * 128: tok0 + (tb + 1) * 128, :])
                    for q4 in (range(4) if KSUB >= 0 else []):
                        bi = rot("bk", 2)
                        bk = banks[bi]
                        for j in range(4):
                            dc = q4 * 4 + j
                            TR(["xin%d" % b, "ident_f"], [BK[bi]], out=bk[:, j * 128:(j + 1) * 128],
                               in_=xin[b][:, dc * 128:(dc + 1) * 128], identity=ident_f)
                        if KSUB >= 1:
                            for j in range(4):
                                ACT([BK[bi]], ["XT"], out=XT[:, q4 * 4 + j, tb * 128:(tb + 1) * 128], in_=bk[:, j * 128:(j + 1) * 128], func=AF.Identity)
                        if KSUB >= 2:
                            DVE("tensor_copy", [BK[bi]], ["xst%d" % b], out=xstf[b][:, q4 * 512:(q4 + 1) * 512], in_=bk[:, :])
                    if KSUB >= 3:
                        DMA("sp", dsem_o[b], ["xst%d" % b], ["XR"], out=XRv[:, :, tb * 128:(tb + 1) * 128], in_=xst[b])
                P.barrier()
                if KSUB < 4:
                    raise _Stop()

                for l in range(L):
                    cvl = CVT[:, l, :]
                    CV.reset(off_after_xt)
                    wsl = [CV.take([KC, 256], BF16) for _ in range(NWS)]

                    winv = w_in_d[l].rearrange("(c p) n -> p c n", p=128)

                    def mk_ld(col0, ncols, wsl=wsl, winv=winv):
                        def fn(i):
                            DMA("pool", wsem[i][0], [], [WK[i][0]] if ncols <= 128 else [WK[i][0], WK[i][1]],
                                out=wsl[i][:, :, 0:ncols], in_=winv[:, :, col0:col0 + ncols])
                        return (col0, fn)
                    plan1 = []
                    if DBG >= 1:
                        plan1 += [mk_ld(g * 128, 128) for g in range(4)]
                    if DBG >= 2:
                        for hp in range(2):
                            for h in (2 * hp, 2 * hp + 1):
                                plan1 += [mk_ld(1024 + h * 128, 128), mk_ld(512 + h * 128, 128), mk_ld(1536 + h * 128, 128)]
                            for h in (2 * hp, 2 * hp + 1):
                                plan1.append(mk_ld(2048 + h * 128, 128))
                    if DBG >= 3:
                        for h in range(8):
                            plan1 += [mk_ld(2560 + h * 128, 128), mk_ld(3584 + h * 128, 128)]
                            if h % 2 == 0:
                                plan1.append(mk_ld(4608 + h * 128, 256))
                    WP1 = WPlan(plan1)

                    def inproj_fm(col0, m, consume, wtile=None, nbk=4, wsl=wsl, XT=XT, WP1=WP1):
                        if wtile is None:
                            i = WP1.next(col0)
                            wt, wk = wsl[i][:, :, 0:m], [WK[i][0]]
                        else:
                            wt, wk = wtile
                        for tg in range(NTG):
                            bi = rot("bk", nbk)
                            bk = banks[bi]
                            for kc in range(KC):
                                MM(wk + ["XT"], [BK[bi]], bk[0:m, :], lhsT=wt[:, kc, :], rhs=XT[:, kc, tg * 512:(tg + 1) * 512],
                                   start=(kc == 0), stop=(kc == KC - 1))
                            consume(tg, bk[0:m, :], BK[bi])

                    tmp_off = CV.off
                    yst_i = {"i": 0}

                    CV.reset(tmp_off)
                    U = CV.take([16 + S], F32)
                    A = CV.take([16 + S], F32)
                    B = CV.take([16 + S], F32)
                    Dt = CV.take([S], BF16)
                    PW = CV.take([128], BF16)
                    YST = [CV.take([S], BF16) for _ in range(2)]
                    small = CV.take([32], F32)
                    for buf, nm in ((U, "U"), (A, "A"), (B, "B")):
                        POOL("memset", [], [nm], buf[:, 0:16], 0.0)
                    for g in (range(4) if DBG >= 1 else []):
                        w = 2 ** (g + 1)
                        DMA("pool", dsem_m[0], [], ["PW"], out=PW, in_=pool_w_d[l, g])

                        def cons_u(tg, ps, bkk, U=U):
                            ACT([bkk], ["U"], out=U[:, 16 + tg * 512:16 + (tg + 1) * 512], in_=ps, func=AF.Identity)
                        inproj_fm(g * 128, 128, cons_u)
                        src, sn = U, "U"
                        pp = [(A, "A"), (B, "B")]
                        for stp in range(g + 1):
                            sh = 2 ** stp
                            dst, dn = pp[stp % 2]
                            DVE("tensor_tensor", [sn], [dn], out=dst[:, 16:16 + S], in0=src[:, 16:16 + S],
                                in1=src[:, 16 - sh:16 - sh + S], op=ALU.add)
                            src, sn = dst, dn
                        DVE("scalar_tensor_tensor", [sn, "U"], ["Dt"], out=Dt, in0=src[:, 16:16 + S], scalar=1.0 / w,
                            in1=U[:, 16:16 + S], op0=ALU.mult, op1=ALU.subtract)
                        DVE("tensor_tensor", [sn, "invc"], ["small"], out=small[:, 0:w - 1], in0=src[:, 16:16 + w - 1],
                            in1=invc[:, 0:w - 1], op=ALU.mult)
                        DVE("tensor_tensor", ["small", "U"], ["Dt"], out=Dt[:, 0:w - 1], in0=small[:, 0:w - 1],
                            in1=U[:, 16:16 + w - 1], op=ALU.subtract)
                        yi = yst_i["i"]; yst_i["i"] ^= 1
                        for tg in range(NTG):
                            bi = 4 + rot("m", 2)
                            bk = banks[bi]
                            MM(["PW", "Dt"], [BK[bi]], bk[:, :], lhsT=PW, rhs=Dt[:, tg * 512:(tg + 1) * 512], start=True, stop=True)
                            ACT([BK[bi], "cvt"], ["YST%d" % yi], out=YST[yi][:, tg * 512:(tg + 1) * 512], in_=bk[:, :],
                                func=AF.Identity, scale=cvl[:, 108 + g:109 + g])
                        DMA("sp", dsem_o[yi], ["YST%d" % yi], ["YD"], out=YD[g * 128:(g + 1) * 128, :], in_=YST[yi])
                    P.barrier()

                    CV.reset(tmp_off)
                    T1 = CV.take([S], F32)
                    T2 = CV.take([S], F32)
                    T3 = CV.take([S], F32)
                    T4 = CV.take([S], F32)
                    SMK = CV.take([S], BF16)
                    SQB = CV.take([512], BF16)
                    RS = CV.take([512], F32)
                    YST = [CV.take([S], BF16) for _ in range(2)]
                    HB = []
                    for hi in range(2):
                        HB.append(dict(
                            QD=CV.take([S], BF16), KD=CV.take([S], BF16), V64=CV.take([NCK, 128], BF16), OE=CV.take([S], F32),
                            EBL=CV.take([NCK], F32), SS=CV.take([128], F32), SBF=CV.take([128], BF16),
                            KLC=[CV.take([CH], BF16) for _ in range(2)], ATM=[CV.take([64], BF16) for _ in range(2)],
                            KLT=[CV.take([128], BF16) for _ in range(2)],
                            bk=(4, 5, 6, 7) if hi == 0 else (0, 1, 2, 3), n="h%d" % hi))
                    POOL("memset", [], ["SMK"], SMK, 1.0)
                    POOL("memset", ["SMK"], ["SMK"], SMK.rearrange("p (c k) -> p c k", k=CH)[:, :, 0:1], 0.0)

                    def h_prep(h, hb, T1=T1, T2=T2, T3=T3, T4=T4, SMK=SMK, l=l):
                        n = hb["n"]
                        QD, KD, V64, EBL = hb["QD"], hb["KD"], hb["V64"], hb["EBL"]
                        lb_ap = LBT[:, l, h:h + 1]
                        oml_ap = OMLT[:, l, h:h + 1]

                        def cons_z(tg, ps, bkk):
                            sl = slice(tg * 512, (tg + 1) * 512)
                            DVE("tensor_scalar", [bkk], ["T2"], out=T2[:, sl], in0=ps, scalar1=-1.0, scalar2=80.0, op0=ALU.mult, op1=ALU.min)
                        inproj_fm(1024 + h * 128, 128, cons_z)
                        ACT(["T2"], ["T1"], out=T1, in_=T2, func=AF.Exp)
                        ACT(["T1", "lbt"], ["T2"], out=T2, in_=T1, func=AF.Ln, scale=lb_ap, bias=1.0)
                        ACT(["T1"], ["T3"], out=T3, in_=T1, func=AF.Ln, bias=1.0)
                        DVE("tensor_tensor", ["T2", "T3"], ["T2"], out=T2, in0=T2, in1=T3, op=ALU.subtract)
                        DVE("tensor_scalar", ["T2"], ["T2"], out=T2, in0=T2, scalar1=0.0, scalar2=None, op0=ALU.min)
                        DVE("tensor_scalar", ["T1"], ["T3"], out=T3, in0=T1, scalar1=1.0, scalar2=None, op0=ALU.add)
                        DVE("reciprocal", ["T3"], ["T3"], out=T3, in_=T3)
                        DVE("scalar_tensor_tensor", ["T1", "T3", "omlt"], ["T1"], out=T1, in0=T1, scalar=oml_ap, in1=T3, op0=ALU.mult, op1=ALU.mult)
                        DVE("tensor_tensor_scan", ["SMK", "T2"], ["T3"], out=T3, data0=SMK, data1=T2, initial=0.0, op0=ALU.mult, op1=ALU.add)
                        ACT(["T3"], ["T4"], out=T4, in_=T3, func=AF.Exp)
                        ACT(["T3"], ["T2"], out=T2, in_=T3, func=AF.Exp, scale=-1.0)

                        def cons_q(tg, ps, bkk):
                            sl = slice(tg * 512, (tg + 1) * 512)
                            DVE("tensor_tensor", [bkk, "T4"], ["QD" + n], out=QD[:, sl], in0=ps, in1=T4[:, sl], op=ALU.mult)
                        inproj_fm(512 + h * 128, 128, cons_q)
                        DVE("tensor_tensor", ["T1", "T2"], ["KD" + n], out=KD, in0=T1, in1=T2, op=ALU.mult)
                        DVE("tensor_copy", ["T4"], ["EBL" + n], out=EBL, in_=T4.rearrange("p (c k) -> p c k", k=CH)[:, :, CH - 1])
                        wi = WP1.next(1536 + h * 128)
                        for c in range(NCK):
                            bi = rot("bk", 4)
                            bk = banks[bi]
                            for kc in range(KC):
                                MM([WK[wi][0], "XT"], [BK[bi]], bk[0:64, 0:128], lhsT=XT[:, kc, c * 64:(c + 1) * 64], rhs=wsl[wi][:, kc, 0:128],
                                   start=(kc == 0), stop=(kc == KC - 1))
                            ACT([BK[bi]], ["V64" + n], out=V64[0:64, c, :], in_=bk[0:64, 0:128], func=AF.Identity)
                        DVE("memset", [], ["SS" + n], hb["SS"], 0.0)
                        DVE("memset", [], ["SBF" + n], hb["SBF"], 0.0)

                    def hA(hb, c):
                        n = hb["n"]
                        KD, QD, ATM, KLC, KLT, EBL = hb["KD"], hb["QD"], hb["ATM"], hb["KLC"], hb["KLT"], hb["EBL"]
                        b_att, b_o, b_tr, b_ds = hb["bk"]
                        ptb = banks[b_tr][:, :].bitcast(BF16)
                        cs = slice(c * CH, (c + 1) * CH)
                        ai = c % 2
                        MM(["KD" + n, "QD" + n], [BK[b_att]], banks[b_att][0:64, 0:64], lhsT=KD[:, cs], rhs=QD[:, cs], start=True, stop=True)
                        DVE("tensor_tensor", [BK[b_att], "triu"], ["ATM%d%s" % (ai, n)], out=ATM[ai][0:64, :], in0=banks[b_att][0:64, 0:64], in1=triu, op=ALU.mult)
                        DVE("tensor_scalar", ["KD" + n, "EBL" + n], ["KLC%d%s" % (ai, n)], out=KLC[ai], in0=KD[:, cs],
                            scalar1=EBL[:, c:c + 1], scalar2=None, op0=ALU.mult)
                        TR(["KLC%d%s" % (ai, n), "ident_b"], [BK[b_tr]], out=ptb[0:64, 0:128], in_=KLC[ai], identity=ident_b)
                        ACT([BK[b_tr]], ["KLT%d%s" % (ai, n)], out=KLT[ai][0:64, :], in_=ptb[0:64, 0:128], func=AF.Identity)

                    def hDS(hb, c):
                        n = hb["n"]
                        ai = c % 2
                        b_ds = hb["bk"][3]
                        MM(["KLT%d%s" % (ai, n), "V64" + n], [BK[b_ds]], banks[b_ds][:, 0:128], lhsT=hb["KLT"][ai][0:64, :], rhs=hb["V64"][0:64, c, :], start=True, stop=True)

                    def hO(hb, c):
                        n = hb["n"]
                        cs = slice(c * CH, (c + 1) * CH)
                        ai = c % 2
                        b_o = hb["bk"][1]
                        oc = slice((c % 8) * CH, (c % 8 + 1) * CH)
                        MM(["V64" + n, "ATM%d%s" % (ai, n)], [BK[b_o]], banks[b_o][:, oc], lhsT=hb["V64"][0:64, c, :], rhs=hb["ATM"][ai][0:64, :], start=True, stop=False)
                        MM(["SBF" + n, "QD" + n], [BK[b_o]], banks[b_o][:, oc], lhsT=hb["SBF"], rhs=hb["QD"][:, cs], start=False, stop=True)

                    def hUpd(hb, c):
                        n = hb["n"]
                        b_ds = hb["bk"][3]
                        DVE("scalar_tensor_tensor", ["SS" + n, "EBL" + n, BK[b_ds]], ["SS" + n], out=hb["SS"], in0=hb["SS"],
                            scalar=hb["EBL"][:, c:c + 1], in1=banks[b_ds][:, 0:128], op0=ALU.mult, op1=ALU.add)
                        ACT(["SS" + n], ["SBF" + n], out=hb["SBF"], in_=hb["SS"], func=AF.Identity)

                    def hEvac(hb, c):
                        n = hb["n"]
                        b_o = hb["bk"][1]
                        if c % 8 == 7:
                            sl = slice((c // 8) * 512, (c // 8 + 1) * 512)
                            ACT([BK[b_o]], ["OE" + n], out=hb["OE"][:, sl], in_=banks[b_o][:, :], func=AF.Identity)

                    def h_post(h, hb, T2=T2, SQB=SQB, RS=RS, YST=YST, cvl=cvl):
                        n = hb["n"]
                        OE = hb["OE"]
                        yi = yst_i["i"]; yst_i["i"] ^= 1

                        def cons_g(tg, ps, bkk):
                            sl = slice(tg * 512, (tg + 1) * 512)
                            ACT([bkk], ["T2"], out=T2[:, sl], in_=ps, func=AF.Silu)
                        inproj_fm(2048 + h * 128, 128, cons_g)
                        for tg in range(NTG):
                            sl = slice(tg * 512, (tg + 1) * 512)
                            ACT(["OE" + n], ["SQB"], out=SQB, in_=OE[:, sl], func=AF.Square)
                            MM(["SQB", "cstb"], [BK[4]], banks[4][:, :], lhsT=ones_b, rhs=SQB, start=True, stop=True)
                            ACT([BK[4]], ["RS"], out=RS, in_=banks[4][:, :], func=AF.Ln, scale=1.0 / 128, bias=RMS_EPS)
                            ACT(["RS"], ["RS"], out=RS, in_=RS, func=AF.Exp, scale=-0.5)
                            DVE("tensor_tensor", ["OE" + n, "RS"], ["OE" + n], out=OE[:, sl], in0=OE[:, sl], in1=RS, op=ALU.mult)
                            DVE("scalar_tensor_tensor", ["OE" + n, "T2", "cvt"], ["YST%d" % yi], out=YST[yi][:, sl], in0=OE[:, sl],
                                scalar=cvl[:, 112 + h:113 + h], in1=T2[:, sl], op0=ALU.mult, op1=ALU.mult)
                        DMA("sp", dsem_o[yi], ["YST%d" % yi], ["YD"], out=YD[512 + h * 128:512 + (h + 1) * 128, :], in_=YST[yi])

                    for hp in (range(2) if DBG >= 2 else []):
                        hs = (2 * hp, 2 * hp + 1)
                        for hi in range(2):
                            h_prep(hs[hi], HB[hi])
                        for hi in range(2):
                            hA(HB[hi], 0)
                            hDS(HB[hi], 0)
                        for c in range(NCK):
                            for hi in range(2):
                                hb = HB[hi]
                                if c + 1 < NCK:
                                    hA(hb, c + 1)
                                hO(hb, c)
                                hUpd(hb, c)
                                if c + 1 < NCK:
                                    hDS(hb, c + 1)
                                hEvac(hb, c)
                        for hi in range(2):
                            h_post(hs[hi], HB[hi])
                    P.barrier()

                    CV.reset(tmp_off)
                    W80 = CV.take([KC, 80], BF16)
                    W8 = CV.take([KC, 8], BF16)
                    CUMP = CV.take([S], F32)
                    R1 = CV.take([S], F32)
                    R2 = CV.take([S], F32)
                    MID = CV.take([S], BF16)
                    CST = CV.take([S], BF16)
                    LH = CV.take([S], BF16)
                    RH = CV.take([S], BF16)
                    QT = CV.take([S], BF16)
                    KT = CV.take([S], BF16)
                    VT = CV.take([NTB, 256], BF16)
                    PT = [CV.take([512], BF16) for _ in range(2)]
                    SM = [CV.take([128], F32) for _ in range(2)]
                    RC = CV.take([512], F32)
                    YST = [CV.take([S], BF16) for _ in range(2)]
                    ONESB = CV.take([S], BF16)
                    POOL("memset", [], ["ONESB"], ONESB, 1.0)
                    DVE("memset", [], ["W80"], W80, 0.0)
                    for off in (0, 8, 32, 40, 64, 72):
                        DMA("pool", dsem_m[1], ["W80"], ["W80d"], out=W80[:, :, off:off + 8], in_=winv[:, :, 5632:5640])
                    nfb_ap = NFBT[0:80, l:l + 1]

                    def cons_f(tg, ps, bkk, R1=R1, R2=R2, nfb_ap=nfb_ap):
                        sl = slice(tg * 512, (tg + 1) * 512)
                        ACT([bkk, "nfbt"], ["R1"], out=R1[0:80, sl], in_=ps, func=AF.Exp, scale=-1.0, bias=nfb_ap)
                        ACT(["R1"], ["R2"], out=R2[0:80, sl], in_=R1[0:80, sl], func=AF.Ln, bias=1.0)
                    inproj_fm(5632, 80, cons_f, wtile=(W80, ["W80", "W80d"]), nbk=2)
                    DVE("tensor_tensor_scan", ["R2", "ONESB"], ["CUMP"], out=CUMP[0:80, :],
                        data0=ONESB[0:80, :], data1=R2[0:80, :], initial=0.0, op0=ALU.mult, op1=ALU.add)
                    DVE("tensor_copy", ["CUMP"], ["CST"], out=CST[0:80, :], in_=CUMP[0:80, :])
                    DVE("tensor_tensor", ["CUMP", "CST"], ["R1"], out=R1[32:64, :], in0=CUMP[32:64, :], in1=CST[32:64, :], op=ALU.subtract)
                    DVE("tensor_tensor", ["CUMP", "CST"], ["R1"], out=R1[64:80, :], in0=CUMP[64:80, :], in1=CST[64:80, :], op=ALU.subtract)
                    DVE("tensor_copy", ["R1"], ["CST"], out=CST[32:48, :], in_=R1[32:48, :])
                    DVE("tensor_copy", ["R1"], ["MID"], out=MID[64:80, :], in_=R1[64:80, :])
                    DVE("tensor_tensor", ["R1", "MID"], ["R2"], out=R2[64:80, :], in0=R1[64:80, :], in1=MID[64:80, :], op=ALU.subtract)
                    DVE("tensor_copy", ["R2"], ["CST"], out=CST[64:80, :], in_=R2[64:80, :])
                    scale_q = 128.0 ** -0.5
                    for h in (range(8) if DBG >= 3 else []):
                        DVE("tensor_scalar", ["CST", "sel"], ["LH"], out=LH[0:80, :], in0=CST[0:80, :], scalar1=sel2[0:80, h:h + 1],
                            scalar2=sel1[0:80, h:h + 1], op0=ALU.mult, op1=ALU.add)
                        DVE("tensor_scalar", ["CST", "sel", "nsel1"], ["RH"], out=RH[0:80, :], in0=CST[0:80, :], scalar1=nsel1[0:80, h:h + 1],
                            scalar2=sel2[0:80, h:h + 1], op0=ALU.mult, op1=ALU.add)

                        def cons_qf(tg, ps, bkk, QT=QT):
                            ACT([bkk], ["QT"], out=QT[:, tg * 512:(tg + 1) * 512], in_=ps, func=AF.Identity, scale=scale_q)
                        inproj_fm(2560 + h * 128, 128, cons_qf, nbk=2)

                        def cons_kf(tg, ps, bkk, KT=KT):
                            ACT([bkk], ["KT"], out=KT[:, tg * 512:(tg + 1) * 512], in_=ps, func=AF.Identity)
                        inproj_fm(3584 + h * 128, 128, cons_kf, nbk=2)
                        if h % 2 == 0:
                            wi = WP1.next(4608 + h * 128)
                            for tb in range(NTB):
                                bi = rot("bk", 2)
                                bk = banks[bi]
                                for kc in range(KC):
                                    MM([WK[wi][0], WK[wi][1], "XT"], [BK[bi]], bk[:, 0:256], lhsT=XT[:, kc, tb * 128:(tb + 1) * 128],
                                       rhs=wsl[wi][:, kc, 0:256], start=(kc == 0), stop=(kc == KC - 1))
                                ACT([BK[bi]], ["VT"], out=VT[:, tb, :], in_=bk[:, 0:256], func=AF.Identity)
                        ho = (h % 2) * 128
                        yi = yst_i["i"]; yst_i["i"] ^= 1
                        ORP = [(6, 7), (2, 3)]

                        def stageA(G, j, KT=KT, QT=QT, LH=LH, RH=RH, SM=SM, PT=PT):
                            c0 = max(0, j - 4 * G)
                            cols = slice(c0 * 128, 512)
                            gcols = slice(G * 512 + c0 * 128, G * 512 + 512)
                            sbi = 4 + rot("m", 2)
                            sbk = banks[sbi]
                            pi = rot("t", 2)
                            MM(["KT", "QT"], [BK[sbi]], sbk[:, cols], lhsT=KT[:, j * 128:(j + 1) * 128], rhs=QT[:, gcols], start=True, stop=False)
                            MM(["LH", "RH"], [BK[sbi]], sbk[:, cols], lhsT=LH[0:80, j * 128:(j + 1) * 128], rhs=RH[0:80, gcols], start=False, stop=True)
                            if j >= 4 * G:
                                dsl = slice(c0 * 128, (c0 + 1) * 128)
                                DVE("tensor_tensor", [BK[sbi], "maskneg"], ["SM%d" % pi], out=SM[pi], in0=sbk[:, dsl], in1=maskneg, op=ALU.add)
                                ACT(["SM%d" % pi], ["PT%d" % pi], out=PT[pi][:, dsl], in_=SM[pi], func=AF.Exp)
                                if c0 < 3:
                                    rsl = slice((c0 + 1) * 128, 512)
                                    ACT([BK[sbi]], ["PT%d" % pi], out=PT[pi][:, rsl], in_=sbk[:, rsl], func=AF.Exp)
                            else:
                                ACT([BK[sbi]], ["PT%d" % pi], out=PT[pi], in_=sbk[:, :], func=AF.Exp)
                            return (G, j, cols, pi)

                        def stageB(G, j, cols, pi, VT=VT, PT=PT, RC=RC, YST=YST, ho=ho, yi=yi):
                            nkb = 4 * G + 4
                            obi, rbi = ORP[G % 2]
                            MM(["VT", "PT%d" % pi], [BK[obi]], banks[obi][:, cols], lhsT=VT[:, j, ho:ho + 128], rhs=PT[pi][:, cols],
                               start=(j == 0), stop=(j == nkb - 1))
                            MM(["cstb", "PT%d" % pi], [BK[rbi]], banks[rbi][:, cols], lhsT=ones_b, rhs=PT[pi][:, cols],
                               start=(j == 0), stop=(j == nkb - 1))
                            if j == nkb - 1:
                                DVE("reciprocal", [BK[rbi]], ["RC"], out=RC, in_=banks[rbi][:, :])
                                DVE("tensor_tensor", [BK[obi], "RC"], ["YST%d" % yi], out=YST[yi][:, G * 512:(G + 1) * 512], in0=banks[obi][:, :], in1=RC, op=ALU.mult)
                        prev = None
                        for G in range(NTG):
                            for j in range(4 * G + 4):
                                cur = stageA(G, j)
                                if prev is not None:
                                    stageB(*prev)
                                prev = cur
                        stageB(*prev)
                        DMA("sp", dsem_o[yi], ["YST%d" % yi], ["YD"], out=YD[1024 + h * 128:1024 + (h + 1) * 128, :], in_=YST[yi])
                    P.barrier()

                    def ln_a_piece(PRE, kf, nb, tmp):
                        XB, SQ, MU, M2, RSTD = tmp
                        xi = nb % 2
                        ACT([kf(nb)], ["XB%d" % xi], out=XB[xi], in_=PRE[:, nb, :], func=AF.Identity)
                        ACT([kf(nb)], ["SQ%d" % xi], out=SQ[xi], in_=PRE[:, nb, :], func=AF.Square)
                        MM(["XB%d" % xi, "cstb"], [BK[6]], banks[6][:, :], lhsT=ones_b, rhs=XB[xi], start=(nb == 0), stop=(nb == KC - 1))
                        MM(["SQ%d" % xi, "cstb"], [BK[7]], banks[7][:, :], lhsT=ones_b, rhs=SQ[xi], start=(nb == 0), stop=(nb == KC - 1))

                    def ln_a_final(tmp):
                        XB, SQ, MU, M2, RSTD = tmp
                        DVE("tensor_scalar", [BK[6]], ["MU"], out=MU, in0=banks[6][:, :], scalar1=1.0 / D, scalar2=None, op0=ALU.mult)
                        DVE("tensor_tensor", ["MU"], ["M2"], out=M2, in0=MU, in1=MU, op=ALU.mult)
                        DVE("scalar_tensor_tensor", [BK[7], "M2"], ["M2"], out=M2, in0=banks[7][:, :], scalar=1.0 / D, in1=M2, op0=ALU.mult, op1=ALU.subtract)
                        ACT(["M2"], ["RSTD"], out=RSTD, in_=M2, func=AF.Ln, bias=LN_EPS)
                        ACT(["RSTD"], ["RSTD"], out=RSTD, in_=RSTD, func=AF.Exp, scale=-0.5)

                    def ln_b_piece(PRE, kf, nb, tcols, gcol, bcol, tmp, XT=XT, cvl=cvl):
                        XB, SQ, MU, M2, RSTD = tmp
                        k = kf(nb)
                        DVE("tensor_tensor", [k, "MU"], [k], out=PRE[:, nb, :], in0=PRE[:, nb, :], in1=MU, op=ALU.subtract)
                        DVE("tensor_tensor", [k, "RSTD"], [k], out=PRE[:, nb, :], in0=PRE[:, nb, :], in1=RSTD, op=ALU.mult)
                        ACT([k, "cvt"], [k], out=PRE[:, nb, :], in_=PRE[:, nb, :], func=AF.Identity,
                            scale=cvl[:, gcol + nb:gcol + nb + 1], bias=cvl[:, bcol + nb:bcol + nb + 1])
                        DVE("tensor_copy", [k], ["XT"], out=XT[:, nb, tcols], in_=PRE[:, nb, :])

                    CV.reset(off_after_xt)
                    wsl = [CV.take([KC, 256], BF16) for _ in range(NWS)]
                    YT = [CV.take([KC, 512], BF16) for _ in range(2)]
                    PREb = [CV.take([KC, 512], F32) for _ in range(2)]
                    lntmp = ([CV.take([512], BF16) for _ in range(2)], [CV.take([512], BF16) for _ in range(2)],
                             CV.take([512], F32), CV.take([512], F32), CV.take([512], F32))
                    woutv = w_out_d[l].rearrange("(c p) n -> p c n", p=128)

                    def mk_ld2(np2, wsl=wsl, woutv=woutv):
                        def fn(i):
                            DMA("pool", wsem[i][0], [], [WK[i][0], WK[i][1]], out=wsl[i][:, :, 0:256], in_=woutv[:, :, np2 * 256:(np2 + 1) * 256])
                        return (np2, fn)
                    WP2 = WPlan([mk_ld2(np2) for tg in (range(NTG) if DBG >= 4 else []) for np2 in range(KC // 2)])
                    def pk2(bb):
                        return lambda nb: "PRE%d_%d" % (bb, nb)
                    allk = lambda bb: ["PRE%d_%d" % (bb, nb) for nb in range(KC)]
                    prev_tg = None
                    for tg in (range(NTG) if DBG >= 4 else []):
                        tcols = slice(tg * 512, (tg + 1) * 512)
                        b = tg % 2
                        DMA("sp", dsem_x[b], ["YD"], ["YT%d" % b], out=YT[b], in_=YDv[:, :, tcols])
                        DMA("sp", dsem_m[2 + b], ["XR"], allk(b), out=PREb[b], in_=XRv[:, :, tcols])
                        for np2 in range(KC // 2):
                            wi = WP2.next(np2)
                            for hf in range(2):
                                nb = np2 * 2 + hf
                                bi = rot("bk", 4)
                                bk = banks[bi]
                                for kc in range(KC):
                                    MM([WK[wi][hf], "YT%d" % b], [BK[bi]], bk[:, :], lhsT=wsl[wi][:, kc, hf * 128:(hf + 1) * 128],
                                       rhs=YT[b][:, kc, :], start=(kc == 0), stop=(kc == KC - 1))
                                DVE("scalar_tensor_tensor", [pk2(b)(nb), BK[bi]], [pk2(b)(nb)], out=PREb[b][:, nb, :], in0=PREb[b][:, nb, :],
                                    scalar=ALPHA, in1=bk[:, :], op0=ALU.mult, op1=ALU.add)
                                if nb > 0:
                                    ln_a_piece(PREb[b], pk2(b), nb - 1, lntmp)
                                if prev_tg is not None:
                                    pb_, ptc = prev_tg
                                    ln_b_piece(PREb[pb_], pk2(pb_), nb, ptc, 0, 16, lntmp)
                        if prev_tg is not None:
                            pb_, ptc = prev_tg
                            DMA("sp", dsem_o[pb_], allk(pb_), ["XR"], out=XRv[:, :, ptc], in_=PREb[pb_])
                        ln_a_piece(PREb[b], pk2(b), KC - 1, lntmp)
                        ln_a_final(lntmp)
                        prev_tg = (b, tcols)
                    if prev_tg is not None:
                        pb_, ptc = prev_tg
                        for nb in range(KC):
                            ln_b_piece(PREb[pb_], pk2(pb_), nb, ptc, 0, 16, lntmp)
                        DMA("sp", dsem_o[pb_], allk(pb_), ["XR"], out=XRv[:, :, ptc], in_=PREb[pb_])
                    P.barrier()

                    CV.reset(off_after_xt)
                    wsl = [CV.take([KC, 256], BF16) for _ in range(NWS)]
                    ACC = CV.take([KC, TH], F32)
                    AT = CV.take([GF, TH], BF16)
                    GS = [CV.take([2 + 512], F32) for _ in range(2)]
                    CC = [CV.take([512], F32) for _ in range(2)]
                    EE = [CV.take([512], F32) for _ in range(2)]
                    lntmp = ([CV.take([512], BF16) for _ in range(2)], [CV.take([512], BF16) for _ in range(2)],
                             CV.take([512], F32), CV.take([512], F32), CV.take([512], F32))
                    wgv = w_gate_d[l].rearrange("(c p) n -> p c n", p=128)
                    wvv = w_val_d[l].rearrange("(c p) n -> p c n", p=128)
                    wdv = w_down_d[l].rearrange("(f p) n -> p f n", p=128)
                    DVE("memset", [], ["HALO"], HALO, 0.0)
                    NG = NFB // GF

                    def mk_gv(f, wsl=wsl, wgv=wgv, wvv=wvv):
                        def fn(i):
                            DMA("pool", wsem[i][0], [], [WK[i][0]], out=wsl[i][:, :, 0:128], in_=wgv[:, :, f * 128:(f + 1) * 128])
                            DMA("pool", wsem[i][1], [], [WK[i][1]], out=wsl[i][:, :, 128:256], in_=wvv[:, :, f * 128:(f + 1) * 128])
                        return (("gv", f), fn)

                    def mk_wd(f0, np2, wsl=wsl, wdv=wdv):
                        def fn(i):
                            DMA("pool", wsem[i][0], [], [WK[i][0], WK[i][1]], out=wsl[i][:, 0:GF, :], in_=wdv[:, f0:f0 + GF, np2 * 256:(np2 + 1) * 256])
                        return (("wd", f0, np2), fn)
                    plan3 = []
                    for hfq in (range(NHALF) if DBG >= 5 else []):
                        for g in range(NG):
                            plan3 += [mk_gv(g * GF + fi) for fi in range(GF)]
                            plan3 += [mk_wd(g * GF, np2) for np2 in range(KC // 2)]
                    WP3 = WPlan(plan3)
                    for hfq in (range(NHALF) if DBG >= 5 else []):
                        t0 = hfq * TH
                        ak = lambda nb, tg: "ACC_%d_%d" % (nb, tg)
                        akall = [ak(nb, tg) for nb in range(KC) for tg in range(NTGH)]
                        DMA("sp", dsem_m[2], ["XR"], akall, out=ACC, in_=XRv[:, :, t0:t0 + TH])
                        for nb in range(KC):
                            ACT([], [ak(nb, tg) for tg in range(NTGH)], out=ACC[:, nb, :], in_=ACC[:, nb, :], func=AF.Identity, scale=ALPHA)
                        for g in range(NG):
                            for fi in range(GF):
                                f = g * GF + fi
                                wi = WP3.next(("gv", f))
                                cw = [cvl[:, 120 + j * 44 + f:121 + j * 44 + f] for j in range(3)]
                                cb = cvl[:, 64 + f:65 + f]
                                for tg in range(NTGH):
                                    tsl = slice(t0 + tg * 512, t0 + (tg + 1) * 512)
                                    si = rot("t", 2)
                                    bgi = si * 2
                                    bvi = si * 2 + 1
                                    for (bi, hf) in ((bgi, 0), (bvi, 1)):
                                        for kc in range(KC):
                                            MM([WK[wi][hf], "XT"], [BK[bi]], banks[bi][:, :], lhsT=wsl[wi][:, kc, hf * 128:(hf + 1) * 128],
                                               rhs=XT[:, kc, tsl], start=(kc == 0), stop=(kc == KC - 1))
                                    gs_, cc_, ee_ = GS[si], CC[si], EE[si]
                                    gk, ck, ek = "GS%d" % si, "CC%d" % si, "EE%d" % si
                                    ACT(["HALO"], [gk], out=gs_[:, 0:2], in_=HALO[:, f, :], func=AF.Identity)
                                    ACT([BK[bgi]], [gk], out=gs_[:, 2:514], in_=banks[bgi][:, :], func=AF.Identity)
                                    ACT([gk], ["HALO"], out=HALO[:, f, :], in_=gs_[:, 512:514], func=AF.Identity)
                                    ACT([gk, "cvt"], [ck], out=cc_, in_=gs_[:, 2:514], func=AF.Identity, scale=cw[2], bias=cb)
                                    DVE("scalar_tensor_tensor", [gk, ck, "cvt"], [ck], out=cc_, in0=gs_[:, 1:513], scalar=cw[1], in1=cc_, op0=ALU.mult, op1=ALU.add)
                                    DVE("scalar_tensor_tensor", [gk, ck, "cvt"], [ck], out=cc_, in0=gs_[:, 0:512], scalar=cw[0], in1=cc_, op0=ALU.mult, op1=ALU.add)
                                    ACT([ck], [ek], out=ee_, in_=cc_, func=AF.Silu)
                                    DVE("tensor_tensor", [ek, BK[bvi]], ["AT"], out=AT[:, fi, tg * 512:(tg + 1) * 512], in0=ee_, in1=banks[bvi][:, :], op=ALU.mult)
                            f0 = g * GF
                            for np2 in range(KC // 2):
                                wi = WP3.next(("wd", f0, np2))
                                for hf in range(2):
                                    nb = np2 * 2 + hf
                                    for tg in range(NTGH):
                                        bi = 4 + rot("m", 2)
                                        for fi in range(GF):
                                            MM([WK[wi][hf], "AT"], [BK[bi]], banks[bi][:, :], lhsT=wsl[wi][:, fi, hf * 128:(hf + 1) * 128],
                                               rhs=AT[:, fi, tg * 512:(tg + 1) * 512], start=(fi == 0), stop=(fi == GF - 1))
                                        DVE("tensor_tensor", [ak(nb, tg), BK[bi]], [ak(nb, tg)], out=ACC[:, nb, tg * 512:(tg + 1) * 512],
                                            in0=ACC[:, nb, tg * 512:(tg + 1) * 512], in1=banks[bi][:, :], op=ALU.add)
                        for tg in range(NTGH):
                            tcols = slice(t0 + tg * 512, t0 + (tg + 1) * 512)
                            PREv = ACC[:, :, tg * 512:(tg + 1) * 512]
                            kf = (lambda tg: (lambda nb: ak(nb, tg)))(tg)
                            for nb in range(KC):
                                ln_a_piece(PREv, kf, nb, lntmp)
                            ln_a_final(lntmp)
                            for nb in range(KC):
                                ln_b_piece(PREv, kf, nb, tcols, 32, 48, lntmp)
                        DMA("sp", dsem_o[0], akall, ["XR"], out=XRv[:, :, t0:t0 + TH], in_=ACC)
                    P.barrier()

                CV.reset(off_after_xt)
                xo_in = [CV.take([KC, 128], F32) for _ in range(2)]
                xo_st = [CV.take([D], F32) for _ in range(2)]
                for tb in range(NTB):
                    b = tb % 2
                    DMA("sp", dsem_x[b], ["XR"], ["xoin%d" % b], out=xo_in[b], in_=XRv[:, :, tb * 128:(tb + 1) * 128])
                    for q4 in range(4):
                        bi = rot("bk", 2)
                        bk = banks[bi]
                        for j in range(4):
                            dc = q4 * 4 + j
                            TR(["xoin%d" % b, "ident_f"], [BK[bi]], out=bk[:, j * 128:(j + 1) * 128], in_=xo_in[b][:, dc, :], identity=ident_f)
                        DVE("tensor_copy", [BK[bi]], ["xost%d" % b], out=xo_st[b][:, q4 * 512:(q4 + 1) * 512], in_=bk[:, :])
                    DMA("sp", dsem_o[b], ["xost%d" % b], ["OUT"], out=out_d[tok0 + tb * 128: tok0 + (tb + 1) * 128, :], in_=xo_st[b])
                P.barrier()
        except _Stop:
            pass
        P.barrier()
        P.emit()
    return nc


_NC_CACHE = {}

WNAMES = ["w_in", "fox_f_bias", "pool_w", "pool_scale", "hgrn_lb_logits", "hgrn_norm_g", "w_out", "ln1_g", "ln1_b",
          "w_gate", "w_val", "conv_w", "conv_b", "w_down", "ln2_g", "ln2_b"]


def kernel(**inputs):
    x = np.ascontiguousarray(inputs["x"], dtype=np.float32)
    B, S, Dm = x.shape
    L = inputs["w_in"].shape[0]
    ncores = 8
    nseq = B // ncores
    key = (S, nseq, L)
    if key not in _NC_CACHE:
        _NC_CACHE[key] = build(S, nseq, L)
    nc = _NC_CACHE[key]
    ws = {k: np.ascontiguousarray(inputs[k], dtype=np.float32) for k in WNAMES}
    in_maps = []
    for c in range(ncores):
        m = {"x": x[c * nseq:(c + 1) * nseq].reshape(nseq * S, Dm)}
        m.update(ws)
        in_maps.append(m)
    res = run_bass_kernel_spmd(nc, in_maps, core_ids=list(range(ncores)))
    outs = [np.asarray(r["out"]).reshape(nseq, S, Dm) for r in res.results]
    return np.concatenate(outs, axis=0).astype(np.float32)
```

```python
import contextlib
import os
import numpy as np
import concourse.bass as bass
import concourse.mybir as mybir
from concourse.bass_utils import run_bass_kernel_spmd

F32 = mybir.dt.float32
BF16 = mybir.dt.bfloat16
AF = mybir.ActivationFunctionType
ALU = mybir.AluOpType

SAME_ENGINE_SYNC = True

D = 2048
KC = 16
DFF = 5632
NFB = 44
INC = 5640
ALPHA = 8.0 ** 0.25
LN_EPS = 1e-5
RMS_EPS = 1e-6
CH = 64


class Prog:
    ENGS = ("pe", "act", "dve", "pool", "sp")

    def __init__(self, nc):
        self.nc = nc
        self.q = {e: [] for e in self.ENGS}
        self.last_write = {}
        self.reads_since = {}
        self.seen = {e: {} for e in self.ENGS}
        self.needed = {e: set() for e in self.ENGS}
        self.dma_cnt = {}
        self.dma_sems = []

    def dma_sem(self, name):
        self.dma_cnt[name] = 0
        self.dma_sems.append(name)
        return name

    def _deps(self, reads, writes):
        deps = []
        for k in reads:
            ev = self.last_write.get(k)
            if ev is not None:
                deps.append(ev)
        for k in writes:
            ev = self.last_write.get(k)
            if ev is not None:
                deps.append(ev)
            deps.extend(self.reads_since.get(k, ()))
        return deps

    def _commit(self, ev, reads, writes):
        for k in reads:
            self.reads_since.setdefault(k, []).append(ev)
        for k in writes:
            self.last_write[k] = ev
            self.reads_since[k] = []

    def _waits(self, eng, deps):
        waits = {}
        seen = self.seen[eng]
        for (sk, idx) in deps:
            if sk == eng and (eng == "pe" or not SAME_ENGINE_SYNC):
                continue
            if seen.get(sk, -1) >= idx:
                continue
            if waits.get(sk, -1) < idx:
                waits[sk] = idx
        for sk, idx in waits.items():
            seen[sk] = idx
            if sk in self.needed:
                self.needed[sk].add(idx)
        return waits

    def op(self, eng, fn, reads=(), writes=()):
        writes = list(writes) + [k for k in reads if k.startswith("pb") and k not in writes]
        deps = self._deps(reads, writes)
        waits = self._waits(eng, deps)
        idx = len(self.q[eng])
        self.q[eng].append(("op", fn, waits, None))
        ev = (eng, idx)
        self._commit(ev, reads, writes)
        return ev

    def dma(self, queue, sem, fn, reads=(), writes=()):
        deps = self._deps(reads, writes)
        waits = self._waits(queue, deps)
        self.dma_cnt[sem] += 16
        ev = (sem, self.dma_cnt[sem])
        self.q[queue].append(("dma", fn, waits, sem))
        self._commit(ev, reads, writes)
        return ev

    def barrier(self):
        deps = []
        for e in self.ENGS:
            for i in range(len(self.q[e]) - 1, -1, -1):
                if self.q[e][i][0] == "op":
                    deps.append((e, i))
                    break
        for s in self.dma_sems:
            if self.dma_cnt[s] > 0:
                deps.append((s, self.dma_cnt[s]))
        for e in self.ENGS:
            waits = self._waits(e, [d for d in deps if d[0] != e])
            self.q[e].append(("wait", None, waits, None))

    def emit(self):
        nc = self.nc
        with contextlib.ExitStack() as st:
            sems = {}
            for e in self.ENGS:
                sems[e] = st.enter_context(nc.semaphore("s_" + e))
            for d in self.dma_sems:
                sems[d] = st.enter_context(nc.semaphore("d_" + d))
            val = {}
            for e in self.ENGS:
                c = 0
                v = {}
                for i in range(len(self.q[e])):
                    if i in self.needed[e]:
                        c += 1
                        v[i] = c
                val[e] = v
            block = st.enter_context(nc.Block())

            def run(e, engh):
                needed = self.needed[e]
                for i, (kind, fn, waits, dsem) in enumerate(self.q[e]):
                    for sk, idx in waits.items():
                        if sk in val:
                            engh.wait_ge(sems[sk], val[sk][idx])
                        else:
                            engh.wait_ge(sems[sk], idx)
                    if kind == "wait":
                        continue
                    mname, margs, mkw = fn
                    bi = getattr(engh, mname)(*margs, **mkw)
                    if kind == "dma":
                        bi.then_inc(sems[dsem], 16)
                    elif i in needed:
                        bi.then_inc(sems[e], 1)

            @block.tensor
            def _(eh):
                run("pe", eh)

            @block.scalar
            def _(eh):
                run("act", eh)

            @block.vector
            def _(eh):
                run("dve", eh)

            @block.gpsimd
            def _(eh):
                run("pool", eh)

            @block.sync
            def _(eh):
                run("sp", eh)


class _Stop(Exception):
    pass


def build(S, NSEQ, L, GF=11):
    assert S % 512 == 0
    NTG = S // 512
    NTB = S // 128
    NCK = S // CH
    TH = min(1024, S)
    NHALF = S // TH
    NTGH = TH // 512
    nc = bass.Bass("TRN2", target_bir_lowering=False)
    P = Prog(nc)
    DBG = int(os.environ.get("KDBG", "9"))
    KSUB = int(os.environ.get("KSUB", "9"))

    def ACT(reads, writes, **kw):
        P.op("act", ("activation", (), kw), reads, writes)

    def MM(reads, writes, out, **kw):
        P.op("pe", ("matmul", (out,), kw), reads, writes)

    def TR(reads, writes, **kw):
        P.op("pe", ("transpose", (), kw), reads, writes)

    def DVE(m, reads, writes, *a, **kw):
        P.op("dve", (m, a, kw), reads, writes)

    def POOL(m, reads, writes, *a, **kw):
        P.op("pool", (m, a, kw), reads, writes)

    def DMA(queue, sem, reads, writes, **kw):
        P.dma(queue, sem, ("dma_start", (), kw), reads, writes)

    def din(name, shape):
        return nc.dram_tensor(name, shape, F32, kind="ExternalInput").ap()

    x_d = din("x", [NSEQ * S, D])
    w_in_d = din("w_in", [L, D, INC])
    fbias_d = din("fox_f_bias", [L, 8])
    pool_w_d = din("pool_w", [L, 4, 128, 128])
    pool_scale_d = din("pool_scale", [L, 512])
    lbl_d = din("hgrn_lb_logits", [L, 512])
    normg_d = din("hgrn_norm_g", [L, 512])
    w_out_d = din("w_out", [L, D, D])
    ln1g_d = din("ln1_g", [L, D])
    ln1b_d = din("ln1_b", [L, D])
    w_gate_d = din("w_gate", [L, D, DFF])
    w_val_d = din("w_val", [L, D, DFF])
    conv_w_d = din("conv_w", [L, 3, DFF])
    conv_b_d = din("conv_b", [L, DFF])
    w_down_d = din("w_down", [L, DFF, D])
    ln2g_d = din("ln2_g", [L, D])
    ln2b_d = din("ln2_b", [L, D])
    out_d = nc.dram_tensor("out", [NSEQ * S, D], F32, kind="ExternalOutput").ap()
    XR = nc.dram_tensor("xr_scr", [D, S], F32).ap()
    YD = nc.dram_tensor("y_scr", [D, S], BF16).ap()
    XRv = XR.rearrange("(c p) t -> p c t", p=128)
    YDv = YD.rearrange("(c p) t -> p c t", p=128)

    st = contextlib.ExitStack()
    with st:
        def sb(name, shape, dt):
            return st.enter_context(nc.sbuf_tensor(name, shape, dt))

        ARENA_B = 198 * 1024
        arena = sb("arena", [128, ARENA_B // 2], BF16)
        NCST = 128 * 4 + 16 + 8 * 4 + L * 8 + L + 2 + L * 256 + 2 * 128
        cst = sb("cst", [128, NCST], F32)
        cstb = sb("cstb", [128, 128 + 128 + 64], BF16)
        banks = [st.enter_context(nc.psum_tensor("pb%d" % i, [128, 512], F32)) for i in range(8)]
        BK = ["pb%d" % i for i in range(8)]

        class Carver:
            def __init__(self):
                self.off = 0

            def reset(self, off=0):
                self.off = off

            def take(self, free_shape, dt):
                n = int(np.prod(free_shape))
                esz = 4 if dt == F32 else 2
                nb = n * esz
                nbp = (nb + 63) // 64 * 64
                assert self.off + nbp <= ARENA_B, (self.off, nbp, ARENA_B)
                a = arena[:, self.off // 2: self.off // 2 + nb // 2]
                self.off += nbp
                if dt == F32:
                    a = a.bitcast(F32)
                if len(free_shape) == 2:
                    a = a.rearrange("p (a b) -> p a b", b=free_shape[1])
                elif len(free_shape) == 3:
                    a = a.rearrange("p (a b c) -> p a b c", b=free_shape[1], c=free_shape[2])
                return a

        CV = Carver()
        try:

            o = 0
            ident_f = cst[:, o:o + 128]; o += 128
            maskneg = cst[:, o:o + 128]; o += 128
            seltmp = cst[:, o:o + 128]; o += 128
            o += 128
            invc = cst[:, o:o + 16]; o += 16
            sel1 = cst[:, o:o + 8]; o += 8
            sel2 = cst[:, o:o + 8]; o += 8
            nsel1 = cst[:, o:o + 8]; o += 8
            o += 8
            LBT = cst[:, o:o + L * 4].rearrange("p (l h) -> p l h", h=4); o += L * 4
            OMLT = cst[:, o:o + L * 4].rearrange("p (l h) -> p l h", h=4); o += L * 4
            NFBT = cst[:, o:o + L]; o += L
            ones_col = cst[:, o:o + 1]; o += 1
            o += 1
            CVT = cst[:, o:o + L * 256].rearrange("p (l c) -> p l c", c=256); o += L * 256
            HALO = cst[:, o:o + 128].rearrange("p (f j) -> p f j", j=2)[:, 0:NFB, :]; o += 128
            SST = cst[:, o:o + 128]; o += 128
            assert o <= NCST
            ident_b = cstb[:, 0:128]
            ones_b = cstb[:, 128:256]
            triu = cstb[0:64, 256:320]

            POOL("memset", [], ["cst"], cst[:], 0.0)
            POOL("memset", ["cst"], ["ident_f"], ident_f, 1.0)
            POOL("affine_select", ["ident_f"], ["ident_f"], out=ident_f, in_=ident_f, pattern=[[1, 128]],
                 compare_op=ALU.is_equal, fill=0.0, base=0, channel_multiplier=-1)
            POOL("memset", [], ["cstb"], cstb[:], 1.0)
            POOL("affine_select", ["cstb"], ["ident_b"], out=ident_b, in_=ident_b, pattern=[[1, 128]],
                 compare_op=ALU.is_equal, fill=0.0, base=0, channel_multiplier=-1)
            POOL("affine_select", ["cstb"], ["triu"], out=triu, in_=triu, pattern=[[1, 64]],
                 compare_op=ALU.is_ge, fill=0.0, base=0, channel_multiplier=-1)
            POOL("affine_select", ["cst"], ["maskneg"], out=maskneg, in_=maskneg, pattern=[[1, 128]],
                 compare_op=ALU.is_ge, fill=-30000.0, base=0, channel_multiplier=-1)
            POOL("memset", ["cst"], ["ones_col"], ones_col, 1.0)
            for t in range(16):
                POOL("memset", ["cst"], ["invc"], invc[:, t:t + 1], 1.0 / (t + 1))
            for si, (dst, offs) in enumerate(((sel1, (0, 32, 64)), (sel2, (8, 40, 72)))):
                for oi, off in enumerate(offs):
                    tmp = seltmp[:, (si * 3 + oi) * 8:(si * 3 + oi) * 8 + 8]
                    POOL("memset", ["cst"], ["seltmp"], tmp, 1.0)
                    POOL("affine_select", ["seltmp"], ["seltmp"], out=tmp, in_=tmp, pattern=[[-1, 8]],
                         compare_op=ALU.is_equal, fill=0.0, base=-off, channel_multiplier=1)
                    POOL("tensor_tensor", ["seltmp", "cst"], ["sel"], out=dst, in0=dst, in1=tmp, op=ALU.add)
            POOL("tensor_scalar", ["sel"], ["nsel1"], out=nsel1, in0=sel1, scalar1=-1.0, scalar2=None, op0=ALU.mult)

            if DBG == -3:
                raise _Stop()
            CV.reset()
            stgf = CV.take([L * 256], F32)
            dsem_c = P.dma_sem("c")
            DVE("memset", [], ["stg"], stgf, 0.0)
            for l in range(L):
                s0 = stgf[:, (l * 2) * 128:(l * 2 + 1) * 128]
                s1 = stgf[:, (l * 2 + 1) * 128:(l * 2 + 2) * 128]
                rows = [(ln1g_d[l], 0, 16), (ln1b_d[l], 16, 16), (ln2g_d[l], 32, 16), (ln2b_d[l], 48, 16),
                        (conv_b_d[l], 64, 44), (pool_scale_d[l], 108, 4), (normg_d[l], 112, 4), (lbl_d[l], 116, 4)]
                for (src, r0, nr) in rows:
                    DMA("sp", dsem_c, ["stg"], ["stgd"], out=s0[r0:r0 + nr, :], in_=src.rearrange("(c p) -> c p", p=128))
                cwv = conv_w_d[l].rearrange("j (c p) -> (j c) p", p=128)
                DMA("sp", dsem_c, ["stg"], ["stgd"], out=s0[120:128, :], in_=cwv[0:8, :])
                DMA("sp", dsem_c, ["stg"], ["stgd"], out=s1[0:124, :], in_=cwv[8:132, :])
                for off in (0, 8, 32, 40, 64, 72):
                    DMA("sp", dsem_c, ["cst"], ["nfbt"], out=NFBT[off:off + 8, l:l + 1],
                        in_=fbias_d[l].rearrange("(h o) -> h o", o=1))
            P.barrier()
            for l in range(L):
                for hh in range(2):
                    bk = banks[hh]
                    TR(["stgd", "stg", "ident_f"], [BK[hh]], out=bk[:, 0:128], in_=stgf[:, (l * 2 + hh) * 128:(l * 2 + hh + 1) * 128], identity=ident_f)
                    DVE("tensor_copy", [BK[hh], "cst"], ["cvt"], out=CVT[:, l, hh * 128:(hh + 1) * 128], in_=bk[:, 0:128])
            DVE("tensor_scalar", ["nfbt"], ["nfbt"], out=NFBT, in0=NFBT, scalar1=-1.0, scalar2=None, op0=ALU.mult)
            if DBG == -2:
                raise _Stop()
            EL = CV.take([L, 4], F32)
            TOT = CV.take([4], F32)
            for l in range(L):
                ACT(["cvt"], ["el"], out=EL[:, l, :], in_=CVT[:, l, 116:120], func=AF.Exp)
            DVE("tensor_copy", ["el"], ["tot"], out=TOT, in_=EL[:, 0, :])
            for l in range(1, L):
                DVE("tensor_tensor", ["el", "tot"], ["tot"], out=TOT, in0=TOT, in1=EL[:, l, :], op=ALU.add)
            DVE("reciprocal", ["tot"], ["tot"], out=TOT, in_=TOT)
            for l in range(1, L):
                DVE("tensor_tensor", ["el", "tot"], ["el"], out=EL[:, l, :], in0=EL[:, l, :], in1=TOT, op=ALU.mult)
                DVE("tensor_tensor", ["el", "lbt", "cst"], ["lbt"], out=LBT[:, l, :], in0=LBT[:, l - 1, :], in1=EL[:, l, :], op=ALU.add)
            DVE("tensor_scalar", ["lbt", "cst"], ["omlt"], out=OMLT, in0=LBT, scalar1=-1.0, scalar2=1.0, op0=ALU.mult, op1=ALU.add)
            P.barrier()

            if DBG == -1:
                raise _Stop()
            NWS = 3
            wsem = [[P.dma_sem("w%d_%d" % (i, j)) for j in range(2)] for i in range(NWS)]
            WK = [["ws%d_%d" % (i, j) for j in range(2)] for i in range(NWS)]
            wstate = {"i": 0}

            def wslot_take():
                i = wstate["i"]
                wstate["i"] = (i + 1) % NWS
                return i

            class WPlan:
                def __init__(self, loaders):
                    self.loaders = loaders
                    self.issued = 0
                    self.consumed = 0
                    self.slots = {}

                def next(self, tag=None):
                    while self.issued < len(self.loaders) and self.issued < self.consumed + NWS:
                        i = wslot_take()
                        t, fn = self.loaders[self.issued]
                        fn(i)
                        self.slots[self.issued] = (i, t)
                        self.issued += 1
                    i, t = self.slots.pop(self.consumed)
                    assert tag is None or t == tag, (t, tag)
                    self.consumed += 1
                    return i

            dsem_x = [P.dma_sem("x0"), P.dma_sem("x1")]
            dsem_o = [P.dma_sem("o0"), P.dma_sem("o1")]
            dsem_m = [P.dma_sem("m%d" % i) for i in range(4)]
            cnt = {"bk": 0, "m": 0, "t": 0}

            def rot(name, n):
                v = cnt[name]
                cnt[name] = (v + 1) % n
                return v % n

            for sq in range(NSEQ):
                tok0 = sq * S
                CV.reset()
                XT = CV.take([KC, S], BF16)
                off_after_xt = CV.off
                xin = [CV.take([D], F32) for _ in range(2)]
                xstf = [CV.take([KC * 128], F32) for _ in range(2)]
                xst = [a.rearrange("p (a b) -> p a b", b=128) for a in xstf]
                for tb in range(NTB):
                    b = tb % 2
                    DMA("sp", dsem_x[b], [], ["xin%d" % b], out=xin[b], in_=x_d[tok0 + tb * 128: tok0 + (tb + 1) * 128, :])
                    for q4 in (range(4) if KSUB >= 0 else []):
                        bi = rot("bk", 2)
                        bk = banks[bi]
                        for j in range(4):
                            dc = q4 * 4 + j
                            TR(["xin%d" % b, "ident_f"], [BK[bi]], out=bk[:, j * 128:(j + 1) * 128],
                               in_=xin[b][:, dc * 128:(dc + 1) * 128], identity=ident_f)
                        if KSUB >= 1:
                            for j in range(4):
                                ACT([BK[bi]], ["XT"], out=XT[:, q4 * 4 + j, tb * 128:(tb + 1) * 128], in_=bk[:, j * 128:(j + 1) * 128], func=AF.Identity)
                        if KSUB >= 2:
                            DVE("tensor_copy", [BK[bi]], ["xst%d" % b], out=xstf[b][:, q4 * 512:(q4 + 1) * 512], in_=bk[:, :])
                    if KSUB >= 3:
                        DMA("sp", dsem_o[b], ["xst%d" % b], ["XR"], out=XRv[:, :, tb * 128:(tb + 1) * 128], in_=xst[b])
                P.barrier()
                if KSUB < 4:
                    raise _Stop()

                for l in range(L):
                    cvl = CVT[:, l, :]
                    CV.reset(off_after_xt)
                    wsl = [CV.take([KC, 256], BF16) for _ in range(NWS)]

                    winv = w_in_d[l].rearrange("(c p) n -> p c n", p=128)

                    def mk_ld(col0, ncols, wsl=wsl, winv=winv):
                        def fn(i):
                            DMA("pool", wsem[i][0], [], [WK[i][0]] if ncols <= 128 else [WK[i][0], WK[i][1]],
                                out=wsl[i][:, :, 0:ncols], in_=winv[:, :, col0:col0 + ncols])
                        return (col0, fn)
                    plan1 = []
                    if DBG >= 1:
                        plan1 += [mk_ld(g * 128, 128) for g in range(4)]
                    if DBG >= 2:
                        for hp in range(2):
                            for h in (2 * hp, 2 * hp + 1):
                                plan1 += [mk_ld(1024 + h * 128, 128), mk_ld(1536 + h * 128, 128), mk_ld(512 + h * 128, 128)]
                            for h in (2 * hp, 2 * hp + 1):
                                plan1.append(mk_ld(2048 + h * 128, 128))
                    if DBG >= 3:
                        for h in range(8):
                            plan1 += [mk_ld(2560 + h * 128, 128), mk_ld(3584 + h * 128, 128)]
                            if h % 2 == 0:
                                plan1.append(mk_ld(4608 + h * 128, 256))
                    WP1 = WPlan(plan1)

                    def inproj_fm(col0, m, consume, wtile=None, nbk=4, wsl=wsl, XT=XT, WP1=WP1):
                        if wtile is None:
                            i = WP1.next(col0)
                            wt, wk = wsl[i][:, :, 0:m], [WK[i][0]]
                        else:
                            wt, wk = wtile
                        for tg in range(NTG):
                            bi = rot("bk", nbk)
                            bk = banks[bi]
                            for kc in range(KC):
                                MM(wk + ["XT"], [BK[bi]], bk[0:m, :], lhsT=wt[:, kc, :], rhs=XT[:, kc, tg * 512:(tg + 1) * 512],
                                   start=(kc == 0), stop=(kc == KC - 1))
                            consume(tg, bk[0:m, :], BK[bi])

                    tmp_off = CV.off
                    yst_i = {"i": 0}

                    CV.reset(tmp_off)
                    U = CV.take([16 + S], F32)
                    A = CV.take([16 + S], F32)
                    B = CV.take([16 + S], F32)
                    Dt = CV.take([S], BF16)
                    PW = CV.take([128], BF16)
                    YST = [CV.take([S], BF16) for _ in range(2)]
                    small = CV.take([32], F32)
                    for buf, nm in ((U, "U"), (A, "A"), (B, "B")):
                        POOL("memset", [], [nm], buf[:, 0:16], 0.0)
                    for g in (range(4) if DBG >= 1 else []):
                        w = 2 ** (g + 1)
                        DMA("pool", dsem_m[0], [], ["PW"], out=PW, in_=pool_w_d[l, g])

                        def cons_u(tg, ps, bkk, U=U):
                            ACT([bkk], ["U"], out=U[:, 16 + tg * 512:16 + (tg + 1) * 512], in_=ps, func=AF.Identity)
                        inproj_fm(g * 128, 128, cons_u)
                        src, sn = U, "U"
                        pp = [(A, "A"), (B, "B")]
                        for stp in range(g + 1):
                            sh = 2 ** stp
                            dst, dn = pp[stp % 2]
                            DVE("tensor_tensor", [sn], [dn], out=dst[:, 16:16 + S], in0=src[:, 16:16 + S],
                                in1=src[:, 16 - sh:16 - sh + S], op=ALU.add)
                            src, sn = dst, dn
                        DVE("scalar_tensor_tensor", [sn, "U"], ["Dt"], out=Dt, in0=src[:, 16:16 + S], scalar=1.0 / w,
                            in1=U[:, 16:16 + S], op0=ALU.mult, op1=ALU.subtract)
                        DVE("tensor_tensor", [sn, "invc"], ["small"], out=small[:, 0:w - 1], in0=src[:, 16:16 + w - 1],
                            in1=invc[:, 0:w - 1], op=ALU.mult)
                        DVE("tensor_tensor", ["small", "U"], ["Dt"], out=Dt[:, 0:w - 1], in0=small[:, 0:w - 1],
                            in1=U[:, 16:16 + w - 1], op=ALU.subtract)
                        yi = yst_i["i"]; yst_i["i"] ^= 1
                        for tg in range(NTG):
                            bi = 4 + rot("m", 2)
                            bk = banks[bi]
                            MM(["PW", "Dt"], [BK[bi]], bk[:, :], lhsT=PW, rhs=Dt[:, tg * 512:(tg + 1) * 512], start=True, stop=True)
                            ACT([BK[bi], "cvt"], ["YST%d" % yi], out=YST[yi][:, tg * 512:(tg + 1) * 512], in_=bk[:, :],
                                func=AF.Identity, scale=cvl[:, 108 + g:109 + g])
                        DMA("sp", dsem_o[yi], ["YST%d" % yi], ["YD"], out=YD[g * 128:(g + 1) * 128, :], in_=YST[yi])
                    P.barrier()

                    CV.reset(tmp_off)
                    T1 = CV.take([S], F32)
                    T2 = CV.take([S], F32)
                    T3 = CV.take([S], F32)
                    T4 = CV.take([S], F32)
                    SMK = CV.take([S], BF16)
                    SQB = CV.take([512], BF16)
                    RS = CV.take([512], F32)
                    YST = [CV.take([S], BF16) for _ in range(2)]
                    HB = []
                    for hi in range(2):
                        HB.append(dict(
                            QD=CV.take([S], BF16), KD=CV.take([S], BF16), V64=CV.take([NCK, 128], BF16), OE=CV.take([S], F32),
                            EBL=CV.take([NCK], F32), SS=CV.take([128], F32), SBF=CV.take([128], BF16),
                            KLC=[CV.take([CH], BF16) for _ in range(2)], ATM=[CV.take([64], BF16) for _ in range(2)],
                            KLT=[CV.take([128], BF16) for _ in range(2)],
                            bk=(4, 5, 6, 7) if hi == 0 else (0, 1, 2, 3), n="h%d" % hi))
                    POOL("memset", [], ["SMK"], SMK, 1.0)
                    POOL("memset", ["SMK"], ["SMK"], SMK.rearrange("p (c k) -> p c k", k=CH)[:, :, 0:1], 0.0)

                    def h_prep(h, hb, T1=T1, T2=T2, T3=T3, T4=T4, SMK=SMK, l=l):
                        n = hb["n"]
                        QD, KD, V64, EBL = hb["QD"], hb["KD"], hb["V64"], hb["EBL"]
                        lb_ap = LBT[:, l, h:h + 1]
                        oml_ap = OMLT[:, l, h:h + 1]

                        def cons_z(tg, ps, bkk):
                            sl = slice(tg * 512, (tg + 1) * 512)
                            DVE("tensor_scalar", [bkk], ["T2"], out=T2[:, sl], in0=ps, scalar1=-1.0, scalar2=80.0, op0=ALU.mult, op1=ALU.min)
                        inproj_fm(1024 + h * 128, 128, cons_z)
                        ACT(["T2"], ["T1"], out=T1, in_=T2, func=AF.Exp)
                        ACT(["T1", "lbt"], ["T2"], out=T2, in_=T1, func=AF.Ln, scale=lb_ap, bias=1.0)
                        ACT(["T1"], ["T3"], out=T3, in_=T1, func=AF.Ln, bias=1.0)
                        wi = WP1.next(1536 + h * 128)
                        for c in range(NCK):
                            bi = rot("bk", 4)
                            bk = banks[bi]
                            for kc in range(KC):
                                MM([WK[wi][0], "XT"], [BK[bi]], bk[0:64, 0:128], lhsT=XT[:, kc, c * 64:(c + 1) * 64], rhs=wsl[wi][:, kc, 0:128],
                                   start=(kc == 0), stop=(kc == KC - 1))
                            ACT([BK[bi]], ["V64" + n], out=V64[0:64, c, :], in_=bk[0:64, 0:128], func=AF.Identity)
                        DVE("tensor_tensor", ["T2", "T3"], ["T2"], out=T2, in0=T2, in1=T3, op=ALU.subtract)
                        DVE("tensor_scalar", ["T2"], ["T2"], out=T2, in0=T2, scalar1=0.0, scalar2=None, op0=ALU.min)
                        DVE("tensor_scalar", ["T1"], ["T3"], out=T3, in0=T1, scalar1=1.0, scalar2=None, op0=ALU.add)
                        DVE("reciprocal", ["T3"], ["T3"], out=T3, in_=T3)
                        DVE("scalar_tensor_tensor", ["T1", "T3", "omlt"], ["T1"], out=T1, in0=T1, scalar=oml_ap, in1=T3, op0=ALU.mult, op1=ALU.mult)
                        DVE("tensor_tensor_scan", ["SMK", "T2"], ["T3"], out=T3, data0=SMK, data1=T2, initial=0.0, op0=ALU.mult, op1=ALU.add)
                        ACT(["T3"], ["T4"], out=T4, in_=T3, func=AF.Exp)
                        ACT(["T3"], ["T2"], out=T2, in_=T3, func=AF.Exp, scale=-1.0)

                        def cons_q(tg, ps, bkk):
                            sl = slice(tg * 512, (tg + 1) * 512)
                            DVE("tensor_tensor", [bkk, "T4"], ["QD" + n], out=QD[:, sl], in0=ps, in1=T4[:, sl], op=ALU.mult)
                        inproj_fm(512 + h * 128, 128, cons_q)
                        DVE("tensor_tensor", ["T1", "T2"], ["KD" + n], out=KD, in0=T1, in1=T2, op=ALU.mult)
                        DVE("tensor_copy", ["T4"], ["EBL" + n], out=EBL, in_=T4.rearrange("p (c k) -> p c k", k=CH)[:, :, CH - 1])
                        DVE("memset", [], ["SS" + n], hb["SS"], 0.0)
                        DVE("memset", [], ["SBF" + n], hb["SBF"], 0.0)

                    def hA(hb, c):
                        n = hb["n"]
                        KD, QD, ATM, KLC, KLT, EBL = hb["KD"], hb["QD"], hb["ATM"], hb["KLC"], hb["KLT"], hb["EBL"]
                        b_att, b_o, b_tr, b_ds = hb["bk"]
                        ptb = banks[b_tr][:, :].bitcast(BF16)
                        cs = slice(c * CH, (c + 1) * CH)
                        ai = c % 2
                        MM(["KD" + n, "QD" + n], [BK[b_att]], banks[b_att][0:64, 0:64], lhsT=KD[:, cs], rhs=QD[:, cs], start=True, stop=True)
                        DVE("tensor_tensor", [BK[b_att], "triu"], ["ATM%d%s" % (ai, n)], out=ATM[ai][0:64, :], in0=banks[b_att][0:64, 0:64], in1=triu, op=ALU.mult)
                        DVE("tensor_scalar", ["KD" + n, "EBL" + n], ["KLC%d%s" % (ai, n)], out=KLC[ai], in0=KD[:, cs],
                            scalar1=EBL[:, c:c + 1], scalar2=None, op0=ALU.mult)
                        TR(["KLC%d%s" % (ai, n), "ident_b"], [BK[b_tr]], out=ptb[0:64, 0:128], in_=KLC[ai], identity=ident_b)
                        ACT([BK[b_tr]], ["KLT%d%s" % (ai, n)], out=KLT[ai][0:64, :], in_=ptb[0:64, 0:128], func=AF.Identity)

                    def hDS(hb, c):
                        n = hb["n"]
                        ai = c % 2
                        b_ds = hb["bk"][3]
                        MM(["KLT%d%s" % (ai, n), "V64" + n], [BK[b_ds]], banks[b_ds][:, 0:128], lhsT=hb["KLT"][ai][0:64, :], rhs=hb["V64"][0:64, c, :], start=True, stop=True)

                    def hO(hb, c):
                        n = hb["n"]
                        cs = slice(c * CH, (c + 1) * CH)
                        ai = c % 2
                        b_o = hb["bk"][1]
                        oc = slice((c % 8) * CH, (c % 8 + 1) * CH)
                        MM(["V64" + n, "ATM%d%s" % (ai, n)], [BK[b_o]], banks[b_o][:, oc], lhsT=hb["V64"][0:64, c, :], rhs=hb["ATM"][ai][0:64, :], start=True, stop=False)
                        MM(["SBF" + n, "QD" + n], [BK[b_o]], banks[b_o][:, oc], lhsT=hb["SBF"], rhs=hb["QD"][:, cs], start=False, stop=True)

                    def hUpd(hb, c):
                        n = hb["n"]
                        b_ds = hb["bk"][3]
                        DVE("scalar_tensor_tensor", ["SS" + n, "EBL" + n, BK[b_ds]], ["SS" + n], out=hb["SS"], in0=hb["SS"],
                            scalar=hb["EBL"][:, c:c + 1], in1=banks[b_ds][:, 0:128], op0=ALU.mult, op1=ALU.add)
                        ACT(["SS" + n], ["SBF" + n], out=hb["SBF"], in_=hb["SS"], func=AF.Identity)

                    def hEvac(hb, c):
                        n = hb["n"]
                        b_o = hb["bk"][1]
                        if c % 8 == 7:
                            sl = slice((c // 8) * 512, (c // 8 + 1) * 512)
                            ACT([BK[b_o]], ["OE" + n], out=hb["OE"][:, sl], in_=banks[b_o][:, :], func=AF.Identity)

                    def h_post(h, hb, T2=T2, SQB=SQB, RS=RS, YST=YST, cvl=cvl):
                        n = hb["n"]
                        OE = hb["OE"]
                        yi = yst_i["i"]; yst_i["i"] ^= 1

                        def cons_g(tg, ps, bkk):
                            sl = slice(tg * 512, (tg + 1) * 512)
                            ACT([bkk], ["T2"], out=T2[:, sl], in_=ps, func=AF.Silu)
                        inproj_fm(2048 + h * 128, 128, cons_g)
                        for tg in range(NTG):
                            sl = slice(tg * 512, (tg + 1) * 512)
                            ACT(["OE" + n], ["SQB"], out=SQB, in_=OE[:, sl], func=AF.Square)
                            MM(["SQB", "cstb"], [BK[4]], banks[4][:, :], lhsT=ones_b, rhs=SQB, start=True, stop=True)
                            ACT([BK[4]], ["RS"], out=RS, in_=banks[4][:, :], func=AF.Ln, scale=1.0 / 128, bias=RMS_EPS)
                            ACT(["RS"], ["RS"], out=RS, in_=RS, func=AF.Exp, scale=-0.5)
                            DVE("tensor_tensor", ["OE" + n, "RS"], ["OE" + n], out=OE[:, sl], in0=OE[:, sl], in1=RS, op=ALU.mult)
                            DVE("scalar_tensor_tensor", ["OE" + n, "T2", "cvt"], ["YST%d" % yi], out=YST[yi][:, sl], in0=OE[:, sl],
                                scalar=cvl[:, 112 + h:113 + h], in1=T2[:, sl], op0=ALU.mult, op1=ALU.mult)
                        DMA("sp", dsem_o[yi], ["YST%d" % yi], ["YD"], out=YD[512 + h * 128:512 + (h + 1) * 128, :], in_=YST[yi])

                    for hp in (range(2) if DBG >= 2 else []):
                        hs = (2 * hp, 2 * hp + 1)
                        for hi in range(2):
                            h_prep(hs[hi], HB[hi])
                        for hi in range(2):
                            hA(HB[hi], 0)
                            hDS(HB[hi], 0)
                        for c in range(NCK):
                            for hi in range(2):
                                hb = HB[hi]
                                if c + 1 < NCK:
                                    hA(hb, c + 1)
                                hO(hb, c)
                                hUpd(hb, c)
                                if c + 1 < NCK:
                                    hDS(hb, c + 1)
                                hEvac(hb, c)
                        for hi in range(2):
                            h_post(hs[hi], HB[hi])
                    P.barrier()

                    CV.reset(tmp_off)
                    W80 = CV.take([KC, 80], BF16)
                    W8 = CV.take([KC, 8], BF16)
                    CUMP = CV.take([S], F32)
                    R1 = CV.take([S], F32)
                    R2 = CV.take([S], F32)
                    MID = CV.take([S], BF16)
                    CST = CV.take([S], BF16)
                    LH = CV.take([S], BF16)
                    RH = CV.take([S], BF16)
                    QT = CV.take([S], BF16)
                    KT = CV.take([S], BF16)
                    VT = CV.take([NTB, 256], BF16)
                    PT = [CV.take([512], BF16) for _ in range(2)]
                    SM = [CV.take([128], F32) for _ in range(2)]
                    RC = CV.take([512], F32)
                    YST = [CV.take([S], BF16) for _ in range(2)]
                    ONESB = CV.take([S], BF16)
                    POOL("memset", [], ["ONESB"], ONESB, 1.0)
                    DVE("memset", [], ["W80"], W80, 0.0)
                    for off in (0, 8, 32, 40, 64, 72):
                        DMA("pool", dsem_m[1], ["W80"], ["W80d"], out=W80[:, :, off:off + 8], in_=winv[:, :, 5632:5640])
                    nfb_ap = NFBT[0:80, l:l + 1]

                    def cons_f(tg, ps, bkk, R1=R1, R2=R2, nfb_ap=nfb_ap):
                        sl = slice(tg * 512, (tg + 1) * 512)
                        ACT([bkk, "nfbt"], ["R1"], out=R1[0:80, sl], in_=ps, func=AF.Exp, scale=-1.0, bias=nfb_ap)
                        ACT(["R1"], ["R2"], out=R2[0:80, sl], in_=R1[0:80, sl], func=AF.Ln, bias=1.0)
                    inproj_fm(5632, 80, cons_f, wtile=(W80, ["W80", "W80d"]), nbk=2)
                    DVE("tensor_tensor_scan", ["R2", "ONESB"], ["CUMP"], out=CUMP[0:80, :],
                        data0=ONESB[0:80, :], data1=R2[0:80, :], initial=0.0, op0=ALU.mult, op1=ALU.add)
                    DVE("tensor_copy", ["CUMP"], ["CST"], out=CST[0:80, :], in_=CUMP[0:80, :])
                    DVE("tensor_tensor", ["CUMP", "CST"], ["R1"], out=R1[32:64, :], in0=CUMP[32:64, :], in1=CST[32:64, :], op=ALU.subtract)
                    DVE("tensor_tensor", ["CUMP", "CST"], ["R1"], out=R1[64:80, :], in0=CUMP[64:80, :], in1=CST[64:80, :], op=ALU.subtract)
                    DVE("tensor_copy", ["R1"], ["CST"], out=CST[32:48, :], in_=R1[32:48, :])
                    DVE("tensor_copy", ["R1"], ["MID"], out=MID[64:80, :], in_=R1[64:80, :])
                    DVE("tensor_tensor", ["R1", "MID"], ["R2"], out=R2[64:80, :], in0=R1[64:80, :], in1=MID[64:80, :], op=ALU.subtract)
                    DVE("tensor_copy", ["R2"], ["CST"], out=CST[64:80, :], in_=R2[64:80, :])
                    scale_q = 128.0 ** -0.5
                    for h in (range(8) if DBG >= 3 else []):
                        DVE("tensor_scalar", ["CST", "sel"], ["LH"], out=LH[0:80, :], in0=CST[0:80, :], scalar1=sel2[0:80, h:h + 1],
                            scalar2=sel1[0:80, h:h + 1], op0=ALU.mult, op1=ALU.add)
                        DVE("tensor_scalar", ["CST", "sel", "nsel1"], ["RH"], out=RH[0:80, :], in0=CST[0:80, :], scalar1=nsel1[0:80, h:h + 1],
                            scalar2=sel2[0:80, h:h + 1], op0=ALU.mult, op1=ALU.add)

                        def cons_qf(tg, ps, bkk, QT=QT):
                            ACT([bkk], ["QT"], out=QT[:, tg * 512:(tg + 1) * 512], in_=ps, func=AF.Identity, scale=scale_q)
                        inproj_fm(2560 + h * 128, 128, cons_qf, nbk=2)

                        def cons_kf(tg, ps, bkk, KT=KT):
                            ACT([bkk], ["KT"], out=KT[:, tg * 512:(tg + 1) * 512], in_=ps, func=AF.Identity)
                        inproj_fm(3584 + h * 128, 128, cons_kf, nbk=2)
                        if h % 2 == 0:
                            wi = WP1.next(4608 + h * 128)
                            for tb in range(NTB):
                                bi = rot("bk", 2)
                                bk = banks[bi]
                                for kc in range(KC):
                                    MM([WK[wi][0], WK[wi][1], "XT"], [BK[bi]], bk[:, 0:256], lhsT=XT[:, kc, tb * 128:(tb + 1) * 128],
                                       rhs=wsl[wi][:, kc, 0:256], start=(kc == 0), stop=(kc == KC - 1))
                                ACT([BK[bi]], ["VT"], out=VT[:, tb, :], in_=bk[:, 0:256], func=AF.Identity)
                        ho = (h % 2) * 128
                        yi = yst_i["i"]; yst_i["i"] ^= 1
                        ORP = [(6, 7), (2, 3)]

                        def stageA(G, j, KT=KT, QT=QT, LH=LH, RH=RH, SM=SM, PT=PT):
                            c0 = max(0, j - 4 * G)
                            cols = slice(c0 * 128, 512)
                            gcols = slice(G * 512 + c0 * 128, G * 512 + 512)
                            sbi = 4 + rot("m", 2)
                            sbk = banks[sbi]
                            pi = rot("t", 2)
                            MM(["KT", "QT"], [BK[sbi]], sbk[:, cols], lhsT=KT[:, j * 128:(j + 1) * 128], rhs=QT[:, gcols], start=True, stop=False)
                            MM(["LH", "RH"], [BK[sbi]], sbk[:, cols], lhsT=LH[0:80, j * 128:(j + 1) * 128], rhs=RH[0:80, gcols], start=False, stop=True)
                            if j >= 4 * G:
                                dsl = slice(c0 * 128, (c0 + 1) * 128)
                                DVE("tensor_tensor", [BK[sbi], "maskneg"], ["SM%d" % pi], out=SM[pi], in0=sbk[:, dsl], in1=maskneg, op=ALU.add)
                                ACT(["SM%d" % pi], ["PT%d" % pi], out=PT[pi][:, dsl], in_=SM[pi], func=AF.Exp)
                                if c0 < 3:
                                    rsl = slice((c0 + 1) * 128, 512)
                                    ACT([BK[sbi]], ["PT%d" % pi], out=PT[pi][:, rsl], in_=sbk[:, rsl], func=AF.Exp)
                            else:
                                ACT([BK[sbi]], ["PT%d" % pi], out=PT[pi], in_=sbk[:, :], func=AF.Exp)
                            return (G, j, cols, pi)

                        def stageB(G, j, cols, pi, VT=VT, PT=PT, RC=RC, YST=YST, ho=ho, yi=yi):
                            nkb = 4 * G + 4
                            obi, rbi = ORP[G % 2]
                            MM(["VT", "PT%d" % pi], [BK[obi]], banks[obi][:, cols], lhsT=VT[:, j, ho:ho + 128], rhs=PT[pi][:, cols],
                               start=(j == 0), stop=(j == nkb - 1))
                            MM(["cstb", "PT%d" % pi], [BK[rbi]], banks[rbi][:, cols], lhsT=ones_b, rhs=PT[pi][:, cols],
                               start=(j == 0), stop=(j == nkb - 1))
                            if j == nkb - 1:
                                DVE("reciprocal", [BK[rbi]], ["RC"], out=RC, in_=banks[rbi][:, :])
                                DVE("tensor_tensor", [BK[obi], "RC"], ["YST%d" % yi], out=YST[yi][:, G * 512:(G + 1) * 512], in0=banks[obi][:, :], in1=RC, op=ALU.mult)
                        prev = None
                        for G in range(NTG):
                            for j in range(4 * G + 4):
                                cur = stageA(G, j)
                                if prev is not None:
                                    stageB(*prev)
                                prev = cur
                        stageB(*prev)
                        DMA("sp", dsem_o[yi], ["YST%d" % yi], ["YD"], out=YD[1024 + h * 128:1024 + (h + 1) * 128, :], in_=YST[yi])
                    P.barrier()

                    def ln_a_piece(PRE, kf, nb, tmp, sb=(6, 7), xi=None):
                        XB, SQ, MU, M2, RSTD = tmp
                        if xi is None:
                            xi = nb % 2
                        ACT([kf(nb)], ["XB%d" % xi], out=XB[xi], in_=PRE[:, nb, :], func=AF.Identity)
                        ACT([kf(nb)], ["SQ%d" % xi], out=SQ[xi], in_=PRE[:, nb, :], func=AF.Square)
                        MM(["XB%d" % xi, "cstb"], [BK[sb[0]]], banks[sb[0]][:, :], lhsT=ones_b, rhs=XB[xi], start=(nb == 0), stop=(nb == KC - 1))
                        MM(["SQ%d" % xi, "cstb"], [BK[sb[1]]], banks[sb[1]][:, :], lhsT=ones_b, rhs=SQ[xi], start=(nb == 0), stop=(nb == KC - 1))

                    def ln_a_final(tmp, sb=(6, 7)):
                        XB, SQ, MU, M2, RSTD = tmp
                        DVE("tensor_scalar", [BK[sb[0]]], ["MU"], out=MU, in0=banks[sb[0]][:, :], scalar1=1.0 / D, scalar2=None, op0=ALU.mult)
                        DVE("tensor_tensor", ["MU"], ["M2"], out=M2, in0=MU, in1=MU, op=ALU.mult)
                        DVE("scalar_tensor_tensor", [BK[sb[1]], "M2"], ["M2"], out=M2, in0=banks[sb[1]][:, :], scalar=1.0 / D, in1=M2, op0=ALU.mult, op1=ALU.subtract)
                        ACT(["M2"], ["RSTD"], out=RSTD, in_=M2, func=AF.Ln, bias=LN_EPS)
                        ACT(["RSTD"], ["RSTD"], out=RSTD, in_=RSTD, func=AF.Exp, scale=-0.5)

                    def ln_b_piece(PRE, kf, nb, tcols, gcol, bcol, tmp, XT=XT, cvl=cvl):
                        XB, SQ, MU, M2, RSTD = tmp
                        k = kf(nb)
                        DVE("tensor_tensor", [k, "MU"], [k], out=PRE[:, nb, :], in0=PRE[:, nb, :], in1=MU, op=ALU.subtract)
                        DVE("tensor_tensor", [k, "RSTD"], [k], out=PRE[:, nb, :], in0=PRE[:, nb, :], in1=RSTD, op=ALU.mult)
                        ACT([k, "cvt"], ["XT"], out=XT[:, nb, tcols], in_=PRE[:, nb, :], func=AF.Identity,
                            scale=cvl[:, gcol + nb:gcol + nb + 1], bias=cvl[:, bcol + nb:bcol + nb + 1])
                        ACT([k, "cvt"], [k], out=PRE[:, nb, :], in_=PRE[:, nb, :], func=AF.Identity,
                            scale=cvl[:, gcol + nb:gcol + nb + 1], bias=cvl[:, bcol + nb:bcol + nb + 1])

                    CV.reset(off_after_xt)
                    wsl = [CV.take([KC, 256], BF16) for _ in range(NWS)]
                    YT = [CV.take([KC, 512], BF16) for _ in range(2)]
                    PREb = [CV.take([KC, 512], F32) for _ in range(2)]
                    lntmp = ([CV.take([512], BF16) for _ in range(2)], [CV.take([512], BF16) for _ in range(2)],
                             CV.take([512], F32), CV.take([512], F32), CV.take([512], F32))
                    woutv = w_out_d[l].rearrange("(c p) n -> p c n", p=128)

                    def mk_ld2(np2, wsl=wsl, woutv=woutv):
                        def fn(i):
                            DMA("pool", wsem[i][0], [], [WK[i][0], WK[i][1]], out=wsl[i][:, :, 0:256], in_=woutv[:, :, np2 * 256:(np2 + 1) * 256])
                        return (np2, fn)
                    WP2 = WPlan([mk_ld2(np2) for tg in (range(NTG) if DBG >= 4 else []) for np2 in range(KC // 2)])
                    def pk2(bb):
                        return lambda nb: "PRE%d_%d" % (bb, nb)
                    allk = lambda bb: ["PRE%d_%d" % (bb, nb) for nb in range(KC)]
                    prev_tg = None
                    for tg in (range(NTG) if DBG >= 4 else []):
                        tcols = slice(tg * 512, (tg + 1) * 512)
                        b = tg % 2
                        DMA("sp", dsem_x[b], ["YD"], ["YT%d" % b], out=YT[b], in_=YDv[:, :, tcols])
                        DMA("sp", dsem_m[2 + b], ["XR"], allk(b), out=PREb[b], in_=XRv[:, :, tcols])
                        for np2 in range(KC // 2):
                            wi = WP2.next(np2)
                            for hf in range(2):
                                nb = np2 * 2 + hf
                                bi = rot("bk", 4)
                                bk = banks[bi]
                                for kc in range(KC):
                                    MM([WK[wi][hf], "YT%d" % b], [BK[bi]], bk[:, :], lhsT=wsl[wi][:, kc, hf * 128:(hf + 1) * 128],
                                       rhs=YT[b][:, kc, :], start=(kc == 0), stop=(kc == KC - 1))
                                DVE("scalar_tensor_tensor", [pk2(b)(nb), BK[bi]], [pk2(b)(nb)], out=PREb[b][:, nb, :], in0=PREb[b][:, nb, :],
                                    scalar=ALPHA, in1=bk[:, :], op0=ALU.mult, op1=ALU.add)
                                if nb > 0:
                                    ln_a_piece(PREb[b], pk2(b), nb - 1, lntmp)
                                if prev_tg is not None:
                                    pb_, ptc = prev_tg
                                    ln_b_piece(PREb[pb_], pk2(pb_), nb, ptc, 0, 16, lntmp)
                        if prev_tg is not None:
                            pb_, ptc = prev_tg
                            DMA("sp", dsem_o[pb_], allk(pb_), ["XR"], out=XRv[:, :, ptc], in_=PREb[pb_])
                        ln_a_piece(PREb[b], pk2(b), KC - 1, lntmp)
                        ln_a_final(lntmp)
                        prev_tg = (b, tcols)
                    if prev_tg is not None:
                        pb_, ptc = prev_tg
                        for nb in range(KC):
                            ln_b_piece(PREb[pb_], pk2(pb_), nb, ptc, 0, 16, lntmp)
                        DMA("sp", dsem_o[pb_], allk(pb_), ["XR"], out=XRv[:, :, ptc], in_=PREb[pb_])
                    P.barrier()

                    CV.reset(off_after_xt)
                    wsl = [CV.take([KC, 256], BF16) for _ in range(NWS)]
                    ACC = CV.take([KC, TH], F32)
                    AT = CV.take([GF, TH], BF16)
                    GS = [CV.take([2 + 512], F32) for _ in range(2)]
                    CC = [CV.take([512], F32) for _ in range(2)]
                    EE = [CV.take([512], F32) for _ in range(2)]
                    lntmp = ([CV.take([512], BF16) for _ in range(2)], [CV.take([512], BF16) for _ in range(2)],
                             CV.take([512], F32), CV.take([512], F32), CV.take([512], F32))
                    wgv = w_gate_d[l].rearrange("(c p) n -> p c n", p=128)
                    wvv = w_val_d[l].rearrange("(c p) n -> p c n", p=128)
                    wdv = w_down_d[l].rearrange("(f p) n -> p f n", p=128)
                    DVE("memset", [], ["HALO"], HALO, 0.0)
                    NG = NFB // GF

                    def mk_gv(f, wsl=wsl, wgv=wgv, wvv=wvv):
                        def fn(i):
                            DMA("pool", wsem[i][0], [], [WK[i][0]], out=wsl[i][:, :, 0:128], in_=wgv[:, :, f * 128:(f + 1) * 128])
                            DMA("pool", wsem[i][1], [], [WK[i][1]], out=wsl[i][:, :, 128:256], in_=wvv[:, :, f * 128:(f + 1) * 128])
                        return (("gv", f), fn)

                    def mk_wd(f0, np2, wsl=wsl, wdv=wdv):
                        def fn(i):
                            DMA("pool", wsem[i][0], [], [WK[i][0], WK[i][1]], out=wsl[i][:, 0:GF, :], in_=wdv[:, f0:f0 + GF, np2 * 256:(np2 + 1) * 256])
                        return (("wd", f0, np2), fn)
                    plan3 = []
                    for hfq in (range(NHALF) if DBG >= 5 else []):
                        for g in range(NG):
                            plan3 += [mk_gv(g * GF + fi) for fi in range(GF)]
                            plan3 += [mk_wd(g * GF, np2) for np2 in range(KC // 2)]
                    WP3 = WPlan(plan3)
                    for hfq in (range(NHALF) if DBG >= 5 else []):
                        t0 = hfq * TH
                        ak = lambda nb, tg: "ACC_%d_%d" % (nb, tg)
                        akall = [ak(nb, tg) for nb in range(KC) for tg in range(NTGH)]
                        kfs = [(lambda tg: (lambda nb: ak(nb, tg)))(tg) for tg in range(NTGH)]
                        SBK = [(0, 1), (2, 3)]
                        DMA("sp", dsem_m[2], ["XR"], akall, out=ACC, in_=XRv[:, :, t0:t0 + TH])
                        for nb in range(KC):
                            ACT([], [ak(nb, tg) for tg in range(NTGH)], out=ACC[:, nb, :], in_=ACC[:, nb, :], func=AF.Identity, scale=ALPHA)
                        for g in range(NG):
                            for fi in range(GF):
                                f = g * GF + fi
                                wi = WP3.next(("gv", f))
                                cw = [cvl[:, 120 + j * 44 + f:121 + j * 44 + f] for j in range(3)]
                                cb = cvl[:, 64 + f:65 + f]
                                for tg in range(NTGH):
                                    tsl = slice(t0 + tg * 512, t0 + (tg + 1) * 512)
                                    si = rot("t", 2)
                                    bgi = si * 2
                                    bvi = si * 2 + 1
                                    for (bi, hf) in ((bgi, 0), (bvi, 1)):
                                        for kc in range(KC):
                                            MM([WK[wi][hf], "XT"], [BK[bi]], banks[bi][:, :], lhsT=wsl[wi][:, kc, hf * 128:(hf + 1) * 128],
                                               rhs=XT[:, kc, tsl], start=(kc == 0), stop=(kc == KC - 1))
                                    gs_, cc_, ee_ = GS[si], CC[si], EE[si]
                                    gk, ck, ek = "GS%d" % si, "CC%d" % si, "EE%d" % si
                                    ACT(["HALO"], [gk], out=gs_[:, 0:2], in_=HALO[:, f, :], func=AF.Identity)
                                    ACT([BK[bgi]], [gk], out=gs_[:, 2:514], in_=banks[bgi][:, :], func=AF.Identity)
                                    ACT([gk], ["HALO"], out=HALO[:, f, :], in_=gs_[:, 512:514], func=AF.Identity)
                                    ACT([gk, "cvt"], [ck], out=cc_, in_=gs_[:, 2:514], func=AF.Identity, scale=cw[2], bias=cb)
                                    DVE("scalar_tensor_tensor", [gk, ck, "cvt"], [ck], out=cc_, in0=gs_[:, 1:513], scalar=cw[1], in1=cc_, op0=ALU.mult, op1=ALU.add)
                                    DVE("scalar_tensor_tensor", [gk, ck, "cvt"], [ck], out=cc_, in0=gs_[:, 0:512], scalar=cw[0], in1=cc_, op0=ALU.mult, op1=ALU.add)
                                    ACT([ck], [ek], out=ee_, in_=cc_, func=AF.Silu)
                                    DVE("tensor_tensor", [ek, BK[bvi]], ["AT"], out=AT[:, fi, tg * 512:(tg + 1) * 512], in0=ee_, in1=banks[bvi][:, :], op=ALU.mult)
                            f0 = g * GF
                            for np2 in range(KC // 2):
                                wi = WP3.next(("wd", f0, np2))
                                for hf in range(2):
                                    nb = np2 * 2 + hf
                                    for tg in range(NTGH):
                                        bi = 4 + rot("m", 2)
                                        for fi in range(GF):
                                            MM([WK[wi][hf], "AT"], [BK[bi]], banks[bi][:, :], lhsT=wsl[wi][:, fi, hf * 128:(hf + 1) * 128],
                                               rhs=AT[:, fi, tg * 512:(tg + 1) * 512], start=(fi == 0), stop=(fi == GF - 1))
                                        DVE("tensor_tensor", [ak(nb, tg), BK[bi]], [ak(nb, tg)], out=ACC[:, nb, tg * 512:(tg + 1) * 512],
                                            in0=ACC[:, nb, tg * 512:(tg + 1) * 512], in1=banks[bi][:, :], op=ALU.add)
                                        if g == NG - 1 and nb > 0:
                                            ln_a_piece(ACC[:, :, tg * 512:(tg + 1) * 512], kfs[tg], nb - 1, lntmp, sb=SBK[tg], xi=tg % 2)
                        for tg in range(NTGH):
                            ln_a_piece(ACC[:, :, tg * 512:(tg + 1) * 512], kfs[tg], KC - 1, lntmp, sb=SBK[tg], xi=tg % 2)
                        for tg in range(NTGH):
                            tcols = slice(t0 + tg * 512, t0 + (tg + 1) * 512)
                            PREv = ACC[:, :, tg * 512:(tg + 1) * 512]
                            ln_a_final(lntmp, sb=SBK[tg])
                            for nb in range(KC):
                                ln_b_piece(PREv, kfs[tg], nb, tcols, 32, 48, lntmp)
                        DMA("sp", dsem_o[0], akall, ["XR"], out=XRv[:, :, t0:t0 + TH], in_=ACC)
                    P.barrier()

                CV.reset(off_after_xt)
                xo_in = [CV.take([KC, 128], F32) for _ in range(2)]
                xo_st = [CV.take([D], F32) for _ in range(2)]
                for tb in range(NTB):
                    b = tb % 2
                    DMA("sp", dsem_x[b], ["XR"], ["xoin%d" % b], out=xo_in[b], in_=XRv[:, :, tb * 128:(tb + 1) * 128])
                    for q4 in range(4):
                        bi = rot("bk", 2)
                        bk = banks[bi]
                        for j in range(4):
                            dc = q4 * 4 + j
                            TR(["xoin%d" % b, "ident_f"], [BK[bi]], out=bk[:, j * 128:(j + 1) * 128], in_=xo_in[b][:, dc, :], identity=ident_f)
                        DVE("tensor_copy", [BK[bi]], ["xost%d" % b], out=xo_st[b][:, q4 * 512:(q4 + 1) * 512], in_=bk[:, :])
                    DMA("sp", dsem_o[b], ["xost%d" % b], ["OUT"], out=out_d[tok0 + tb * 128: tok0 + (tb + 1) * 128, :], in_=xo_st[b])
                P.barrier()
        except _Stop:
            pass
        P.barrier()
        P.emit()
    return nc


_NC_CACHE = {}

WNAMES = ["w_in", "fox_f_bias", "pool_w", "pool_scale", "hgrn_lb_logits", "hgrn_norm_g", "w_out", "ln1_g", "ln1_b",
          "w_gate", "w_val", "conv_w", "conv_b", "w_down", "ln2_g", "ln2_b"]


def kernel(**inputs):
    x = np.ascontiguousarray(inputs["x"], dtype=np.float32)
    B, S, Dm = x.shape
    L = inputs["w_in"].shape[0]
    ncores = 8
    nseq = B // ncores
    key = (S, nseq, L)
    if key not in _NC_CACHE:
        _NC_CACHE[key] = build(S, nseq, L)
    nc = _NC_CACHE[key]
    ws = {k: np.ascontiguousarray(inputs[k], dtype=np.float32) for k in WNAMES}
    in_maps = []
    for c in range(ncores):
        m = {"x": x[c * nseq:(c + 1) * nseq].reshape(nseq * S, Dm)}
        m.update(ws)
        in_maps.append(m)
    res = run_bass_kernel_spmd(nc, in_maps, core_ids=list(range(ncores)))
    outs = [np.asarray(r["out"]).reshape(nseq, S, Dm) for r in res.results]
    return np.concatenate(outs, axis=0).astype(np.float32)
```

```python
import contextlib
import os
import numpy as np
import concourse.bass as bass
import concourse.mybir as mybir
from concourse.bass_utils import run_bass_kernel_spmd

F32 = mybir.dt.float32
BF16 = mybir.dt.bfloat16
AF = mybir.ActivationFunctionType
ALU = mybir.AluOpType

SAME_ENGINE_SYNC = True

D = 2048
KC = 16
DFF = 5632
NFB = 44
INC = 5640
ALPHA = 8.0 ** 0.25
LN_EPS = 1e-5
RMS_EPS = 1e-6
CH = 64


class Prog:
    ENGS = ("pe", "act", "dve", "pool", "sp")

    def __init__(self, nc):
        self.nc = nc
        self.q = {e: [] for e in self.ENGS}
        self.last_write = {}
        self.reads_since = {}
        self.seen = {e: {} for e in self.ENGS}
        self.needed = {e: set() for e in self.ENGS}
        self.dma_cnt = {}
        self.dma_sems = []

    def dma_sem(self, name):
        self.dma_cnt[name] = 0
        self.dma_sems.append(name)
        return name

    def _deps(self, reads, writes):
        deps = []
        for k in reads:
            ev = self.last_write.get(k)
            if ev is not None:
                deps.append(ev)
        for k in writes:
            ev = self.last_write.get(k)
            if ev is not None:
                deps.append(ev)
            deps.extend(self.reads_since.get(k, ()))
        return deps

    def _commit(self, ev, reads, writes):
        for k in reads:
            self.reads_since.setdefault(k, []).append(ev)
        for k in writes:
            self.last_write[k] = ev
            self.reads_since[k] = []

    def _waits(self, eng, deps):
        waits = {}
        seen = self.seen[eng]
        for (sk, idx) in deps:
            if sk == eng and (eng == "pe" or not SAME_ENGINE_SYNC):
                continue
            if seen.get(sk, -1) >= idx:
                continue
            if waits.get(sk, -1) < idx:
                waits[sk] = idx
        for sk, idx in waits.items():
            seen[sk] = idx
            if sk in self.needed:
                self.needed[sk].add(idx)
        return waits

    def op(self, eng, fn, reads=(), writes=()):
        writes = list(writes) + [k for k in reads if k.startswith("pb") and k not in writes]
        deps = self._deps(reads, writes)
        waits = self._waits(eng, deps)
        idx = len(self.q[eng])
        self.q[eng].append(("op", fn, waits, None))
        ev = (eng, idx)
        self._commit(ev, reads, writes)
        return ev

    def dma(self, queue, sem, fn, reads=(), writes=()):
        deps = self._deps(reads, writes)
        waits = self._waits(queue, deps)
        self.dma_cnt[sem] += 16
        ev = (sem, self.dma_cnt[sem])
        self.q[queue].append(("dma", fn, waits, sem))
        self._commit(ev, reads, writes)
        return ev

    def barrier(self):
        deps = []
        for e in self.ENGS:
            for i in range(len(self.q[e]) - 1, -1, -1):
                if self.q[e][i][0] == "op":
                    deps.append((e, i))
                    break
        for s in self.dma_sems:
            if self.dma_cnt[s] > 0:
                deps.append((s, self.dma_cnt[s]))
        for e in self.ENGS:
            waits = self._waits(e, [d for d in deps if d[0] != e])
            self.q[e].append(("wait", None, waits, None))

    def emit(self):
        nc = self.nc
        with contextlib.ExitStack() as st:
            sems = {}
            for e in self.ENGS:
                sems[e] = st.enter_context(nc.semaphore("s_" + e))
            for d in self.dma_sems:
                sems[d] = st.enter_context(nc.semaphore("d_" + d))
            val = {}
            for e in self.ENGS:
                c = 0
                v = {}
                for i in range(len(self.q[e])):
                    if i in self.needed[e]:
                        c += 1
                        v[i] = c
                val[e] = v
            block = st.enter_context(nc.Block())

            def run(e, engh):
                needed = self.needed[e]
                for i, (kind, fn, waits, dsem) in enumerate(self.q[e]):
                    for sk, idx in waits.items():
                        if sk in val:
                            engh.wait_ge(sems[sk], val[sk][idx])
                        else:
                            engh.wait_ge(sems[sk], idx)
                    if kind == "wait":
                        continue
                    mname, margs, mkw = fn
                    bi = getattr(engh, mname)(*margs, **mkw)
                    if kind == "dma":
                        bi.then_inc(sems[dsem], 16)
                    elif i in needed:
                        bi.then_inc(sems[e], 1)

            @block.tensor
            def _(eh):
                run("pe", eh)

            @block.scalar
            def _(eh):
                run("act", eh)

            @block.vector
            def _(eh):
                run("dve", eh)

            @block.gpsimd
            def _(eh):
                run("pool", eh)

            @block.sync
            def _(eh):
                run("sp", eh)


class _Stop(Exception):
    pass


def build(S, NSEQ, L, GF=11):
    assert S % 512 == 0
    NTG = S // 512
    NTB = S // 128
    NCK = S // CH
    TH = min(1024, S)
    NHALF = S // TH
    NTGH = TH // 512
    nc = bass.Bass("TRN2", target_bir_lowering=False)
    P = Prog(nc)
    DBG = int(os.environ.get("KDBG", "9"))
    KSUB = int(os.environ.get("KSUB", "9"))

    def ACT(reads, writes, **kw):
        P.op("act", ("activation", (), kw), reads, writes)

    def MM(reads, writes, out, **kw):
        P.op("pe", ("matmul", (out,), kw), reads, writes)

    def TR(reads, writes, **kw):
        P.op("pe", ("transpose", (), kw), reads, writes)

    def DVE(m, reads, writes, *a, **kw):
        P.op("dve", (m, a, kw), reads, writes)

    def POOL(m, reads, writes, *a, **kw):
        P.op("pool", (m, a, kw), reads, writes)

    def DMA(queue, sem, reads, writes, **kw):
        P.dma(queue, sem, ("dma_start", (), kw), reads, writes)

    def din(name, shape):
        return nc.dram_tensor(name, shape, F32, kind="ExternalInput").ap()

    x_d = din("x", [NSEQ * S, D])
    w_in_d = din("w_in", [L, D, INC])
    fbias_d = din("fox_f_bias", [L, 8])
    pool_w_d = din("pool_w", [L, 4, 128, 128])
    pool_scale_d = din("pool_scale", [L, 512])
    lbl_d = din("hgrn_lb_logits", [L, 512])
    normg_d = din("hgrn_norm_g", [L, 512])
    w_out_d = din("w_out", [L, D, D])
    ln1g_d = din("ln1_g", [L, D])
    ln1b_d = din("ln1_b", [L, D])
    w_gate_d = din("w_gate", [L, D, DFF])
    w_val_d = din("w_val", [L, D, DFF])
    conv_w_d = din("conv_w", [L, 3, DFF])
    conv_b_d = din("conv_b", [L, DFF])
    w_down_d = din("w_down", [L, DFF, D])
    ln2g_d = din("ln2_g", [L, D])
    ln2b_d = din("ln2_b", [L, D])
    out_d = nc.dram_tensor("out", [NSEQ * S, D], F32, kind="ExternalOutput").ap()
    XR = nc.dram_tensor("xr_scr", [D, S], F32).ap()
    YD = nc.dram_tensor("y_scr", [D, S], BF16).ap()
    XRv = XR.rearrange("(c p) t -> p c t", p=128)
    YDv = YD.rearrange("(c p) t -> p c t", p=128)

    st = contextlib.ExitStack()
    with st:
        def sb(name, shape, dt):
            return st.enter_context(nc.sbuf_tensor(name, shape, dt))

        ARENA_B = 198 * 1024
        arena = sb("arena", [128, ARENA_B // 2], BF16)
        NCST = 128 * 4 + 16 + 8 * 4 + L * 8 + L + 2 + L * 256 + 2 * 128
        cst = sb("cst", [128, NCST], F32)
        cstb = sb("cstb", [128, 128 + 128 + 64], BF16)
        banks = [st.enter_context(nc.psum_tensor("pb%d" % i, [128, 512], F32)) for i in range(8)]
        BK = ["pb%d" % i for i in range(8)]

        class Carver:
            def __init__(self):
                self.off = 0

            def reset(self, off=0):
                self.off = off

            def take(self, free_shape, dt):
                n = int(np.prod(free_shape))
                esz = 4 if dt == F32 else 2
                nb = n * esz
                nbp = (nb + 63) // 64 * 64
                assert self.off + nbp <= ARENA_B, (self.off, nbp, ARENA_B)
                a = arena[:, self.off // 2: self.off // 2 + nb // 2]
                self.off += nbp
                if dt == F32:
                    a = a.bitcast(F32)
                if len(free_shape) == 2:
                    a = a.rearrange("p (a b) -> p a b", b=free_shape[1])
                elif len(free_shape) == 3:
                    a = a.rearrange("p (a b c) -> p a b c", b=free_shape[1], c=free_shape[2])
                return a

        CV = Carver()
        try:

            o = 0
            ident_f = cst[:, o:o + 128]; o += 128
            maskneg = cst[:, o:o + 128]; o += 128
            seltmp = cst[:, o:o + 128]; o += 128
            o += 128
            invc = cst[:, o:o + 16]; o += 16
            sel1 = cst[:, o:o + 8]; o += 8
            sel2 = cst[:, o:o + 8]; o += 8
            nsel1 = cst[:, o:o + 8]; o += 8
            o += 8
            LBT = cst[:, o:o + L * 4].rearrange("p (l h) -> p l h", h=4); o += L * 4
            OMLT = cst[:, o:o + L * 4].rearrange("p (l h) -> p l h", h=4); o += L * 4
            NFBT = cst[:, o:o + L]; o += L
            ones_col = cst[:, o:o + 1]; o += 1
            o += 1
            CVT = cst[:, o:o + L * 256].rearrange("p (l c) -> p l c", c=256); o += L * 256
            HALO = cst[:, o:o + 128].rearrange("p (f j) -> p f j", j=2)[:, 0:NFB, :]; o += 128
            SST = cst[:, o:o + 128]; o += 128
            assert o <= NCST
            ident_b = cstb[:, 0:128]
            ones_b = cstb[:, 128:256]
            triu = cstb[0:64, 256:320]

            POOL("memset", [], ["cst"], cst[:], 0.0)
            POOL("memset", ["cst"], ["ident_f"], ident_f, 1.0)
            POOL("affine_select", ["ident_f"], ["ident_f"], out=ident_f, in_=ident_f, pattern=[[1, 128]],
                 compare_op=ALU.is_equal, fill=0.0, base=0, channel_multiplier=-1)
            POOL("memset", [], ["cstb"], cstb[:], 1.0)
            POOL("affine_select", ["cstb"], ["ident_b"], out=ident_b, in_=ident_b, pattern=[[1, 128]],
                 compare_op=ALU.is_equal, fill=0.0, base=0, channel_multiplier=-1)
            POOL("affine_select", ["cstb"], ["triu"], out=triu, in_=triu, pattern=[[1, 64]],
                 compare_op=ALU.is_ge, fill=0.0, base=0, channel_multiplier=-1)
            POOL("affine_select", ["cst"], ["maskneg"], out=maskneg, in_=maskneg, pattern=[[1, 128]],
                 compare_op=ALU.is_ge, fill=-30000.0, base=0, channel_multiplier=-1)
            POOL("memset", ["cst"], ["ones_col"], ones_col, 1.0)
            for t in range(16):
                POOL("memset", ["cst"], ["invc"], invc[:, t:t + 1], 1.0 / (t + 1))
            for si, (dst, offs) in enumerate(((sel1, (0, 32, 64)), (sel2, (8, 40, 72)))):
                for oi, off in enumerate(offs):
                    tmp = seltmp[:, (si * 3 + oi) * 8:(si * 3 + oi) * 8 + 8]
                    POOL("memset", ["cst"], ["seltmp"], tmp, 1.0)
                    POOL("affine_select", ["seltmp"], ["seltmp"], out=tmp, in_=tmp, pattern=[[-1, 8]],
                         compare_op=ALU.is_equal, fill=0.0, base=-off, channel_multiplier=1)
                    POOL("tensor_tensor", ["seltmp", "cst"], ["sel"], out=dst, in0=dst, in1=tmp, op=ALU.add)
            POOL("tensor_scalar", ["sel"], ["nsel1"], out=nsel1, in0=sel1, scalar1=-1.0, scalar2=None, op0=ALU.mult)

            if DBG == -3:
                raise _Stop()
            CV.reset()
            stgf = CV.take([L * 256], F32)
            dsem_c = P.dma_sem("c")
            DVE("memset", [], ["stg"], stgf, 0.0)
            for l in range(L):
                s0 = stgf[:, (l * 2) * 128:(l * 2 + 1) * 128]
                s1 = stgf[:, (l * 2 + 1) * 128:(l * 2 + 2) * 128]
                rows = [(ln1g_d[l], 0, 16), (ln1b_d[l], 16, 16), (ln2g_d[l], 32, 16), (ln2b_d[l], 48, 16),
                        (conv_b_d[l], 64, 44), (pool_scale_d[l], 108, 4), (normg_d[l], 112, 4), (lbl_d[l], 116, 4)]
                for (src, r0, nr) in rows:
                    DMA("sp", dsem_c, ["stg"], ["stgd"], out=s0[r0:r0 + nr, :], in_=src.rearrange("(c p) -> c p", p=128))
                cwv = conv_w_d[l].rearrange("j (c p) -> (j c) p", p=128)
                DMA("sp", dsem_c, ["stg"], ["stgd"], out=s0[120:128, :], in_=cwv[0:8, :])
                DMA("sp", dsem_c, ["stg"], ["stgd"], out=s1[0:124, :], in_=cwv[8:132, :])
                for off in (0, 8, 32, 40, 64, 72):
                    DMA("sp", dsem_c, ["cst"], ["nfbt"], out=NFBT[off:off + 8, l:l + 1],
                        in_=fbias_d[l].rearrange("(h o) -> h o", o=1))
            P.barrier()
            for l in range(L):
                for hh in range(2):
                    bk = banks[hh]
                    TR(["stgd", "stg", "ident_f"], [BK[hh]], out=bk[:, 0:128], in_=stgf[:, (l * 2 + hh) * 128:(l * 2 + hh + 1) * 128], identity=ident_f)
                    DVE("tensor_copy", [BK[hh], "cst"], ["cvt"], out=CVT[:, l, hh * 128:(hh + 1) * 128], in_=bk[:, 0:128])
            DVE("tensor_scalar", ["nfbt"], ["nfbt"], out=NFBT, in0=NFBT, scalar1=-1.0, scalar2=None, op0=ALU.mult)
            if DBG == -2:
                raise _Stop()
            EL = CV.take([L, 4], F32)
            TOT = CV.take([4], F32)
            for l in range(L):
                ACT(["cvt"], ["el"], out=EL[:, l, :], in_=CVT[:, l, 116:120], func=AF.Exp)
            DVE("tensor_copy", ["el"], ["tot"], out=TOT, in_=EL[:, 0, :])
            for l in range(1, L):
                DVE("tensor_tensor", ["el", "tot"], ["tot"], out=TOT, in0=TOT, in1=EL[:, l, :], op=ALU.add)
            DVE("reciprocal", ["tot"], ["tot"], out=TOT, in_=TOT)
            for l in range(1, L):
                DVE("tensor_tensor", ["el", "tot"], ["el"], out=EL[:, l, :], in0=EL[:, l, :], in1=TOT, op=ALU.mult)
                DVE("tensor_tensor", ["el", "lbt", "cst"], ["lbt"], out=LBT[:, l, :], in0=LBT[:, l - 1, :], in1=EL[:, l, :], op=ALU.add)
            DVE("tensor_scalar", ["lbt", "cst"], ["omlt"], out=OMLT, in0=LBT, scalar1=-1.0, scalar2=1.0, op0=ALU.mult, op1=ALU.add)
            P.barrier()

            if DBG == -1:
                raise _Stop()
            NWS = 3
            wsem = [[P.dma_sem("w%d_%d" % (i, j)) for j in range(2)] for i in range(NWS)]
            WK = [["ws%d_%d" % (i, j) for j in range(2)] for i in range(NWS)]
            wstate = {"i": 0}

            def wslot_take():
                i = wstate["i"]
                wstate["i"] = (i + 1) % NWS
                return i

            class WPlan:
                def __init__(self, loaders):
                    self.loaders = loaders
                    self.issued = 0
                    self.consumed = 0
                    self.slots = {}

                def next(self, tag=None):
                    while self.issued < len(self.loaders) and self.issued < self.consumed + NWS:
                        i = wslot_take()
                        t, fn = self.loaders[self.issued]
                        fn(i)
                        self.slots[self.issued] = (i, t)
                        self.issued += 1
                    i, t = self.slots.pop(self.consumed)
                    assert tag is None or t == tag, (t, tag)
                    self.consumed += 1
                    return i

            dsem_x = [P.dma_sem("x0"), P.dma_sem("x1")]
            dsem_o = [P.dma_sem("o0"), P.dma_sem("o1")]
            dsem_m = [P.dma_sem("m%d" % i) for i in range(4)]
            ldp = [P.dma_sem("lp%d" % i) for i in range(KC)]
            wrp = [P.dma_sem("wp%d" % i) for i in range(KC)]
            cnt = {"bk": 0, "m": 0, "t": 0}

            def rot(name, n):
                v = cnt[name]
                cnt[name] = (v + 1) % n
                return v % n

            for sq in range(NSEQ):
                tok0 = sq * S
                CV.reset()
                XT = CV.take([KC, S], BF16)
                off_after_xt = CV.off
                xin = [CV.take([D], F32) for _ in range(2)]
                xstf = [CV.take([KC * 128], F32) for _ in range(2)]
                xst = [a.rearrange("p (a b) -> p a b", b=128) for a in xstf]
                for tb in range(NTB):
                    b = tb % 2
                    DMA("sp", dsem_x[b], [], ["xin%d" % b], out=xin[b], in_=x_d[tok0 + tb * 128: tok0 + (tb + 1) * 128, :])
                    for q4 in (range(4) if KSUB >= 0 else []):
                        bi = rot("bk", 2)
                        bk = banks[bi]
                        for j in range(4):
                            dc = q4 * 4 + j
                            TR(["xin%d" % b, "ident_f"], [BK[bi]], out=bk[:, j * 128:(j + 1) * 128],
                               in_=xin[b][:, dc * 128:(dc + 1) * 128], identity=ident_f)
                        if KSUB >= 1:
                            for j in range(4):
                                ACT([BK[bi]], ["XT"], out=XT[:, q4 * 4 + j, tb * 128:(tb + 1) * 128], in_=bk[:, j * 128:(j + 1) * 128], func=AF.Identity)
                        if KSUB >= 2:
                            DVE("tensor_copy", [BK[bi]], ["xst%d" % b], out=xstf[b][:, q4 * 512:(q4 + 1) * 512], in_=bk[:, :])
                    if KSUB >= 3:
                        DMA("sp", dsem_o[b], ["xst%d" % b], ["XR"], out=XRv[:, :, tb * 128:(tb + 1) * 128], in_=xst[b])
                P.barrier()
                if KSUB < 4:
                    raise _Stop()

                for l in range(L):
                    cvl = CVT[:, l, :]
                    CV.reset(off_after_xt)
                    wsl = [CV.take([KC, 256], BF16) for _ in range(NWS)]

                    winv = w_in_d[l].rearrange("(c p) n -> p c n", p=128)

                    def mk_ld(col0, ncols, wsl=wsl, winv=winv):
                        def fn(i):
                            DMA("pool", wsem[i][0], [], [WK[i][0]] if ncols <= 128 else [WK[i][0], WK[i][1]],
                                out=wsl[i][:, :, 0:ncols], in_=winv[:, :, col0:col0 + ncols])
                        return (col0, fn)
                    plan1 = []
                    if DBG >= 1:
                        plan1 += [mk_ld(g * 128, 128) for g in range(4)]
                    if DBG >= 2:
                        for hp in range(2):
                            for h in (2 * hp, 2 * hp + 1):
                                plan1 += [mk_ld(1024 + h * 128, 128), mk_ld(1536 + h * 128, 128), mk_ld(512 + h * 128, 128)]
                            for h in (2 * hp, 2 * hp + 1):
                                plan1.append(mk_ld(2048 + h * 128, 128))
                    if DBG >= 3:
                        for h in range(8):
                            plan1 += [mk_ld(2560 + h * 128, 128), mk_ld(3584 + h * 128, 128)]
                            if h % 2 == 0:
                                plan1.append(mk_ld(4608 + h * 128, 256))
                    WP1 = WPlan(plan1)

                    def inproj_fm(col0, m, consume, wtile=None, nbk=4, wsl=wsl, XT=XT, WP1=WP1):
                        if wtile is None:
                            i = WP1.next(col0)
                            wt, wk = wsl[i][:, :, 0:m], [WK[i][0]]
                        else:
                            wt, wk = wtile
                        for tg in range(NTG):
                            bi = rot("bk", nbk)
                            bk = banks[bi]
                            for kc in range(KC):
                                MM(wk + ["XT"], [BK[bi]], bk[0:m, :], lhsT=wt[:, kc, :], rhs=XT[:, kc, tg * 512:(tg + 1) * 512],
                                   start=(kc == 0), stop=(kc == KC - 1))
                            consume(tg, bk[0:m, :], BK[bi])

                    tmp_off = CV.off
                    yst_i = {"i": 0}

                    CV.reset(tmp_off)
                    U = CV.take([16 + S], F32)
                    A = CV.take([16 + S], F32)
                    B = CV.take([16 + S], F32)
                    Dt = CV.take([S], BF16)
                    PW = CV.take([128], BF16)
                    YST = [CV.take([S], BF16) for _ in range(2)]
                    small = CV.take([32], F32)
                    for buf, nm in ((U, "U"), (A, "A"), (B, "B")):
                        POOL("memset", [], [nm], buf[:, 0:16], 0.0)
                    for g in (range(4) if DBG >= 1 else []):
                        w = 2 ** (g + 1)
                        DMA("pool", dsem_m[0], [], ["PW"], out=PW, in_=pool_w_d[l, g])

                        def cons_u(tg, ps, bkk, U=U):
                            ACT([bkk], ["U"], out=U[:, 16 + tg * 512:16 + (tg + 1) * 512], in_=ps, func=AF.Identity)
                        inproj_fm(g * 128, 128, cons_u)
                        src, sn = U, "U"
                        pp = [(A, "A"), (B, "B")]
                        for stp in range(g + 1):
                            sh = 2 ** stp
                            dst, dn = pp[stp % 2]
                            DVE("tensor_tensor", [sn], [dn], out=dst[:, 16:16 + S], in0=src[:, 16:16 + S],
                                in1=src[:, 16 - sh:16 - sh + S], op=ALU.add)
                            src, sn = dst, dn
                        DVE("scalar_tensor_tensor", [sn, "U"], ["Dt"], out=Dt, in0=src[:, 16:16 + S], scalar=1.0 / w,
                            in1=U[:, 16:16 + S], op0=ALU.mult, op1=ALU.subtract)
                        DVE("tensor_tensor", [sn, "invc"], ["small"], out=small[:, 0:w - 1], in0=src[:, 16:16 + w - 1],
                            in1=invc[:, 0:w - 1], op=ALU.mult)
                        DVE("tensor_tensor", ["small", "U"], ["Dt"], out=Dt[:, 0:w - 1], in0=small[:, 0:w - 1],
                            in1=U[:, 16:16 + w - 1], op=ALU.subtract)
                        yi = yst_i["i"]; yst_i["i"] ^= 1
                        for tg in range(NTG):
                            bi = 4 + rot("m", 2)
                            bk = banks[bi]
                            MM(["PW", "Dt"], [BK[bi]], bk[:, :], lhsT=PW, rhs=Dt[:, tg * 512:(tg + 1) * 512], start=True, stop=True)
                            ACT([BK[bi], "cvt"], ["YST%d" % yi], out=YST[yi][:, tg * 512:(tg + 1) * 512], in_=bk[:, :],
                                func=AF.Identity, scale=cvl[:, 108 + g:109 + g])
                        DMA("sp", dsem_o[yi], ["YST%d" % yi], ["YD"], out=YD[g * 128:(g + 1) * 128, :], in_=YST[yi])
                    P.barrier()

                    CV.reset(tmp_off)
                    T1 = CV.take([S], F32)
                    T2 = CV.take([S], F32)
                    T3 = CV.take([S], F32)
                    T4 = CV.take([S], F32)
                    SMK = CV.take([S], BF16)
                    SQB = CV.take([512], BF16)
                    RS = CV.take([512], F32)
                    YST = [CV.take([S], BF16) for _ in range(2)]
                    HB = []
                    for hi in range(2):
                        HB.append(dict(
                            QD=CV.take([S], BF16), KD=CV.take([S], BF16), V64=CV.take([NCK, 128], BF16), OE=CV.take([S], F32),
                            EBL=CV.take([NCK], F32), SS=CV.take([128], F32), SBF=CV.take([128], BF16),
                            KLC=[CV.take([CH], BF16) for _ in range(2)], ATM=[CV.take([64], BF16) for _ in range(2)],
                            KLT=[CV.take([128], BF16) for _ in range(2)],
                            bk=(4, 5, 6, 7) if hi == 0 else (0, 1, 2, 3), n="h%d" % hi))
                    POOL("memset", [], ["SMK"], SMK, 1.0)
                    POOL("memset", ["SMK"], ["SMK"], SMK.rearrange("p (c k) -> p c k", k=CH)[:, :, 0:1], 0.0)

                    def h_prep(h, hb, T1=T1, T2=T2, T3=T3, T4=T4, SMK=SMK, l=l):
                        n = hb["n"]
                        QD, KD, V64, EBL = hb["QD"], hb["KD"], hb["V64"], hb["EBL"]
                        lb_ap = LBT[:, l, h:h + 1]
                        oml_ap = OMLT[:, l, h:h + 1]

                        def cons_z(tg, ps, bkk):
                            sl = slice(tg * 512, (tg + 1) * 512)
                            DVE("tensor_scalar", [bkk], ["T2"], out=T2[:, sl], in0=ps, scalar1=-1.0, scalar2=80.0, op0=ALU.mult, op1=ALU.min)
                        inproj_fm(1024 + h * 128, 128, cons_z)
                        ACT(["T2"], ["T1"], out=T1, in_=T2, func=AF.Exp)
                        ACT(["T1", "lbt"], ["T2"], out=T2, in_=T1, func=AF.Ln, scale=lb_ap, bias=1.0)
                        ACT(["T1"], ["T3"], out=T3, in_=T1, func=AF.Ln, bias=1.0)
                        wi = WP1.next(1536 + h * 128)
                        for c in range(NCK):
                            bi = rot("bk", 4)
                            bk = banks[bi]
                            for kc in range(KC):
                                MM([WK[wi][0], "XT"], [BK[bi]], bk[0:64, 0:128], lhsT=XT[:, kc, c * 64:(c + 1) * 64], rhs=wsl[wi][:, kc, 0:128],
                                   start=(kc == 0), stop=(kc == KC - 1))
                            ACT([BK[bi]], ["V64" + n], out=V64[0:64, c, :], in_=bk[0:64, 0:128], func=AF.Identity)
                        DVE("tensor_tensor", ["T2", "T3"], ["T2"], out=T2, in0=T2, in1=T3, op=ALU.subtract)
                        DVE("tensor_scalar", ["T2"], ["T2"], out=T2, in0=T2, scalar1=0.0, scalar2=None, op0=ALU.min)
                        DVE("tensor_scalar", ["T1"], ["T3"], out=T3, in0=T1, scalar1=1.0, scalar2=None, op0=ALU.add)
                        DVE("reciprocal", ["T3"], ["T3"], out=T3, in_=T3)
                        DVE("scalar_tensor_tensor", ["T1", "T3", "omlt"], ["T1"], out=T1, in0=T1, scalar=oml_ap, in1=T3, op0=ALU.mult, op1=ALU.mult)
                        DVE("tensor_tensor_scan", ["SMK", "T2"], ["T3"], out=T3, data0=SMK, data1=T2, initial=0.0, op0=ALU.mult, op1=ALU.add)
                        ACT(["T3"], ["T4"], out=T4, in_=T3, func=AF.Exp)
                        ACT(["T3"], ["T2"], out=T2, in_=T3, func=AF.Exp, scale=-1.0)

                        def cons_q(tg, ps, bkk):
                            sl = slice(tg * 512, (tg + 1) * 512)
                            DVE("tensor_tensor", [bkk, "T4"], ["QD" + n], out=QD[:, sl], in0=ps, in1=T4[:, sl], op=ALU.mult)
                        inproj_fm(512 + h * 128, 128, cons_q)
                        DVE("tensor_tensor", ["T1", "T2"], ["KD" + n], out=KD, in0=T1, in1=T2, op=ALU.mult)
                        DVE("tensor_copy", ["T4"], ["EBL" + n], out=EBL, in_=T4.rearrange("p (c k) -> p c k", k=CH)[:, :, CH - 1])
                        DVE("memset", [], ["SS" + n], hb["SS"], 0.0)
                        DVE("memset", [], ["SBF" + n], hb["SBF"], 0.0)

                    def hA(hb, c):
                        n = hb["n"]
                        KD, QD, ATM, KLC, KLT, EBL = hb["KD"], hb["QD"], hb["ATM"], hb["KLC"], hb["KLT"], hb["EBL"]
                        b_att, b_o, b_tr, b_ds = hb["bk"]
                        ptb = banks[b_tr][:, :].bitcast(BF16)
                        cs = slice(c * CH, (c + 1) * CH)
                        ai = c % 2
                        MM(["KD" + n, "QD" + n], [BK[b_att]], banks[b_att][0:64, 0:64], lhsT=KD[:, cs], rhs=QD[:, cs], start=True, stop=True)
                        DVE("tensor_tensor", [BK[b_att], "triu"], ["ATM%d%s" % (ai, n)], out=ATM[ai][0:64, :], in0=banks[b_att][0:64, 0:64], in1=triu, op=ALU.mult)
                        DVE("tensor_scalar", ["KD" + n, "EBL" + n], ["KLC%d%s" % (ai, n)], out=KLC[ai], in0=KD[:, cs],
                            scalar1=EBL[:, c:c + 1], scalar2=None, op0=ALU.mult)
                        TR(["KLC%d%s" % (ai, n), "ident_b"], [BK[b_tr]], out=ptb[0:64, 0:128], in_=KLC[ai], identity=ident_b)
                        ACT([BK[b_tr]], ["KLT%d%s" % (ai, n)], out=KLT[ai][0:64, :], in_=ptb[0:64, 0:128], func=AF.Identity)

                    def hDS(hb, c):
                        n = hb["n"]
                        ai = c % 2
                        b_ds = hb["bk"][3]
                        MM(["KLT%d%s" % (ai, n), "V64" + n], [BK[b_ds]], banks[b_ds][:, 0:128], lhsT=hb["KLT"][ai][0:64, :], rhs=hb["V64"][0:64, c, :], start=True, stop=True)

                    def hO(hb, c):
                        n = hb["n"]
                        cs = slice(c * CH, (c + 1) * CH)
                        ai = c % 2
                        b_o = hb["bk"][1]
                        oc = slice((c % 8) * CH, (c % 8 + 1) * CH)
                        MM(["V64" + n, "ATM%d%s" % (ai, n)], [BK[b_o]], banks[b_o][:, oc], lhsT=hb["V64"][0:64, c, :], rhs=hb["ATM"][ai][0:64, :], start=True, stop=False)
                        MM(["SBF" + n, "QD" + n], [BK[b_o]], banks[b_o][:, oc], lhsT=hb["SBF"], rhs=hb["QD"][:, cs], start=False, stop=True)

                    def hUpd(hb, c):
                        n = hb["n"]
                        b_ds = hb["bk"][3]
                        DVE("scalar_tensor_tensor", ["SS" + n, "EBL" + n, BK[b_ds]], ["SS" + n], out=hb["SS"], in0=hb["SS"],
                            scalar=hb["EBL"][:, c:c + 1], in1=banks[b_ds][:, 0:128], op0=ALU.mult, op1=ALU.add)
                        ACT(["SS" + n], ["SBF" + n], out=hb["SBF"], in_=hb["SS"], func=AF.Identity)

                    def hEvac(hb, c):
                        n = hb["n"]
                        b_o = hb["bk"][1]
                        if c % 8 == 7:
                            sl = slice((c // 8) * 512, (c // 8 + 1) * 512)
                            ACT([BK[b_o]], ["OE" + n], out=hb["OE"][:, sl], in_=banks[b_o][:, :], func=AF.Identity)

                    def h_post(h, hb, T2=T2, SQB=SQB, RS=RS, YST=YST, cvl=cvl):
                        n = hb["n"]
                        OE = hb["OE"]
                        yi = yst_i["i"]; yst_i["i"] ^= 1

                        def cons_g(tg, ps, bkk):
                            sl = slice(tg * 512, (tg + 1) * 512)
                            ACT([bkk], ["T2"], out=T2[:, sl], in_=ps, func=AF.Silu)
                        inproj_fm(2048 + h * 128, 128, cons_g)
                        for tg in range(NTG):
                            sl = slice(tg * 512, (tg + 1) * 512)
                            ACT(["OE" + n], ["SQB"], out=SQB, in_=OE[:, sl], func=AF.Square)
                            MM(["SQB", "cstb"], [BK[4]], banks[4][:, :], lhsT=ones_b, rhs=SQB, start=True, stop=True)
                            ACT([BK[4]], ["RS"], out=RS, in_=banks[4][:, :], func=AF.Ln, scale=1.0 / 128, bias=RMS_EPS)
                            ACT(["RS"], ["RS"], out=RS, in_=RS, func=AF.Exp, scale=-0.5)
                            DVE("tensor_tensor", ["OE" + n, "RS"], ["OE" + n], out=OE[:, sl], in0=OE[:, sl], in1=RS, op=ALU.mult)
                            DVE("scalar_tensor_tensor", ["OE" + n, "T2", "cvt"], ["YST%d" % yi], out=YST[yi][:, sl], in0=OE[:, sl],
                                scalar=cvl[:, 112 + h:113 + h], in1=T2[:, sl], op0=ALU.mult, op1=ALU.mult)
                        DMA("sp", dsem_o[yi], ["YST%d" % yi], ["YD"], out=YD[512 + h * 128:512 + (h + 1) * 128, :], in_=YST[yi])

                    for hp in (range(2) if DBG >= 2 else []):
                        hs = (2 * hp, 2 * hp + 1)
                        for hi in range(2):
                            h_prep(hs[hi], HB[hi])
                        for hi in range(2):
                            hA(HB[hi], 0)
                            hDS(HB[hi], 0)
                        for c in range(NCK):
                            for hi in range(2):
                                hb = HB[hi]
                                if c + 1 < NCK:
                                    hA(hb, c + 1)
                                hO(hb, c)
                                hUpd(hb, c)
                                if c + 1 < NCK:
                                    hDS(hb, c + 1)
                                hEvac(hb, c)
                        for hi in range(2):
                            h_post(hs[hi], HB[hi])
                    P.barrier()

                    CV.reset(tmp_off)
                    W80 = CV.take([KC, 80], BF16)
                    W8 = CV.take([KC, 8], BF16)
                    CUMP = CV.take([S], F32)
                    R1 = CV.take([S], F32)
                    R2 = CV.take([S], F32)
                    MID = CV.take([S], BF16)
                    CST = CV.take([S], BF16)
                    LH = CV.take([S], BF16)
                    RH = CV.take([S], BF16)
                    QT = CV.take([S], BF16)
                    KT = CV.take([S], BF16)
                    VT = CV.take([NTB, 256], BF16)
                    PT = [CV.take([512], BF16) for _ in range(2)]
                    SM = [CV.take([128], F32) for _ in range(2)]
                    RC = CV.take([512], F32)
                    YST = [CV.take([S], BF16) for _ in range(2)]
                    ONESB = CV.take([S], BF16)
                    POOL("memset", [], ["ONESB"], ONESB, 1.0)
                    DVE("memset", [], ["W80"], W80, 0.0)
                    for off in (0, 8, 32, 40, 64, 72):
                        DMA("pool", dsem_m[1], ["W80"], ["W80d"], out=W80[:, :, off:off + 8], in_=winv[:, :, 5632:5640])
                    nfb_ap = NFBT[0:80, l:l + 1]

                    def cons_f(tg, ps, bkk, R1=R1, R2=R2, nfb_ap=nfb_ap):
                        sl = slice(tg * 512, (tg + 1) * 512)
                        ACT([bkk, "nfbt"], ["R1"], out=R1[0:80, sl], in_=ps, func=AF.Exp, scale=-1.0, bias=nfb_ap)
                        ACT(["R1"], ["R2"], out=R2[0:80, sl], in_=R1[0:80, sl], func=AF.Ln, bias=1.0)
                    inproj_fm(5632, 80, cons_f, wtile=(W80, ["W80", "W80d"]), nbk=2)
                    DVE("tensor_tensor_scan", ["R2", "ONESB"], ["CUMP"], out=CUMP[0:80, :],
                        data0=ONESB[0:80, :], data1=R2[0:80, :], initial=0.0, op0=ALU.mult, op1=ALU.add)
                    DVE("tensor_copy", ["CUMP"], ["CST"], out=CST[0:80, :], in_=CUMP[0:80, :])
                    DVE("tensor_tensor", ["CUMP", "CST"], ["R1"], out=R1[32:64, :], in0=CUMP[32:64, :], in1=CST[32:64, :], op=ALU.subtract)
                    DVE("tensor_tensor", ["CUMP", "CST"], ["R1"], out=R1[64:80, :], in0=CUMP[64:80, :], in1=CST[64:80, :], op=ALU.subtract)
                    DVE("tensor_copy", ["R1"], ["CST"], out=CST[32:48, :], in_=R1[32:48, :])
                    DVE("tensor_copy", ["R1"], ["MID"], out=MID[64:80, :], in_=R1[64:80, :])
                    DVE("tensor_tensor", ["R1", "MID"], ["R2"], out=R2[64:80, :], in0=R1[64:80, :], in1=MID[64:80, :], op=ALU.subtract)
                    DVE("tensor_copy", ["R2"], ["CST"], out=CST[64:80, :], in_=R2[64:80, :])
                    scale_q = 128.0 ** -0.5
                    for h in (range(8) if DBG >= 3 else []):
                        DVE("tensor_scalar", ["CST", "sel"], ["LH"], out=LH[0:80, :], in0=CST[0:80, :], scalar1=sel2[0:80, h:h + 1],
                            scalar2=sel1[0:80, h:h + 1], op0=ALU.mult, op1=ALU.add)
                        DVE("tensor_scalar", ["CST", "sel", "nsel1"], ["RH"], out=RH[0:80, :], in0=CST[0:80, :], scalar1=nsel1[0:80, h:h + 1],
                            scalar2=sel2[0:80, h:h + 1], op0=ALU.mult, op1=ALU.add)

                        def cons_qf(tg, ps, bkk, QT=QT):
                            ACT([bkk], ["QT"], out=QT[:, tg * 512:(tg + 1) * 512], in_=ps, func=AF.Identity, scale=scale_q)
                        inproj_fm(2560 + h * 128, 128, cons_qf, nbk=2)

                        def cons_kf(tg, ps, bkk, KT=KT):
                            ACT([bkk], ["KT"], out=KT[:, tg * 512:(tg + 1) * 512], in_=ps, func=AF.Identity)
                        inproj_fm(3584 + h * 128, 128, cons_kf, nbk=2)
                        if h % 2 == 0:
                            wi = WP1.next(4608 + h * 128)
                            for tb in range(NTB):
                                bi = rot("bk", 2)
                                bk = banks[bi]
                                for kc in range(KC):
                                    MM([WK[wi][0], WK[wi][1], "XT"], [BK[bi]], bk[:, 0:256], lhsT=XT[:, kc, tb * 128:(tb + 1) * 128],
                                       rhs=wsl[wi][:, kc, 0:256], start=(kc == 0), stop=(kc == KC - 1))
                                ACT([BK[bi]], ["VT"], out=VT[:, tb, :], in_=bk[:, 0:256], func=AF.Identity)
                        ho = (h % 2) * 128
                        yi = yst_i["i"]; yst_i["i"] ^= 1
                        ORP = [(6, 7), (2, 3)]

                        def stageA(G, j, KT=KT, QT=QT, LH=LH, RH=RH, SM=SM, PT=PT):
                            c0 = max(0, j - 4 * G)
                            cols = slice(c0 * 128, 512)
                            gcols = slice(G * 512 + c0 * 128, G * 512 + 512)
                            sbi = 4 + rot("m", 2)
                            sbk = banks[sbi]
                            pi = rot("t", 2)
                            MM(["KT", "QT"], [BK[sbi]], sbk[:, cols], lhsT=KT[:, j * 128:(j + 1) * 128], rhs=QT[:, gcols], start=True, stop=False)
                            MM(["LH", "RH"], [BK[sbi]], sbk[:, cols], lhsT=LH[0:80, j * 128:(j + 1) * 128], rhs=RH[0:80, gcols], start=False, stop=True)
                            if j >= 4 * G:
                                dsl = slice(c0 * 128, (c0 + 1) * 128)
                                DVE("tensor_tensor", [BK[sbi], "maskneg"], ["SM%d" % pi], out=SM[pi], in0=sbk[:, dsl], in1=maskneg, op=ALU.add)
                                ACT(["SM%d" % pi], ["PT%d" % pi], out=PT[pi][:, dsl], in_=SM[pi], func=AF.Exp)
                                if c0 < 3:
                                    rsl = slice((c0 + 1) * 128, 512)
                                    ACT([BK[sbi]], ["PT%d" % pi], out=PT[pi][:, rsl], in_=sbk[:, rsl], func=AF.Exp)
                            else:
                                ACT([BK[sbi]], ["PT%d" % pi], out=PT[pi], in_=sbk[:, :], func=AF.Exp)
                            return (G, j, cols, pi)

                        def stageB(G, j, cols, pi, VT=VT, PT=PT, RC=RC, YST=YST, ho=ho, yi=yi):
                            nkb = 4 * G + 4
                            obi, rbi = ORP[G % 2]
                            MM(["VT", "PT%d" % pi], [BK[obi]], banks[obi][:, cols], lhsT=VT[:, j, ho:ho + 128], rhs=PT[pi][:, cols],
                               start=(j == 0), stop=(j == nkb - 1))
                            MM(["cstb", "PT%d" % pi], [BK[rbi]], banks[rbi][:, cols], lhsT=ones_b, rhs=PT[pi][:, cols],
                               start=(j == 0), stop=(j == nkb - 1))
                            if j == nkb - 1:
                                DVE("reciprocal", [BK[rbi]], ["RC"], out=RC, in_=banks[rbi][:, :])
                                DVE("tensor_tensor", [BK[obi], "RC"], ["YST%d" % yi], out=YST[yi][:, G * 512:(G + 1) * 512], in0=banks[obi][:, :], in1=RC, op=ALU.mult)
                        prev = None
                        for G in range(NTG):
                            for j in range(4 * G + 4):
                                cur = stageA(G, j)
                                if prev is not None:
                                    stageB(*prev)
                                prev = cur
                        stageB(*prev)
                        DMA("sp", dsem_o[yi], ["YST%d" % yi], ["YD"], out=YD[1024 + h * 128:1024 + (h + 1) * 128, :], in_=YST[yi])
                    P.barrier()

                    def ln_a_piece(PRE, kf, nb, tmp, sb=(6, 7), xi=None):
                        XB, SQ, MU, M2, RSTD = tmp
                        if xi is None:
                            xi = nb % 2
                        ACT([kf(nb)], ["XB%d" % xi], out=XB[xi], in_=PRE[:, nb, :], func=AF.Identity)
                        ACT([kf(nb)], ["SQ%d" % xi], out=SQ[xi], in_=PRE[:, nb, :], func=AF.Square)
                        MM(["XB%d" % xi, "cstb"], [BK[sb[0]]], banks[sb[0]][:, :], lhsT=ones_b, rhs=XB[xi], start=(nb == 0), stop=(nb == KC - 1))
                        MM(["SQ%d" % xi, "cstb"], [BK[sb[1]]], banks[sb[1]][:, :], lhsT=ones_b, rhs=SQ[xi], start=(nb == 0), stop=(nb == KC - 1))

                    def ln_a_final(tmp, sb=(6, 7)):
                        XB, SQ, MU, M2, RSTD = tmp
                        DVE("tensor_scalar", [BK[sb[0]]], ["MU"], out=MU, in0=banks[sb[0]][:, :], scalar1=1.0 / D, scalar2=None, op0=ALU.mult)
                        DVE("tensor_tensor", ["MU"], ["M2"], out=M2, in0=MU, in1=MU, op=ALU.mult)
                        DVE("scalar_tensor_tensor", [BK[sb[1]], "M2"], ["M2"], out=M2, in0=banks[sb[1]][:, :], scalar=1.0 / D, in1=M2, op0=ALU.mult, op1=ALU.subtract)
                        ACT(["M2"], ["RSTD"], out=RSTD, in_=M2, func=AF.Ln, bias=LN_EPS)
                        ACT(["RSTD"], ["RSTD"], out=RSTD, in_=RSTD, func=AF.Exp, scale=-0.5)

                    def ln_b_piece(PRE, kf, nb, tcols, gcol, bcol, tmp, XT=XT, cvl=cvl):
                        XB, SQ, MU, M2, RSTD = tmp
                        k = kf(nb)
                        DVE("tensor_tensor", [k, "MU"], [k], out=PRE[:, nb, :], in0=PRE[:, nb, :], in1=MU, op=ALU.subtract)
                        DVE("tensor_tensor", [k, "RSTD"], [k], out=PRE[:, nb, :], in0=PRE[:, nb, :], in1=RSTD, op=ALU.mult)
                        ACT([k, "cvt"], ["XT"], out=XT[:, nb, tcols], in_=PRE[:, nb, :], func=AF.Identity,
                            scale=cvl[:, gcol + nb:gcol + nb + 1], bias=cvl[:, bcol + nb:bcol + nb + 1])
                        ACT([k, "cvt"], [k], out=PRE[:, nb, :], in_=PRE[:, nb, :], func=AF.Identity,
                            scale=cvl[:, gcol + nb:gcol + nb + 1], bias=cvl[:, bcol + nb:bcol + nb + 1])

                    CV.reset(off_after_xt)
                    wsl = [CV.take([KC, 256], BF16) for _ in range(NWS)]
                    YT = [CV.take([KC, 512], BF16) for _ in range(2)]
                    PREb = [CV.take([KC, 512], F32) for _ in range(2)]
                    lntmp = ([CV.take([512], BF16) for _ in range(2)], [CV.take([512], BF16) for _ in range(2)],
                             CV.take([512], F32), CV.take([512], F32), CV.take([512], F32))
                    woutv = w_out_d[l].rearrange("(c p) n -> p c n", p=128)

                    def mk_ld2(np2, wsl=wsl, woutv=woutv):
                        def fn(i):
                            DMA("pool", wsem[i][0], [], [WK[i][0], WK[i][1]], out=wsl[i][:, :, 0:256], in_=woutv[:, :, np2 * 256:(np2 + 1) * 256])
                        return (np2, fn)
                    WP2 = WPlan([mk_ld2(np2) for tg in (range(NTG) if DBG >= 4 else []) for np2 in range(KC // 2)])
                    def pk2(bb):
                        return lambda nb: "PRE%d_%d" % (bb, nb)
                    allk = lambda bb: ["PRE%d_%d" % (bb, nb) for nb in range(KC)]
                    prev_tg = None
                    xk = lambda nb, tg: "XR2_%d_%d" % (nb, tg)
                    XRc = XR.rearrange("(c p) t -> c p t", p=128)
                    for tg in (range(NTG) if DBG >= 4 else []):
                        tcols = slice(tg * 512, (tg + 1) * 512)
                        b = tg % 2
                        if tg == 0:
                            DMA("sp", dsem_x[0], ["YD"], ["YT0"], out=YT[0], in_=YDv[:, :, tcols])
                            DMA("sp", dsem_m[2], [xk(nb, 0) for nb in range(KC)], allk(0), out=PREb[0], in_=XRv[:, :, tcols])
                            if NTG > 1:
                                DMA("sp", dsem_m[3], [xk(nb, 1) for nb in range(KC)], allk(1), out=PREb[1], in_=XRv[:, :, 512:1024])
                        if tg + 1 < NTG:
                            nb_ = (tg + 1) % 2
                            DMA("sp", dsem_x[nb_], ["YD"], ["YT%d" % nb_], out=YT[nb_], in_=YDv[:, :, (tg + 1) * 512:(tg + 2) * 512])
                        for np2 in range(KC // 2):
                            wi = WP2.next(np2)
                            for hf in range(2):
                                nb = np2 * 2 + hf
                                bi = rot("bk", 4)
                                bk = banks[bi]
                                for kc in range(KC):
                                    MM([WK[wi][hf], "YT%d" % b], [BK[bi]], bk[:, :], lhsT=wsl[wi][:, kc, hf * 128:(hf + 1) * 128],
                                       rhs=YT[b][:, kc, :], start=(kc == 0), stop=(kc == KC - 1))
                                DVE("scalar_tensor_tensor", [pk2(b)(nb), BK[bi]], [pk2(b)(nb)], out=PREb[b][:, nb, :], in0=PREb[b][:, nb, :],
                                    scalar=ALPHA, in1=bk[:, :], op0=ALU.mult, op1=ALU.add)
                                if nb > 0:
                                    ln_a_piece(PREb[b], pk2(b), nb - 1, lntmp)
                                if prev_tg is not None:
                                    pb_, ptc, ptg = prev_tg
                                    ln_b_piece(PREb[pb_], pk2(pb_), nb, ptc, 0, 16, lntmp)
                                    DMA("sp", wrp[nb], [pk2(pb_)(nb)], [xk(nb, ptg)], out=XRc[nb, :, ptc], in_=PREb[pb_][:, nb, :])
                                    if tg + 1 < NTG:
                                        DMA("sp", ldp[nb], [xk(nb, tg + 1)], [pk2(pb_)(nb)], out=PREb[pb_][:, nb, :],
                                            in_=XRc[nb, :, (tg + 1) * 512:(tg + 2) * 512])
                        ln_a_piece(PREb[b], pk2(b), KC - 1, lntmp)
                        ln_a_final(lntmp)
                        prev_tg = (b, tcols, tg)
                    if prev_tg is not None:
                        pb_, ptc, ptg = prev_tg
                        for nb in range(KC):
                            ln_b_piece(PREb[pb_], pk2(pb_), nb, ptc, 0, 16, lntmp)
                            DMA("sp", wrp[nb], [pk2(pb_)(nb)], [xk(nb, ptg)], out=XRc[nb, :, ptc], in_=PREb[pb_][:, nb, :])
                    P.barrier()

                    CV.reset(off_after_xt)
                    wsl = [CV.take([KC, 256], BF16) for _ in range(NWS)]
                    ACC = CV.take([KC, TH], F32)
                    AT = CV.take([GF, TH], BF16)
                    GS = [CV.take([2 + 512], F32) for _ in range(2)]
                    CC = [CV.take([512], F32) for _ in range(2)]
                    EE = [CV.take([512], F32) for _ in range(2)]
                    lntmp = ([CV.take([512], BF16) for _ in range(2)], [CV.take([512], BF16) for _ in range(2)],
                             CV.take([512], F32), CV.take([512], F32), CV.take([512], F32))
                    wgv = w_gate_d[l].rearrange("(c p) n -> p c n", p=128)
                    wvv = w_val_d[l].rearrange("(c p) n -> p c n", p=128)
                    wdv = w_down_d[l].rearrange("(f p) n -> p f n", p=128)
                    DVE("memset", [], ["HALO"], HALO, 0.0)
                    NG = NFB // GF

                    def mk_gv(f, wsl=wsl, wgv=wgv, wvv=wvv):
                        def fn(i):
                            DMA("pool", wsem[i][0], [], [WK[i][0]], out=wsl[i][:, :, 0:128], in_=wgv[:, :, f * 128:(f + 1) * 128])
                            DMA("pool", wsem[i][1], [], [WK[i][1]], out=wsl[i][:, :, 128:256], in_=wvv[:, :, f * 128:(f + 1) * 128])
                        return (("gv", f), fn)

                    def mk_wd(f0, np2, wsl=wsl, wdv=wdv):
                        def fn(i):
                            DMA("pool", wsem[i][0], [], [WK[i][0], WK[i][1]], out=wsl[i][:, 0:GF, :], in_=wdv[:, f0:f0 + GF, np2 * 256:(np2 + 1) * 256])
                        return (("wd", f0, np2), fn)
                    plan3 = []
                    for hfq in (range(NHALF) if DBG >= 5 else []):
                        for g in range(NG):
                            plan3 += [mk_gv(g * GF + fi) for fi in range(GF)]
                            plan3 += [mk_wd(g * GF, np2) for np2 in range(KC // 2)]
                    WP3 = WPlan(plan3)
                    for hfq in (range(NHALF) if DBG >= 5 else []):
                        t0 = hfq * TH
                        ak = lambda nb, tg: "ACC_%d_%d" % (nb, tg)
                        akall = [ak(nb, tg) for nb in range(KC) for tg in range(NTGH)]
                        kfs = [(lambda tg: (lambda nb: ak(nb, tg)))(tg) for tg in range(NTGH)]
                        SBK = [(0, 1), (2, 3)]
                        DMA("sp", dsem_m[2], ["XR"], akall, out=ACC, in_=XRv[:, :, t0:t0 + TH])
                        for nb in range(KC):
                            ACT([], [ak(nb, tg) for tg in range(NTGH)], out=ACC[:, nb, :], in_=ACC[:, nb, :], func=AF.Identity, scale=ALPHA)
                        for g in range(NG):
                            for fi in range(GF):
                                f = g * GF + fi
                                wi = WP3.next(("gv", f))
                                cw = [cvl[:, 120 + j * 44 + f:121 + j * 44 + f] for j in range(3)]
                                cb = cvl[:, 64 + f:65 + f]
                                for tg in range(NTGH):
                                    tsl = slice(t0 + tg * 512, t0 + (tg + 1) * 512)
                                    si = rot("t", 2)
                                    bgi = si * 2
                                    bvi = si * 2 + 1
                                    for (bi, hf) in ((bgi, 0), (bvi, 1)):
                                        for kc in range(KC):
                                            MM([WK[wi][hf], "XT"], [BK[bi]], banks[bi][:, :], lhsT=wsl[wi][:, kc, hf * 128:(hf + 1) * 128],
                                               rhs=XT[:, kc, tsl], start=(kc == 0), stop=(kc == KC - 1))
                                    gs_, cc_, ee_ = GS[si], CC[si], EE[si]
                                    gk, ck, ek = "GS%d" % si, "CC%d" % si, "EE%d" % si
                                    ACT(["HALO"], [gk], out=gs_[:, 0:2], in_=HALO[:, f, :], func=AF.Identity)
                                    ACT([BK[bgi]], [gk], out=gs_[:, 2:514], in_=banks[bgi][:, :], func=AF.Identity)
                                    ACT([gk], ["HALO"], out=HALO[:, f, :], in_=gs_[:, 512:514], func=AF.Identity)
                                    ACT([gk, "cvt"], [ck], out=cc_, in_=gs_[:, 2:514], func=AF.Identity, scale=cw[2], bias=cb)
                                    DVE("scalar_tensor_tensor", [gk, ck, "cvt"], [ck], out=cc_, in0=gs_[:, 1:513], scalar=cw[1], in1=cc_, op0=ALU.mult, op1=ALU.add)
                                    DVE("scalar_tensor_tensor", [gk, ck, "cvt"], [ck], out=cc_, in0=gs_[:, 0:512], scalar=cw[0], in1=cc_, op0=ALU.mult, op1=ALU.add)
                                    ACT([ck], [ek], out=ee_, in_=cc_, func=AF.Silu)
                                    DVE("tensor_tensor", [ek, BK[bvi]], ["AT"], out=AT[:, fi, tg * 512:(tg + 1) * 512], in0=ee_, in1=banks[bvi][:, :], op=ALU.mult)
                            f0 = g * GF
                            for np2 in range(KC // 2):
                                wi = WP3.next(("wd", f0, np2))
                                for hf in range(2):
                                    nb = np2 * 2 + hf
                                    for tg in range(NTGH):
                                        bi = 4 + rot("m", 2)
                                        for fi in range(GF):
                                            MM([WK[wi][hf], "AT"], [BK[bi]], banks[bi][:, :], lhsT=wsl[wi][:, fi, hf * 128:(hf + 1) * 128],
                                               rhs=AT[:, fi, tg * 512:(tg + 1) * 512], start=(fi == 0), stop=(fi == GF - 1))
                                        DVE("tensor_tensor", [ak(nb, tg), BK[bi]], [ak(nb, tg)], out=ACC[:, nb, tg * 512:(tg + 1) * 512],
                                            in0=ACC[:, nb, tg * 512:(tg + 1) * 512], in1=banks[bi][:, :], op=ALU.add)
                                        if g == NG - 1 and nb > 0:
                                            ln_a_piece(ACC[:, :, tg * 512:(tg + 1) * 512], kfs[tg], nb - 1, lntmp, sb=SBK[tg], xi=tg % 2)
                        for tg in range(NTGH):
                            ln_a_piece(ACC[:, :, tg * 512:(tg + 1) * 512], kfs[tg], KC - 1, lntmp, sb=SBK[tg], xi=tg % 2)
                        for tg in range(NTGH):
                            tcols = slice(t0 + tg * 512, t0 + (tg + 1) * 512)
                            PREv = ACC[:, :, tg * 512:(tg + 1) * 512]
                            ln_a_final(lntmp, sb=SBK[tg])
                            for nb in range(KC):
                                ln_b_piece(PREv, kfs[tg], nb, tcols, 32, 48, lntmp)
                        DMA("sp", dsem_o[0], akall, ["XR"], out=XRv[:, :, t0:t0 + TH], in_=ACC)
                    P.barrier()

                CV.reset(off_after_xt)
                xo_in = [CV.take([KC, 128], F32) for _ in range(2)]
                xo_st = [CV.take([D], F32) for _ in range(2)]
                for tb in range(NTB):
                    b = tb % 2
                    DMA("sp", dsem_x[b], ["XR"], ["xoin%d" % b], out=xo_in[b], in_=XRv[:, :, tb * 128:(tb + 1) * 128])
                    for q4 in range(4):
                        bi = rot("bk", 2)
                        bk = banks[bi]
                        for j in range(4):
                            dc = q4 * 4 + j
                            TR(["xoin%d" % b, "ident_f"], [BK[bi]], out=bk[:, j * 128:(j + 1) * 128], in_=xo_in[b][:, dc, :], identity=ident_f)
                        DVE("tensor_copy", [BK[bi]], ["xost%d" % b], out=xo_st[b][:, q4 * 512:(q4 + 1) * 512], in_=bk[:, :])
                    DMA("sp", dsem_o[b], ["xost%d" % b], ["OUT"], out=out_d[tok0 + tb * 128: tok0 + (tb + 1) * 128, :], in_=xo_st[b])
                P.barrier()
        except _Stop:
            pass
        P.barrier()
        P.emit()
    return nc


_NC_CACHE = {}

WNAMES = ["w_in", "fox_f_bias", "pool_w", "pool_scale", "hgrn_lb_logits", "hgrn_norm_g", "w_out", "ln1_g", "ln1_b",
          "w_gate", "w_val", "conv_w", "conv_b", "w_down", "ln2_g", "ln2_b"]


def kernel(**inputs):
    x = np.ascontiguousarray(inputs["x"], dtype=np.float32)
    B, S, Dm = x.shape
    L = inputs["w_in"].shape[0]
    ncores = 8
    nseq = B // ncores
    key = (S, nseq, L)
    if key not in _NC_CACHE:
        _NC_CACHE[key] = build(S, nseq, L)
    nc = _NC_CACHE[key]
    ws = {k: np.ascontiguousarray(inputs[k], dtype=np.float32) for k in WNAMES}
    in_maps = []
    for c in range(ncores):
        m = {"x": x[c * nseq:(c + 1) * nseq].reshape(nseq * S, Dm)}
        m.update(ws)
        in_maps.append(m)
    res = run_bass_kernel_spmd(nc, in_maps, core_ids=list(range(ncores)))
    outs = [np.asarray(r["out"]).reshape(nseq, S, Dm) for r in res.results]
    return np.concatenate(outs, axis=0).astype(np.float32)
```
